# Optimizing a Trainium2 kernel written in Bass

```python
import jax, jax.numpy as jnp
from jax import lax
import numpy as np

D_MODEL = 1024
BATCH = 8
SEQ = 2048
DEPTH = 1
DEC_BATCH = 128
DEC_SEQ = 1
PAST_LEN = 16384
PAGE_SIZE = 128

MIX_WIDTH = D_MODEL
POOL_WIDTH = MIX_WIDTH // 4
POOL_WINDOWS = (2, 4, 8, 16)
POOL_GROUPS = len(POOL_WINDOWS)
POOL_GROUP_DIM = POOL_WIDTH // POOL_GROUPS
POOL_BUF = max(POOL_WINDOWS) - 1
RET_WIDTH = MIX_WIDTH - POOL_WIDTH
RET_HEAD_DIM = 128
RET_HEADS = RET_WIDTH // RET_HEAD_DIM
RET_CHUNK = 128
ROPE_BASE = 10000.0
D_FF = ((8 * D_MODEL // 3 + 127) // 128) * 128
CONV_W = 3
N_MOD = 6
IN_COLS = POOL_WIDTH + 4 * RET_WIDTH
EPS = 1e-6
F32 = jnp.float32

kernel_name = "hymba_pool_retention_convffn_adaln_step"


def rmsnorm(x, g):
    xf = x.astype(F32)
    y = xf * lax.rsqrt(jnp.mean(xf * xf, axis=-1, keepdims=True) + EPS)
    return (y * g.astype(F32)).astype(x.dtype)


def rope(x, pos):
    half = x.shape[-1] // 2
    inv = ROPE_BASE ** (-jnp.arange(half, dtype=F32) / half)
    ang = pos.astype(F32)[:, None] * inv[None, :]
    cos, sin = jnp.cos(ang), jnp.sin(ang)
    x1, x2 = x[..., :half], x[..., half:]
    return jnp.concatenate([x1 * cos - x2 * sin, x1 * sin + x2 * cos], axis=-1)


def ret_log_gammas():
    return jnp.log(1.0 - jnp.exp2(-5.0 - jnp.arange(RET_HEADS, dtype=F32)))


def pool_mix(u_ext, pos0, w_pool, ls_pool):
    B = u_ext.shape[0]
    L = u_ext.shape[1] - POOL_BUF
    uf = u_ext.astype(F32)
    cs = jnp.concatenate([jnp.zeros_like(uf[:, :1]), jnp.cumsum(uf, axis=1)], axis=1)
    hi = POOL_BUF + 1 + jnp.arange(L)
    pos = pos0 + jnp.arange(L)
    u_new = uf[:, POOL_BUF:]
    outs = []
    for g, w in enumerate(POOL_WINDOWS):
        lo_c, hi_c = g * POOL_GROUP_DIM, (g + 1) * POOL_GROUP_DIM
        csg = cs[:, :, lo_c:hi_c]
        s = csg[:, hi] - csg[:, hi - w]
        cnt = jnp.minimum(w, pos + 1).astype(F32)
        outs.append(s / cnt[None, :, None] - u_new[:, :, lo_c:hi_c])
    p = jnp.stack(outs, axis=2)
    y = jnp.einsum('blgc,gcd->blgd', p, w_pool.astype(F32)).reshape(B, L, POOL_WIDTH)
    return y * ls_pool.astype(F32)


def retention(q, k, v, s0):
    B, H, L, d = q.shape
    C = RET_CHUNK if L % RET_CHUNK == 0 else L
    NC = L // C
    lg = ret_log_gammas()
    j = jnp.arange(C, dtype=F32)
    diff = j[:, None] - j[None, :]
    mask = jnp.where(diff >= 0, jnp.exp(lg[:, None, None] * jnp.maximum(diff, 0.0)), 0.0)
    q_dec = jnp.exp(lg[:, None] * (j + 1.0))[None, :, :, None]
    k_dec = jnp.exp(lg[:, None] * (C - 1.0 - j))[None, :, :, None]
    chunk_dec = jnp.exp(lg * C)[None, :, None, None]

    def step(s, qkv):
        qc, kc, vc = qkv
        att = jnp.einsum('bhid,bhjd->bhij', qc, kc) * mask
        o = (jnp.einsum('bhij,bhjd->bhid', att, vc)
             + jnp.einsum('bhik,bhkv->bhiv', qc, s) * q_dec)
        s_new = s * chunk_dec + jnp.einsum('bhjk,bhjv->bhkv', kc * k_dec, vc)
        return s_new, o

    split = lambda t: jnp.moveaxis(t.reshape(B, H, NC, C, d), 2, 0)
    s_fin, o = lax.scan(step, s0, (split(q), split(k), split(v)))
    o = jnp.moveaxis(o, 0, 2).reshape(B, H, L, d)
    return o, s_fin


def block(x, c, pool_prev, ret_prev, conv_prev, pos0,
          g_mix, g_ffn, w_ada, b_ada, w_in, w_pool, ls_pool, w_out,
          w_ffn_in, conv_w, conv_b, w_ffn_out):
    B, L, _ = x.shape
    mod = (jax.nn.silu(c.astype(F32)) @ w_ada + b_ada).reshape(B, N_MOD, 1, D_MODEL)
    sh_m, sc_m, gt_m, sh_f, sc_f, gt_f = (mod[:, i] for i in range(N_MOD))

    h = rmsnorm(x, g_mix) * (1.0 + sc_m) + sh_m
    z = h @ w_in
    P, R = POOL_WIDTH, RET_WIDTH
    u, q, k, v, gate = jnp.split(z, [P, P + R, P + 2 * R, P + 3 * R], axis=-1)

    u_ext = jnp.concatenate([pool_prev.astype(u.dtype), u], axis=1)
    pool_out = pool_mix(u_ext, pos0, w_pool, ls_pool)

    heads = lambda t: t.reshape(B, L, RET_HEADS, RET_HEAD_DIM).transpose(0, 2, 1, 3).astype(F32)
    pos = pos0 + jnp.arange(L)
    qh = rope(heads(q), pos)
    kh = rope(heads(k), pos) * (RET_HEAD_DIM ** -0.5)
    o, s_new = retention(qh, kh, heads(v), ret_prev.astype(F32))
    mu = jnp.mean(o, axis=-1, keepdims=True)
    var = jnp.mean(jnp.square(o - mu), axis=-1, keepdims=True)
    o = ((o - mu) * lax.rsqrt(var + EPS)).transpose(0, 2, 1, 3).reshape(B, L, RET_WIDTH)
    ret_out = jax.nn.silu(gate.astype(F32)) * o

    mix = jnp.concatenate([pool_out, ret_out], axis=-1) @ w_out
    x = x + gt_m * mix

    h = rmsnorm(x, g_ffn) * (1.0 + sc_f) + sh_f
    a, b = jnp.split(h @ w_ffn_in, [D_FF], axis=-1)
    a_ext = jnp.concatenate([conv_prev.astype(a.dtype), a], axis=1)
    acc = a_ext[:, 0:L] * conv_w[0]
    for i in range(1, CONV_W):
        acc = acc + a_ext[:, i:i + L] * conv_w[i]
    f = (jax.nn.silu(acc + conv_b) * b) @ w_ffn_out
    x = x + gt_f * f
    return x, u_ext[:, -POOL_BUF:], s_new, a_ext[:, -(CONV_W - 1):]


def setup_inputs(seed: int = 0) -> dict:
    key = jax.random.key(seed)
    ks = jax.random.split(key, 24)
    n = lambda k, s, scale: jax.random.normal(k, s, F32) * scale
    return {
        "x_prompt": n(ks[0], (BATCH, SEQ, D_MODEL), 1.0),
        "x_sample": n(ks[1], (DEC_BATCH, DEC_SEQ, D_MODEL), 1.0),
        "c_prompt": n(ks[2], (BATCH, D_MODEL), 1.0),
        "c_sample": n(ks[3], (DEC_BATCH, D_MODEL), 1.0),
        "state_pool": n(ks[4], (DEPTH, DEC_BATCH, POOL_BUF, POOL_WIDTH), 1.0),
        "state_ret": n(ks[5], (DEPTH, DEC_BATCH, RET_HEADS, RET_HEAD_DIM, RET_HEAD_DIM), 0.5),
        "state_conv": n(ks[6], (DEPTH, DEC_BATCH, CONV_W - 1, D_FF), 1.0),
        "g_mix": 1.0 + n(ks[7], (DEPTH, D_MODEL), 0.05),
        "g_ffn": 1.0 + n(ks[8], (DEPTH, D_MODEL), 0.05),
        "w_ada": n(ks[9], (DEPTH, D_MODEL, N_MOD * D_MODEL), 0.5 * D_MODEL ** -0.5),
        "b_ada": n(ks[10], (DEPTH, N_MOD * D_MODEL), 0.01),
        "w_in": n(ks[11], (DEPTH, D_MODEL, IN_COLS), D_MODEL ** -0.5),
        "w_pool": n(ks[12], (DEPTH, POOL_GROUPS, POOL_GROUP_DIM, POOL_GROUP_DIM), POOL_GROUP_DIM ** -0.5),
        "ls_pool": 1.0 + n(ks[13], (DEPTH, POOL_WIDTH), 0.1),
        "w_out": n(ks[14], (DEPTH, MIX_WIDTH, D_MODEL), MIX_WIDTH ** -0.5),
        "w_ffn_in": n(ks[15], (DEPTH, D_MODEL, 2 * D_FF), D_MODEL ** -0.5),
        "conv_w": n(ks[16], (DEPTH, CONV_W, D_FF), CONV_W ** -0.5),
        "conv_b": n(ks[17], (DEPTH, D_FF), 0.01),
        "w_ffn_out": n(ks[18], (DEPTH, D_FF, D_MODEL), D_FF ** -0.5),
        "g_final": 1.0 + n(ks[19], (D_MODEL,), 0.05),
    }


def reference(x_prompt, x_sample, c_prompt, c_sample, state_pool, state_ret, state_conv,
              g_mix, g_ffn, w_ada, b_ada, w_in, w_pool, ls_pool, w_out,
              w_ffn_in, conv_w, conv_b, w_ffn_out, g_final):
    yp, ys = x_prompt, x_sample
    bp = x_prompt.shape[0]
    pp_pool, pp_ret, pp_conv, sp_pool, sp_ret, sp_conv = [], [], [], [], [], []
    for l in range(DEPTH):
        params = (g_mix[l], g_ffn[l], w_ada[l], b_ada[l], w_in[l], w_pool[l], ls_pool[l],
                  w_out[l], w_ffn_in[l], conv_w[l], conv_b[l], w_ffn_out[l])
        yp, a1, a2, a3 = block(yp, c_prompt,
                               jnp.zeros((bp, POOL_BUF, POOL_WIDTH), x_prompt.dtype),
                               jnp.zeros((bp, RET_HEADS, RET_HEAD_DIM, RET_HEAD_DIM), F32),
                               jnp.zeros((bp, CONV_W - 1, D_FF), x_prompt.dtype),
                               0, *params)
        ys, b1, b2, b3 = block(ys, c_sample, state_pool[l], state_ret[l], state_conv[l],
                               PAST_LEN, *params)
        pp_pool.append(a1); pp_ret.append(a2); pp_conv.append(a3)
        sp_pool.append(b1); sp_ret.append(b2); sp_conv.append(b3)
    yp = rmsnorm(yp, g_final)
    ys = rmsnorm(ys, g_final)
    return (yp, ys, jnp.stack(pp_pool), jnp.stack(pp_ret), jnp.stack(pp_conv),
            jnp.stack(sp_pool), jnp.stack(sp_ret), jnp.stack(sp_conv))
```

```python
import math
from contextlib import ExitStack

import numpy as np
import concourse.bass as bass
import concourse.mybir as mybir
from concourse.bass_utils import run_bass_kernel_spmd

F32 = mybir.dt.float32
BF16 = mybir.dt.bfloat16
AF = mybir.ActivationFunctionType
ALU = mybir.AluOpType
AX = mybir.AxisListType

NCORES = 8
D = 1024
L = 2048
NT = 16
NS = 16
H = 6
HD = 128
RW = 768
PW = 256
INC = 3328
DFF = 2816
NJ = 22
PAST = 16384
EPS = 1e-6
KC = 8


class Buf:
    __slots__ = ("name", "w", "r")

    def __init__(self, name):
        self.name = name
        self.w = None
        self.r = {}


class Sched:
    ENG = ("pe", "act", "dve", "pool", "sp")

    def __init__(self, nc, es, ring_sizes):
        self.nc = nc
        self.q = {n: [] for n in self.ENG}
        self.sem = {}
        self.cnt = {}
        for n in ("pe", "act", "dve", "pool"):
            self.sem[n] = es.enter_context(nc.semaphore("s_" + n))
            self.cnt[n] = 0
        self.seen = {n: {} for n in self.ENG}
        self.ring = {}
        self.rpos = {}
        for qn, sz in ring_sizes.items():
            lst = []
            for i in range(sz):
                key = "d_%s_%d" % (qn, i)
                self.sem[key] = es.enter_context(nc.semaphore(key))
                lst.append([key, 0])
            self.ring[qn] = lst
            self.rpos[qn] = 0
        self.nwait = 0

    def _waits(self, eng, deps, is_dma=False):
        for tok, kind in deps:
            key, val, src = tok
            if src == eng and not is_dma:
                if eng == "pe":
                    continue
            if src is not None and src in self.cnt and val > self.cnt[src]:
                if src == eng:
                    continue
                raise RuntimeError("dependency on unsignalled op of %s" % src)
            if self.seen[eng].get(key, 0) >= val:
                continue
            self.seen[eng][key] = val
            self.q[eng].append(("wait", key, val))
            self.nwait += 1

    def _deps(self, reads, writes):
        deps = []
        for b in reads:
            if b.w is not None:
                deps.append((b.w, "raw"))
        for b in writes:
            if b.w is not None:
                deps.append((b.w, "waw"))
            for t in b.r.values():
                deps.append((t, "war"))
        return deps

    def op(self, eng, fn, reads=(), writes=(), signal=True, embed=None):
        if embed is None:
            embed = eng in ("dve", "pool", "pe")
        self._waits(eng, self._deps(reads, writes))
        if signal:
            self.cnt[eng] += 1
            tok = (eng, self.cnt[eng], eng)
        else:
            tok = (eng, self.cnt[eng] + 1, eng)
        self.q[eng].append(("op", fn, signal, embed))
        for b in reads:
            b.r[eng] = tok
        for b in writes:
            b.w = tok
            b.r = {}
        return tok

    def dma(self, qn, out, in_, reads=(), writes=(), **kw):
        ring = self.ring[qn]
        i = self.rpos[qn]
        self.rpos[qn] = (i + 1) % len(ring)
        key, val = ring[i]
        deps = self._deps(reads, writes)
        if val > 0:
            deps.append(((key, val, None), "raw"))
        self._waits(qn, deps, is_dma=True)
        ring[i][1] = val + 16
        tok = (key, val + 16, None)
        self.q[qn].append(("dma", out, in_, key, kw))
        for b in reads:
            b.r[key] = tok
        for b in writes:
            b.w = tok
            b.r = {}
        return tok

    def barrier(self, include_pool_ring=True):
        toks = []
        for n in ("pe", "act", "dve", "pool"):
            if self.cnt[n] > 0:
                toks.append((n, self.cnt[n], n))
        for qn, ring in self.ring.items():
            if qn == "pool" and not include_pool_ring:
                continue
            for key, val in ring:
                if val > 0:
                    toks.append((key, val, None))
        for eng in self.ENG:
            for (key, val, src) in toks:
                if src == eng and eng != "pool":
                    continue
                if self.seen[eng].get(key, 0) >= val:
                    continue
                self.seen[eng][key] = val
                self.q[eng].append(("wait", key, val))

    def finish(self):
        for qn, ring in self.ring.items():
            for key, val in ring:
                if val > 0 and self.seen["sp"].get(key, 0) < val:
                    self.q["sp"].append(("wait", key, val))
                    self.seen["sp"][key] = val
        for n in ("pe", "act", "dve", "pool"):
            if self.cnt[n] > 0:
                self.q["sp"].append(("wait", n, self.cnt[n]))

    def replay(self, e, qn):
        q = self.q[qn]
        pend = None
        for i, it in enumerate(q):
            if it[0] == "wait":
                if pend is not None:
                    e.wait_ge(self.sem[pend[1]], pend[2])
                    pend = None
                nxt = q[i + 1] if i + 1 < len(q) else None
                if nxt is not None and nxt[0] == "op" and nxt[3]:
                    pend = it
                else:
                    e.wait_ge(self.sem[it[1]], it[2])
            elif it[0] == "op":
                ins = it[1](e)
                if pend is not None:
                    ins._wait_ge(self.sem[pend[1]], pend[2])
                    pend = None
                if it[2]:
                    ins.then_inc(self.sem[qn], 1)
            else:
                _, out, in_, key, kw = it
                e.dma_start(out=out, in_=in_, **kw).then_inc(self.sem[key], 16)


class T:
    def __init__(self, t, name):
        self.t = t
        self.b = Buf(name)

    def __getitem__(self, k):
        return self.t[k]


def _log_gammas():
    return np.log(1.0 - np.exp2(-5.0 - np.arange(H, dtype=np.float64)))


def _host_consts():
    lg = _log_gammas()
    c = {}
    i = np.arange(128, dtype=np.float64)
    qdec = np.exp(lg[:, None] * (i[None, :] + 1.0)).reshape(1, RW)
    c["qdec"] = np.broadcast_to(qdec, (128, RW))
    j = i
    m = np.exp(-lg[None, :, None] * (j[:, None, None] + 1.0)) * (HD ** -0.5)
    m = m * (i[None, None, :] >= j[:, None, None])
    c["maskT"] = m.reshape(128, RW)
    c["kdec"] = np.exp(lg[None, :] * (127.0 - j[:, None])) * (HD ** -0.5)
    c["gam"] = np.broadcast_to(np.repeat(np.exp(lg), HD)[None, :], (128, RW))
    win = np.array([2, 4, 8, 16], dtype=np.float64)
    p = np.arange(128)
    invw = np.zeros((128, 2))
    corr = np.zeros((128, 2, 16))
    for cc in range(2):
        w = win[2 * cc + p // 64]
        invw[:, cc] = 1.0 / w
        tt = np.arange(16, dtype=np.float64)
        corr[:, cc, :] = w[:, None] / np.minimum(w[:, None], tt[None, :] + 1.0)
    c["invw"] = invw
    c["corr0"] = corr.reshape(128, 32)
    sel = np.zeros((128, 4, 3, 16))
    for g in range(4):
        w = int(win[g])
        for half in range(2):
            for bb in range(8):
                for r in range(15):
                    if r >= 16 - w:
                        sel[bb * 15 + r, g, half, half * 8 + bb] = 1.0 / w
        for bb in range(16):
            sel[bb, g, 2, bb] = 1.0 / w - 1.0
    c["sel"] = sel.reshape(128, 192)
    names = ["qdec", "maskT", "kdec", "gam", "invw", "corr0", "sel"]
    offs = {}
    o = 0
    cols = []
    for n in names:
        a = np.asarray(c[n], dtype=np.float64)
        offs[n] = (o, a.shape[1])
        o += a.shape[1]
        cols.append(a)
    arr = np.concatenate(cols, axis=1).astype(np.float32)
    half = HD // 2
    inv = (np.float32(10000.0) ** (-(np.arange(half, dtype=np.float32) / np.float32(half)))).astype(np.float32)
    pos = np.concatenate([np.arange(L), np.full(NS, PAST)]).astype(np.float32)
    ang = (pos[:, None] * inv[None, :]).astype(np.float32).astype(np.float64)
    rope = np.concatenate([np.cos(ang), np.sin(ang), -np.sin(ang)], axis=1).astype(np.float32)
    return arr, offs, rope


def build_nc(debug=False):
    CONSTS, COFF, _ = _host_consts()
    NCONST = CONSTS.shape[1]
    lg = _log_gammas()
    gamC = [float(np.exp(lg[h] * 128.0)) for h in range(H)]
    gam1 = [float(np.exp(lg[h])) for h in range(H)]

    nc = bass.Bass("TRN2", target_bir_lowering=False)

    def din(name, shape):
        return nc.dram_tensor(name, list(shape), F32, kind="ExternalInput").ap()

    def dout(name, shape):
        return nc.dram_tensor(name, list(shape), F32, kind="ExternalOutput").ap()

    xp = din("xp", [L, D]); xs = din("xs", [NS, D]); cp = din("cp", [1, D]); cs = din("cs", [NS, D])
    spool = din("spool", [NS * 15, PW]); sret = din("sret", [NS, H, HD, HD]); sconv = din("sconv", [NS * 2, DFF])
    g_mix = din("g_mix", [1, D]); g_ffn = din("g_ffn", [1, D]); g_fin = din("g_fin", [1, D])
    w_ada = din("w_ada", [D, 6 * D]); b_ada = din("b_ada", [1, 6 * D])
    w_in = din("w_in", [D, INC]); w_pool = din("w_pool", [PW, 64]); ls_pool = din("ls_pool", [128, 2])
    w_out = din("w_out", [D, D]); w_fi = din("w_fi", [NJ, 128, KC * 256]); w_fo = din("w_fo", [DFF, D])
    cwT = din("cwT", [128, NJ * 3]); cbT = din("cbT", [128, NJ])
    consts = din("consts", [128, NCONST]); ident = din("ident", [128, 128]); rope = din("rope", [L + NS, 192])

    yp = dout("yp", [L, D]); ys = dout("ys", [NS, D])
    npool_p = dout("npool_p", [15, PW]); nret_p = dout("nret_p", [H * HD, HD]); nconv_p = dout("nconv_p", [2, DFF])
    npool_s = dout("npool_s", [NS * 15, PW]); nret_s = dout("nret_s", [NS, H, HD, HD]); nconv_s = dout("nconv_s", [NS * 2, DFF])
    x1s = nc.dram_tensor("x1s", [L, D], F32, kind="Internal").ap()
    if debug:
        dbg_mix = dout("dbg_mix", [128, KC * NS]); dbg_x1 = dout("dbg_x1", [NS, D]); dbg_oi = dout("dbg_oi", [NS, RW])
        dbg_ret = dout("dbg_ret", [NS, RW])

    es = ExitStack()
    with es:
        S = Sched(nc, es, {"sp": 24, "pool": 12})

        uniq = {"n": 0}

        def sb(stack, name, shape, dt=F32):
            uniq["n"] += 1
            return T(stack.enter_context(nc.sbuf_tensor("sb%d_%s" % (uniq["n"], name), list(shape), dt)), name)

        def ps(stack, name, shape, dt=F32):
            return T(stack.enter_context(nc.psum_tensor("ps_" + name, list(shape), dt)), name)

        def mm(out, lhsT, rhs, start, stop, reads, writes, signal):
            S.op("pe", lambda e: e.matmul(out, lhsT=lhsT, rhs=rhs, start=start, stop=stop),
                 reads, writes, signal)

        def tr(out, in_, idn, reads, writes, signal):
            S.op("pe", lambda e: e.transpose(out, in_, idn), reads, writes, signal)

        def act(out, in_, func, reads, writes, scale=None, bias=None, accum=None, signal=True):
            kw = {}
            if scale is not None:
                kw["scale"] = scale
            if bias is not None:
                kw["bias"] = bias
            if accum is not None:
                kw["accum_out"] = accum
            S.op("act", lambda e: e.activation(out=out, in_=in_, func=func, **kw), reads, writes, signal,
                 embed=(accum is None))

        def tt(eng, out, in0, in1, op, reads, writes):
            S.op(eng, lambda e: e.tensor_tensor(out=out, in0=in0, in1=in1, op=op), reads, writes)

        def ts(eng, out, in0, s1, s2, op0, op1, reads, writes):
            if op1 is None:
                S.op(eng, lambda e: e.tensor_scalar(out=out, in0=in0, scalar1=s1, scalar2=None, op0=op0), reads, writes)
            else:
                S.op(eng, lambda e: e.tensor_scalar(out=out, in0=in0, scalar1=s1, scalar2=s2, op0=op0, op1=op1), reads, writes)

        def stt(out, in0, scalar, in1, op0, op1, reads, writes, signal=True):
            S.op("dve", lambda e: e.scalar_tensor_tensor(out=out, in0=in0, scalar=scalar, in1=in1, op0=op0, op1=op1),
                 reads, writes, signal)

        def cpy(eng, out, in_, reads, writes):
            if eng == "act":
                S.op("act", lambda e: e.copy(out=out, in_=in_), reads, writes, embed=True)
            else:
                S.op(eng, lambda e: e.tensor_copy(out=out, in_=in_), reads, writes)

        def red(out, in_, reads, writes):
            S.op("dve", lambda e: e.tensor_reduce(out=out, in_=in_, axis=AX.X, op=ALU.add), reads, writes)

        def rsqrt(out, in_, bias, in_b, out_b, tmp, tmp_b):
            n = tmp.shape[-1]
            ts("pool", tmp, in_, float(bias), None, ALU.add, None, [in_b], [tmp_b])
            tt("pool", out, tmp, NEGH[0:tmp.shape[0], 0:n], ALU.pow, [tmp_b, NEGH.b], [out_b])

        def mset(eng, ap, val, writes):
            S.op(eng, lambda e: e.memset(ap, val), (), writes)

        TAB = [sb(es, "tab%d" % i, [128, D]) for i in range(3)]
        TABS = [sb(es, "tabs%d" % i, [NS, D]) for i in range(3)]
        GF = sb(es, "gf", [128, D])
        IDF = sb(es, "idf", [128, 128]); IDB = sb(es, "idb", [128, 128], BF16)
        CTP = sb(es, "ctp", [128, KC, 128], BF16); CTS = sb(es, "cts", [128, KC, NS], BF16)
        X1S = sb(es, "x1samp", [NS, D])
        LS = sb(es, "ls", [128, 2]); CW = sb(es, "cw", [128, NJ, 3]); CB = sb(es, "cb", [128, NJ])
        SS = sb(es, "ss", [128, 8]); RS = sb(es, "rs", [128, 8]); SQ = sb(es, "sq", [128, 8])
        TRB = ps(es, "trb", [128, 1024], BF16)
        PB = [ps(es, "pb%d" % i, [128, 512]) for i in range(7)]

        NEGH = sb(es, "negh", [128, 8])
        mset("pool", NEGH[:], -0.5, [NEGH.b])
        S.dma("sp", IDF[:], ident[:, :], (), [IDF.b])
        cpy("dve", IDB[:], IDF[:], [IDF.b], [IDB.b])
        S.dma("sp", LS[:], ls_pool[:, :], (), [LS.b])
        S.dma("sp", CW[:].rearrange("p j i -> p (j i)"), cwT[:, :], (), [CW.b])
        S.dma("sp", CB[:], cbT[:, :], (), [CB.b])

        def norm_mod_a(M, src, src_b, tabG, tabSH, tmpA, hbf, sidx):
            act(tmpA[0:M, :], src, AF.Square, [src_b], [tmpA.b, SS.b], accum=SS[0:M, sidx:sidx + 1])
            rsqrt(RS[0:M, sidx:sidx + 1], SS[0:M, sidx:sidx + 1], D * EPS, SS.b, RS.b, SQ[0:M, sidx:sidx + 1], SQ.b)
            stt(tmpA[0:M, :], src, RS[0:M, sidx:sidx + 1], tabG[0:M, :], ALU.mult, ALU.mult,
                [src_b, RS.b, tabG.b], [tmpA.b])
            tt("pool", hbf[0:M, :], tmpA[0:M, :], tabSH[0:M, :], ALU.add, [tmpA.b, tabSH.b], [hbf.b])

        def norm_mod_b(M, hbf, dstT, dst_b, col0, ncols, trb=None):
            trb = TRB if trb is None else trb
            for kc in range(KC):
                tr(trb[:, kc * ncols: kc * ncols + M], hbf[0:M, kc * 128:(kc + 1) * 128], IDB[0:M, 0:M],
                   [hbf.b, IDB.b], [trb.b], kc == KC - 1)
            cpy("act", dstT[:, :, col0:col0 + M],
                trb[:, 0:KC * ncols].rearrange("p (k m) -> p k m", k=KC)[:, :, 0:M], [trb.b], [dst_b])

        def norm_mod_T(M, src, src_b, tabG, tabSH, tmpA, hbf, dstT, dst_b, col0, ncols, sidx):
            norm_mod_a(M, src, src_b, tabG, tabSH, tmpA, hbf, sidx)
            norm_mod_b(M, hbf, dstT, dst_b, col0, ncols)

        def ada_tables(groups, gvec, pst, nbuf=2, hooks=()):
            with ExitStack() as st:
                STG = [sb(st, "stg%d" % i, [128, KC, D], BF16) for i in range(nbuf)]
                BB = [sb(st, "bb%d" % i, [128, D]) for i in range(nbuf)]
                GV = sb(st, "gv", [128, D])
                S.dma("sp", GV[:], gvec[0, :].partition_broadcast(128), (), [GV.b])
                wv = w_ada.rearrange("(k p) c -> p k c", p=128)
                hooks = list(hooks)

                def issue(gi):
                    m = groups[gi][0]
                    stg = STG[gi % nbuf]; bb = BB[gi % nbuf]
                    S.dma("pool", stg[:], wv[:, :, m * D:(m + 1) * D], (), [stg.b])
                    S.dma("sp", bb[:], b_ada[0, m * D:(m + 1) * D].partition_broadcast(128), (), [bb.b])
                    if gi < len(hooks) and hooks[gi] is not None:
                        hooks[gi]()

                for gi in range(min(nbuf, len(groups))):
                    issue(gi)
                for gi, (m, ti, kind) in enumerate(groups):
                    stg = STG[gi % nbuf]; bb = BB[gi % nbuf]
                    for (M, ct, tabs) in ((128, CTP, TAB), (NS, CTS, TABS)):
                        for n in range(2):
                            bank = pst[(2 * gi + n) % len(pst)]
                            for kc in range(KC):
                                mm(bank[0:M, :], ct[:, kc, 0:M], stg[:, kc, n * 512:(n + 1) * 512], kc == 0, kc == KC - 1,
                                   [ct.b, stg.b], [bank.b], kc == KC - 1)
                            tt("dve", tabs[ti][0:M, n * 512:(n + 1) * 512], bank[0:M, :], bb[0:M, n * 512:(n + 1) * 512],
                               ALU.add, [bank.b, bb.b], [tabs[ti].b])
                        if kind == "sc":
                            stt(tabs[ti][0:M, :], tabs[ti][0:M, :], 1.0, GV[0:M, :], ALU.add, ALU.mult,
                                [tabs[ti].b, GV.b], [tabs[ti].b])
                            ts("dve", tabs[ti][0:M, :], tabs[ti][0:M, :], float(math.sqrt(D)), None, ALU.mult, None,
                               [tabs[ti].b], [tabs[ti].b])
                    if gi + nbuf < len(groups):
                        issue(gi + nbuf)

        st1 = ExitStack()
        st1.__enter__()
        WIN = sb(st1, "w_in", [128, KC, INC], BF16)
        WOUT = sb(st1, "w_out", [128, KC, D], BF16)
        WINb = [Buf("w_in%d" % k) for k in range(KC)]
        WOUTb = [Buf("w_out%d" % k) for k in range(KC)]
        CONST = sb(st1, "consts", [128, NCONST])
        WPB = sb(st1, "wpb", [128, 2, 128], BF16)
        S.dma("sp", CONST[:], consts[:, :], (), [CONST.b])

        with ExitStack() as st:
            CP = sb(st, "cpt", [128, D]); CS = sb(st, "cst", [NS, D]); CB16 = sb(st, "cb16", [128, D], BF16)
            S.dma("sp", CP[:], cp[0, :].partition_broadcast(128), (), [CP.b])
            S.dma("sp", CS[:], cs[:, :], (), [CS.b])
            for (M, src, dst) in ((128, CP, CTP), (NS, CS, CTS)):
                act(CB16[0:M, :], src[0:M, :], AF.Silu, [src.b], [CB16.b])
                for kc in range(KC):
                    tr(TRB[:, kc * 128: kc * 128 + M], CB16[0:M, kc * 128:(kc + 1) * 128], IDB[0:M, 0:M],
                       [CB16.b, IDB.b], [TRB.b], kc == KC - 1)
                cpy("act", dst[:, :, 0:M], TRB[:, :].rearrange("p (k m) -> p k m", k=KC)[:, :, 0:M], [TRB.b], [dst.b])
            S.dma("sp", GF[:], g_fin[0, :].partition_broadcast(128), (), [GF.b])
            ts("pool", GF[:], GF[:], float(math.sqrt(D)), None, ALU.mult, None, [GF.b], [GF.b])

        S.barrier(False)
        def load_mixer_weights():
            wiv = w_in.rearrange("(k p) (a c) -> p k a c", p=128, a=2)
            for kc in range(KC):
                S.dma("pool", WIN[:, kc, :].rearrange("p (a c) -> p a c", a=2), wiv[:, kc, :, :], (), [WINb[kc]])
            wov = w_out.rearrange("(k p) c -> p k c", p=128)
            for kc in range(0, KC, 2):
                S.dma("pool", WOUT[:, kc:kc + 2, :], wov[:, kc:kc + 2, :], (), [WOUTb[kc], WOUTb[kc + 1]])
            mset("pool", WPB[:], 0.0, [WPB.b])
            for g in range(4):
                pp = (g % 2) * 64
                S.dma("pool", WPB[pp:pp + 64, g // 2, pp:pp + 64], w_pool[g * 64:(g + 1) * 64, :], (), [WPB.b])

        def cst(name, rows=128):
            o, n = COFF[name]
            return CONST[0:rows, o:o + n]

        ada_tables([(0, 0, "sh"), (1, 1, "sc"), (2, 2, "gt")], g_mix, PB, nbuf=3, hooks=[None, None, load_mixer_weights])
        S.barrier(False)

        ring = {"i": 0}
        RING = PB[1:7]
        UY = PB[0]
        reserved = []

        def nb():
            while True:
                b = RING[ring["i"] % len(RING)]
                ring["i"] += 1
                if b not in reserved:
                    return b

        def rope_block(M, bank, blk, rt, dst, dst_b):
            pv = bank[0:M, :].rearrange("p (h s d) -> p h s d", h=4, s=2)
            cosb = rt[0:M, 0:64].unsqueeze(1).unsqueeze(1).to_broadcast([M, 4, 2, 64])
            sinb = rt[0:M, 64:128].unsqueeze(1).to_broadcast([M, 4, 64])
            nsinb = rt[0:M, 128:192].unsqueeze(1).to_broadcast([M, 4, 64])
            rav = RA[0:M, :].rearrange("p (h s d) -> p h s d", h=4, s=2)
            rbv = RB[0:M, :].rearrange("p (h s d) -> p h s d", h=4, s=2)
            tt("dve", rav, pv, cosb, ALU.mult, [bank.b, rt.b], [RA.b])
            tt("dve", rbv[:, :, 0, :], pv[:, :, 1, :], nsinb, ALU.mult, [bank.b, rt.b], [RB.b])
            tt("dve", rbv[:, :, 1, :], pv[:, :, 0, :], sinb, ALU.mult, [bank.b, rt.b], [RB.b])
            tt("pool", dst[0:M, blk * 512:(blk + 1) * 512], RA[0:M, :], RB[0:M, :], ALU.add, [RA.b, RB.b], [dst_b])

        def zblock(M, ht, blk):
            bank = nb()
            c0 = PW + blk * 512
            for kc in range(KC):
                mm(bank[0:M, :], ht[:, kc, 0:M], WIN[:, kc, c0:c0 + 512], kc == 0, kc == KC - 1,
                   [ht.b, WINb[kc]], [bank.b], kc == KC - 1)
            return bank

        def groupnorm_gate(M, OA, OB, sg, ret):
            act(TMPB[0:M, 0:512], OA[0:M, :], AF.Square, [OA.b], [TMPB.b])
            act(TMPB[0:M, 512:768], OB[0:M, 0:256], AF.Square, [OB.b], [TMPB.b])
            red(ST[0:M, 0:4], OA[0:M, :].rearrange("p (h d) -> p h d", h=4), [OA.b], [ST.b])
            red(ST[0:M, 4:6], OB[0:M, 0:256].rearrange("p (h d) -> p h d", h=2), [OB.b], [ST.b])
            red(ST[0:M, 6:12], TMPB[0:M, 0:768].rearrange("p (h d) -> p h d", h=6), [TMPB.b], [ST.b])
            ts("dve", ST[0:M, 12:18], ST[0:M, 0:6], 1.0 / HD, None, ALU.mult, None, [ST.b], [ST.b])
            tt("dve", ST[0:M, 18:24], ST[0:M, 12:18], ST[0:M, 12:18], ALU.mult, [ST.b], [ST.b])
            stt(ST[0:M, 24:30], ST[0:M, 6:12], 1.0 / HD, ST[0:M, 18:24], ALU.mult, ALU.subtract, [ST.b], [ST.b])
            rsqrt(RSTD[0:M, 0:6], ST[0:M, 24:30], EPS, ST.b, RSTD.b, SQ[0:M, 2:8], SQ.b)
            for h in range(H):
                src = OA[0:M, h * 128:(h + 1) * 128] if h < 4 else OB[0:M, (h - 4) * 128:(h - 3) * 128]
                srcb = OA.b if h < 4 else OB.b
                stt(ON[0:M, h * 128:(h + 1) * 128], src, ST[0:M, 12 + h:13 + h], sg[0:M, h * 128:(h + 1) * 128],
                    ALU.subtract, ALU.mult, [srcb, ST.b, sg.b], [ON.b], signal=(h == H - 1))
            for h in range(H):
                act(ret[0:M, h * 128:(h + 1) * 128], ON[0:M, h * 128:(h + 1) * 128], AF.Identity, [ON.b, RSTD.b], [ret.b],
                    scale=RSTD[0:M, h:h + 1], signal=(h == H - 1))

        def wout_res(M, mixt, xres, xres_b, tabGT, dst, dst_b):
            for n in range(2):
                bank = nb()
                for kc in range(KC):
                    mm(bank[0:M, :], mixt[:, kc, 0:M], WOUT[:, kc, n * 512:(n + 1) * 512], kc == 0, kc == KC - 1,
                       [mixt.b, WOUTb[kc]], [bank.b], kc == KC - 1)
                tt("dve", TMPB[0:M, n * 512:(n + 1) * 512], bank[0:M, :], tabGT[0:M, n * 512:(n + 1) * 512], ALU.mult,
                   [bank.b, tabGT.b], [TMPB.b])
            tt("pool", dst, TMPB[0:M, :], xres, ALU.add, [TMPB.b, xres_b], [dst_b])


        with ExitStack() as st:
            XT = [sb(st, "xt%d" % i, [128, D]) for i in range(4)]
            TMPA = sb(st, "tmpa", [128, D]); TMPB = sb(st, "tmpb", [128, D])
            HBF = sb(st, "hbf", [128, D], BF16)
            HT = [sb(st, "ht%d" % i, [128, KC, 128], BF16) for i in range(2)]
            RT = [sb(st, "rt%d" % i, [128, 192]) for i in range(2)]
            RA = sb(st, "ropea", [128, 512]); RB = sb(st, "ropeb", [128, 512])
            QK2 = [sb(st, "qk%d" % i, [128, 2 * RW], BF16) for i in range(2)]
            VB2 = [sb(st, "vb%d" % i, [128, RW], BF16) for i in range(2)]
            SG2 = [sb(st, "sg%d" % i, [128, RW]) for i in range(2)]
            QST = sb(st, "qst", [128, H, 128], BF16); KT = sb(st, "kt", [128, H, 128], BF16)
            KD = sb(st, "kd", [128, RW], BF16); ATT = sb(st, "att", [128, RW], BF16)
            ON = sb(st, "on", [128, RW]); RET = sb(st, "ret", [128, RW], BF16)
            MIXT2 = [sb(st, "mixt%d" % i, [128, KC, 128], BF16) for i in range(2)]
            UT = sb(st, "ut", [128, 2, 144])
            S2 = sb(st, "s2", [128, 2, 144]); S4 = sb(st, "s4", [128, 2, 144]); S8 = sb(st, "s8", [128, 144])
            WS = sb(st, "wsum", [128, 2, 128]); PT = sb(st, "pt", [128, 2, 128], BF16)
            S32 = sb(st, "s32", [128, RW]); SBF = sb(st, "sbf", [128, RW], BF16)
            ST = sb(st, "stat", [128, 32]); RSTD = sb(st, "rstd", [128, 8])
            X12 = [sb(st, "x1_%d" % i, [128, D]) for i in range(2)]
            NP = sb(st, "npool", [16, PW])
            TRB2 = T(PB[6][:, :].bitcast(BF16), "trb2")
            TRB2.b = PB[6].b
            ZB = [PB[1], PB[2]]
            R0 = PB[3]; R1 = PB[4]; R2 = PB[5]
            zc = {"i": 0}

            mset("dve", UT[:], 0.0, [UT.b])

            def stL(t):
                xt = XT[t % 4]
                S.dma("sp", xt[:], xp[t * 128:(t + 1) * 128, :], (), [xt.b])

            def stFa(t):
                xt = XT[t % 4]
                norm_mod_a(128, xt[:], xt.b, TAB[1], TAB[0], TMPA, HBF, 0)

            def stFb(t):
                ht = HT[t % 2]
                S.dma("sp", RT[t % 2][:], rope[t * 128:(t + 1) * 128, :], (), [RT[t % 2].b])
                norm_mod_b(128, HBF, ht, ht.b, 0, 128)

            def stF(t):
                stL(t)
                stFa(t)
                stFb(t)

            zbank = {}

            def stZa(t, blk):
                ht = HT[t % 2]
                bank = ZB[zc["i"] % 2]
                zc["i"] += 1
                zbank[(t, blk)] = bank
                c0 = PW + blk * 512
                for kc in range(KC):
                    mm(bank[:, :], ht[:, kc, :], WIN[:, kc, c0:c0 + 512], kc == 0, kc == KC - 1,
                       [ht.b, WINb[kc]], [bank.b], kc == KC - 1)

            def stZb(t, blk):
                p = t % 2
                ht = HT[p]
                bank = zbank.pop((t, blk))
                if blk < 3:
                    rope_block(128, bank, blk, RT[p], QK2[p], QK2[p].b)
                elif blk == 3:
                    cpy("act", VB2[p][:, 0:512], bank[:, :], [bank.b], [VB2[p].b])
                elif blk == 4:
                    cpy("act", VB2[p][:, 512:768], bank[:, 0:256], [bank.b], [VB2[p].b])
                    act(SG2[p][:, 0:256], bank[:, 256:512], AF.Silu, [bank.b], [SG2[p].b])
                else:
                    act(SG2[p][:, 256:768], bank[:, :], AF.Silu, [bank.b], [SG2[p].b])
                    if t == NT - 1:
                        bk = ZB[zc["i"] % 2]
                        zc["i"] += 1
                        for kc in range(KC):
                            mm(bk[0:15, 0:PW], ht[:, kc, 113:128], WIN[:, kc, 0:PW], kc == 0, kc == KC - 1,
                               [ht.b, WINb[kc]], [bk.b], kc == KC - 1)
                        cpy("act", NP[0:15, :], bk[0:15, 0:PW], [bk.b], [NP.b])
                        S.dma("sp", npool_p[:, :], NP[0:15, :], [NP.b], ())

            def stZ(t, blk):
                stZa(t, blk)
                stZb(t, blk)

            def stUa(t):
                p = t % 2
                ht = HT[p]
                for c in range(2):
                    for kc in range(KC):
                        mm(UY[:, c * 128:(c + 1) * 128], WIN[:, kc, c * 128:(c + 1) * 128], ht[:, kc, :], kc == 0, kc == KC - 1,
                           [ht.b, WINb[kc]], [UY.b], kc == KC - 1 and c == 1)
                if t > 0:
                    cpy("pool", UT[:, :, 0:16], UT[:, :, 128:144], [UT.b], [UT.b])
                cpy("act", UT[:, :, 16:144], UY[:, 0:256].rearrange("p (c m) -> p c m", c=2), [UY.b], [UT.b])
                U = UT
                tt("pool", S2[:, :, 2:144], U[:, :, 2:144], U[:, :, 1:143], ALU.add, [UT.b], [S2.b])
                cpy("pool", WS[0:64, 0, :], S2[0:64, 0, 16:144], [S2.b], [WS.b])
                tt("pool", S4[:, :, 4:144], S2[:, :, 4:144], S2[:, :, 2:142], ALU.add, [S2.b], [S4.b])
                cpy("pool", WS[64:128, 0, :], S4[64:128, 0, 16:144], [S4.b], [WS.b])
                tt("pool", S8[:, 8:144], S4[:, 1, 8:144], S4[:, 1, 4:140], ALU.add, [S4.b], [S8.b])
                cpy("pool", WS[0:64, 1, :], S8[0:64, 16:144], [S8.b], [WS.b])
                tt("pool", WS[64:128, 1, :], S8[64:128, 16:144], S8[64:128, 8:136], ALU.add, [S8.b], [WS.b])
                if t == 0:
                    tt("pool", WS[:, :, 0:16], WS[:, :, 0:16], cst("corr0").rearrange("p (c m) -> p c m", c=2), ALU.mult,
                       [WS.b, CONST.b], [WS.b])

            def stUb(t):
                mixt = MIXT2[t % 2]
                for c in range(2):
                    stt(PT[:, c, :], WS[:, c, :], cst("invw")[:, c:c + 1], UT[:, c, 16:144], ALU.mult, ALU.subtract,
                        [WS.b, CONST.b, UT.b], [PT.b])
                for c in range(2):
                    mm(UY[:, 256 + c * 128:256 + (c + 1) * 128], WPB[:, c, :], PT[:, c, :], True, True,
                       [WPB.b, PT.b], [UY.b], c == 1)
                for c in range(2):
                    act(mixt[:, c, :], UY[:, 256 + c * 128:256 + (c + 1) * 128], AF.Identity, [UY.b, LS.b], [mixt.b],
                        scale=LS[:, c:c + 1])

            def stR1a(t):
                QK = QK2[t % 2]
                for h in range(H):
                    tr(TRB2[:, h * 128:(h + 1) * 128], QK[:, h * 128:(h + 1) * 128], IDB[:, :], [QK.b, IDB.b], [TRB2.b], h == H - 1)
                tt("dve", QST[:].rearrange("p h m -> p (h m)"), TRB2[:, 0:RW], cst("qdec"), ALU.mult, [TRB2.b, CONST.b], [QST.b])
                for h in range(H):
                    tr(TRB[:, h * 128:(h + 1) * 128], QK[:, RW + h * 128:RW + (h + 1) * 128], IDB[:, :], [QK.b, IDB.b], [TRB.b],
                       h == H - 1)
                cpy("act", KT[:].rearrange("p h m -> p (h m)"), TRB[:, 0:RW], [TRB.b], [KT.b])
                for h in range(H):
                    act(KD[:, h * 128:(h + 1) * 128], QK[:, RW + h * 128:RW + (h + 1) * 128], AF.Identity, [QK.b, CONST.b], [KD.b],
                        scale=cst("kdec")[:, h:h + 1], signal=(h == H - 1))

            def stR1b(t):
                for h in range(H):
                    bank = R0 if h < 4 else R1
                    hh = h if h < 4 else h - 4
                    mm(bank[:, hh * 128:(hh + 1) * 128], KT[:, h, :], QST[:, h, :], True, True, [KT.b, QST.b], [bank.b],
                       h == 3 or h == 5)
                mk = cst("maskT")
                tt("dve", ATT[:, 0:512], R0[:, :], mk[:, 0:512], ALU.mult, [R0.b, CONST.b], [ATT.b])
                tt("dve", ATT[:, 512:768], R1[:, 0:256], mk[:, 512:768], ALU.mult, [R1.b, CONST.b], [ATT.b])

            def stR2a(t):
                VB = VB2[t % 2]
                for h in range(H):
                    bank = R0 if h < 4 else R1
                    hh = h if h < 4 else h - 4
                    last = (h == 3 or h == 5)
                    mm(bank[:, hh * 128:(hh + 1) * 128], ATT[:, h * 128:(h + 1) * 128], VB[:, h * 128:(h + 1) * 128], True, t == 0,
                       [ATT.b, VB.b], [bank.b], last and t == 0)
                    if t > 0:
                        mm(bank[:, hh * 128:(hh + 1) * 128], QST[:, h, :], SBF[:, h * 128:(h + 1) * 128], False, True,
                           [QST.b, SBF.b], [bank.b], last)
                for h in range(H):
                    if h < 4:
                        o_ = R2[:, h * 128:(h + 1) * 128]; ob = R2.b
                    else:
                        o_ = R1[:, 256 + (h - 4) * 128:256 + (h - 3) * 128]; ob = R1.b
                    mm(o_, KD[:, h * 128:(h + 1) * 128], VB[:, h * 128:(h + 1) * 128], True, True, [KD.b, VB.b], [ob],
                       h == 3 or h == 5)
                if t == 0:
                    cpy("act", S32[:, 0:512], R2[:, :], [R2.b], [S32.b])
                    cpy("act", S32[:, 512:768], R1[:, 256:512], [R1.b], [S32.b])
                else:
                    for h in range(H):
                        if h < 4:
                            i_ = R2[:, h * 128:(h + 1) * 128]; ib = R2.b
                        else:
                            i_ = R1[:, 256 + (h - 4) * 128:256 + (h - 3) * 128]; ib = R1.b
                        stt(S32[:, h * 128:(h + 1) * 128], S32[:, h * 128:(h + 1) * 128], gamC[h], i_,
                            ALU.mult, ALU.add, [S32.b, ib], [S32.b], signal=(h == H - 1))
                if t < NT - 1:
                    cpy("act", SBF[:], S32[:], [S32.b], [SBF.b])
                else:
                    S.dma("sp", nret_p.rearrange("(h k) v -> k h v", h=H), S32[:].rearrange("p (h v) -> p h v", h=H), [S32.b], ())

            def stR2b(t):
                groupnorm_gate(128, R0, R1, SG2[t % 2], RET)

            def stR2c(t):
                mixt = MIXT2[t % 2]
                for h in range(H):
                    tr(TRB2[:, h * 128:(h + 1) * 128], RET[:, h * 128:(h + 1) * 128], IDB[:, :], [RET.b, IDB.b], [TRB2.b], h == H - 1)
                cpy("act", mixt[:, 2:8, :].rearrange("p h m -> p (h m)"), TRB2[:, 0:RW], [TRB2.b], [mixt.b])

            def stW(t):
                p = t % 2
                mixt = MIXT2[p]; xt = XT[t % 4]; x1 = X12[p]
                for n in range(2):
                    bank = (R0, R2)[n]
                    for kc in range(KC):
                        mm(bank[:, :], mixt[:, kc, :], WOUT[:, kc, n * 512:(n + 1) * 512], kc == 0, kc == KC - 1,
                           [mixt.b, WOUTb[kc]], [bank.b], kc == KC - 1)
                    tt("dve", TMPB[:, n * 512:(n + 1) * 512], bank[:, :], TAB[2][:, n * 512:(n + 1) * 512], ALU.mult,
                       [bank.b, TAB[2].b], [TMPB.b])
                tt("pool", x1[:], TMPB[:], xt[:], ALU.add, [TMPB.b, xt.b], [x1.b])
                S.dma("sp", x1s[t * 128:(t + 1) * 128, :], x1[:], [x1.b], ())

            for i in range(4):
                stL(i)
            for t0 in range(2):
                stFa(t0)
                stFb(t0)
                for blk in range(6):
                    stZ(t0, blk)
                stUa(t0)
                stUb(t0)
            stFa(2)
            stFb(2)
            stR1a(0)
            stR1b(0)
            for t in range(NT):
                n1 = t + 1 < NT
                n2 = t + 2 < NT
                n3 = t + 3 < NT
                stR2a(t)
                if n1:
                    stR1a(t + 1)
                if n2:
                    stZa(t + 2, 0)
                    stZa(t + 2, 1)
                if n3:
                    stFa(t + 3)
                stR2b(t)
                if n2:
                    stZb(t + 2, 0)
                    stZb(t + 2, 1)
                    stZa(t + 2, 2)
                if n1:
                    stR1b(t + 1)
                if n2:
                    stUa(t + 2)
                stR2c(t)
                stW(t)
                if n2:
                    stZb(t + 2, 2)
                    stZa(t + 2, 3)
                if n3:
                    stFb(t + 3)
                if n2:
                    stZb(t + 2, 3)
                    stZa(t + 2, 4)
                    stZb(t + 2, 4)
                    stZa(t + 2, 5)
                    stZb(t + 2, 5)
                    stUb(t + 2)
                if t + 4 < NT:
                    stL(t + 4)

        S.barrier(False)
        with ExitStack() as st:
            XT = [sb(st, "xts", [NS, D])]
            TMPA = sb(st, "tmpas", [NS, D]); TMPB = sb(st, "tmpbs", [NS, D])
            HBF = sb(st, "hbfs", [NS, D], BF16)
            HT = [sb(st, "hts", [128, KC, 128], BF16)]
            RT = [sb(st, "rts", [NS, 192])]
            RA = sb(st, "ropeas", [NS, 512]); RB = sb(st, "ropebs", [NS, 512])
            SG = sb(st, "sgs", [NS, RW]); ON = sb(st, "ons", [NS, RW]); RET = sb(st, "rets", [NS, RW], BF16)
            MIXT = sb(st, "mixts", [128, KC, 128], BF16)
            PT = sb(st, "pts", [128, 2, 128], BF16)
            ST = sb(st, "stats", [NS, 32]); RSTD = sb(st, "rstds", [NS, 8])
            S32 = sb(st, "ois", [NS, RW])
            SIN_ = [sb(st, "sin%d" % i, [128, RW]) for i in range(3)]
            SOUT = [sb(st, "sout%d" % i, [128, RW]) for i in range(2)]
            QM = [sb(st, "qm%d" % i, [128, H, NS], BF16) for i in range(2)]
            KM = [sb(st, "km%d" % i, [NS, RW], BF16) for i in range(2)]
            SP0 = sb(st, "sp0", [120, PW]); SP1 = sb(st, "sp1", [120, PW])
            QKF = sb(st, "qkf", [NS, 2 * RW]); VF = sb(st, "vf", [NS, RW]); QTS = sb(st, "qts", [128, H, NS], BF16)
            VF16 = sb(st, "vf16", [NS, RW], BF16)
            SB16 = [sb(st, "sb16_%d" % i, [128, RW], BF16) for i in range(2)]
            UNEW = sb(st, "unew", [NS, PW])
            M = NS
            xts = XT[0]; hts = HT[0]; rts = RT[0]
            S.dma("sp", xts[0:M, :], xs[:, :], (), [xts.b])
            S.dma("sp", rts[0:M, :], rope[L:L + M, :], (), [rts.b])
            S.dma("sp", SP0[:], spool[0:120, :], (), [SP0.b])
            S.dma("sp", SP1[:], spool[120:240, :], (), [SP1.b])
            norm_mod_T(M, xts[0:M, :], xts.b, TABS[1], TABS[0], TMPA, HBF, hts, hts.b, 0, 128, 0)
            bk = nb()
            for kc in range(KC):
                mm(bk[0:M, 0:PW], hts[:, kc, 0:M], WIN[:, kc, 0:PW], kc == 0, kc == KC - 1, [hts.b, WINb[kc]], [bk.b], kc == KC - 1)
            cpy("act", UNEW[:], bk[0:M, 0:PW], [bk.b], [UNEW.b])
            npv = npool_s.rearrange("(b r) c -> b r c", r=15)
            S.dma("sp", npv[:, 14, :], UNEW[:], [UNEW.b], ())
            S.dma("sp", npv[:, 0:14, :], spool.rearrange("(b r) c -> b r c", r=15)[:, 1:15, :], (), ())
            selv = cst("sel").rearrange("p (g k m) -> p g k m", g=4, k=3)
            for c in range(2):
                for gg in range(2):
                    g = 2 * c + gg
                    bank = nb()
                    mm(bank[:, 0:M], SP0[:, c * 128:(c + 1) * 128], selv[0:120, g, 0, :], True, False, [SP0.b, CONST.b], [bank.b], False)
                    mm(bank[:, 0:M], SP1[:, c * 128:(c + 1) * 128], selv[0:120, g, 1, :], False, False, [SP1.b, CONST.b], [bank.b], False)
                    mm(bank[:, 0:M], UNEW[:, c * 128:(c + 1) * 128], selv[0:M, g, 2, :], False, True, [UNEW.b, CONST.b], [bank.b], True)
                    cpy("act", PT[gg * 64:(gg + 1) * 64, c, 0:M], bank[gg * 64:(gg + 1) * 64, 0:M], [bank.b], [PT.b])
            for c in range(2):
                mm(UY[:, 256 + c * 128:256 + c * 128 + M], WPB[:, c, :], PT[:, c, 0:M], True, True, [WPB.b, PT.b], [UY.b], c == 1)
            for c in range(2):
                act(MIXT[:, c, 0:M], UY[:, 256 + c * 128:256 + c * 128 + M], AF.Identity, [UY.b, LS.b], [MIXT.b], scale=LS[:, c:c + 1])
            for blk in range(3):
                bank = zblock(M, hts, blk)
                rope_block(M, bank, blk, rts, QKF, QKF.b)
            b3 = zblock(M, hts, 3)
            cpy("act", VF[:, 0:512], b3[0:M, :], [b3.b], [VF.b])
            b4 = zblock(M, hts, 4)
            cpy("act", VF[:, 512:768], b4[0:M, 0:256], [b4.b], [VF.b])
            act(SG[0:M, 0:256], b4[0:M, 256:512], AF.Silu, [b4.b], [SG.b])
            b5 = zblock(M, hts, 5)
            act(SG[0:M, 256:768], b5[0:M, :], AF.Silu, [b5.b], [SG.b])
            ts("pool", QKF[:, RW:2 * RW], QKF[:, RW:2 * RW], float(HD ** -0.5), None, ALU.mult, None, [QKF.b], [QKF.b])
            tt("dve", TMPB[0:M, 0:RW], QKF[:, 0:RW], QKF[:, RW:2 * RW], ALU.mult, [QKF.b], [TMPB.b])
            red(ST[0:M, 0:6], TMPB[0:M, 0:RW].rearrange("p (h d) -> p h d", h=H), [TMPB.b], [ST.b])
            for h in range(H):
                bk = nb()
                tr(bk[:, 0:M], QKF[:, h * 128:(h + 1) * 128], IDF[0:M, 0:M], [QKF.b, IDF.b], [bk.b], True)
                cpy("act", QTS[:, h, :], bk[:, 0:M], [bk.b], [QTS.b])
            OI = T(S32[0:NS, :], "oi")
            OI.b = S32.b
            cpy("act", VF16[:], VF[:], [VF.b], [VF16.b])
            OIA = nb(); OIB = nb()
            reserved.extend([OIA, OIB])
            gam = cst("gam")
            for b in range(NS):
                si = SIN_[b % 3]; so = SOUT[b % 2]; km = KM[b % 2]; qm = QM[b % 2]
                mset("pool", qm[:].rearrange("p h m -> p (h m)"), 0.0, [qm.b])
                cpy("pool", qm[:, :, b], QTS[:, :, b], [QTS.b], [qm.b])
                if b == 0:
                    for b2 in range(2):
                        S.dma("sp", SIN_[b2][:].rearrange("p (h v) -> p h v", h=H), sret[b2].rearrange("h k v -> k h v"), (),
                              [SIN_[b2].b])
                if b + 2 < NS:
                    sn = SIN_[(b + 2) % 3]
                    S.dma("sp", sn[:].rearrange("p (h v) -> p h v", h=H), sret[b + 2].rearrange("h k v -> k h v"), (), [sn.b])
                ts("dve", km[:], QKF[:, RW:2 * RW], IDF[0:M, b:b + 1], None, ALU.mult, None, [QKF.b, IDF.b], [km.b])
                s16 = SB16[b % 2]
                cpy("act", s16[:], si[:], [si.b], [s16.b])
                for h in range(H):
                    bank = OIA if h < 4 else OIB
                    hh = h if h < 4 else h - 4
                    S.op("pe", (lambda o_, l_, r_, st_, sp_: (lambda e: e.matmul(o_, lhsT=l_, rhs=r_, start=st_, stop=sp_,
                                                                                   skip_group_check=True)))(
                        bank[0:M, hh * 128:(hh + 1) * 128], qm[:, h, :], s16[:, h * 128:(h + 1) * 128],
                        b == 0 and hh == 0, b == NS - 1),
                        [qm.b, s16.b], [bank.b], b == NS - 1 and (h == 3 or h == 5))
                SA = nb(); SBk = nb()
                for h in range(H):
                    bank = SA if h < 4 else SBk
                    hh = h if h < 4 else h - 4
                    mm(bank[:, hh * 128:(hh + 1) * 128], km[:, h * 128:(h + 1) * 128], VF16[:, h * 128:(h + 1) * 128], True, True,
                       [km.b, VF16.b], [bank.b], h == 3 or h == 5)
                for h in range(H):
                    bank = SA if h < 4 else SBk
                    hh = h if h < 4 else h - 4
                    stt(so[:, h * 128:(h + 1) * 128], si[:, h * 128:(h + 1) * 128], gam1[h], bank[:, hh * 128:(hh + 1) * 128],
                        ALU.mult, ALU.add, [si.b, bank.b], [so.b])
                S.dma("sp", nret_s[b].rearrange("h k v -> k h v"), so[:].rearrange("p (h v) -> p h v", h=H), [so.b], ())
            tt("dve", OI[:, 0:512], OIA[0:M, :], gam[0:M, 0:512], ALU.mult, [OIA.b, CONST.b], [OI.b])
            tt("dve", OI[:, 512:768], OIB[0:M, 0:256], gam[0:M, 512:768], ALU.mult, [OIB.b, CONST.b], [OI.b])
            for h in range(H):
                stt(OI[:, h * 128:(h + 1) * 128], VF[:, h * 128:(h + 1) * 128], ST[0:M, h:h + 1], OI[:, h * 128:(h + 1) * 128],
                    ALU.mult, ALU.add, [VF.b, ST.b, OI.b], [OI.b])

            class _V:
                def __init__(self, base, off):
                    self.base = base; self.off = off; self.b = base.b

                def __getitem__(self, k):
                    r, c = k
                    if isinstance(c, slice):
                        c0 = 0 if c.start is None else c.start
                        c1 = (512 if self.off == 0 else 256) if c.stop is None else c.stop
                        return self.base.t[r, self.off + c0:self.off + c1]
                    raise KeyError

            if debug:
                S.dma("sp", dbg_oi[:, :], OI[:, :], [OI.b], ())
            groupnorm_gate(M, _V(OI, 0), _V(OI, 512), SG, RET)
            if debug:
                S.dma("pool", dbg_ret[:, :], RET[0:M, :], [RET.b], ())
            for h in range(H):
                tr(TRB[:, h * 128:h * 128 + M], RET[0:M, h * 128:(h + 1) * 128], IDB[0:M, 0:M], [RET.b, IDB.b], [TRB.b], h == H - 1)
            cpy("act", MIXT[:, 2:8, 0:M], TRB[:, 0:RW].rearrange("p (h m) -> p h m", h=H)[:, :, 0:M], [TRB.b], [MIXT.b])
            wout_res(M, MIXT, xts[0:M, :], xts.b, TABS[2], X1S[:], X1S.b)
            if debug:
                S.dma("pool", dbg_mix.rearrange("p (k m) -> p k m", k=KC), MIXT[:, :, 0:M], [MIXT.b], ())
                S.dma("sp", dbg_x1[:, :], X1S[:], [X1S.b], ())

        st1.__exit__(None, None, None)
        S.barrier(True)

        st2 = ExitStack()
        st2.__enter__()
        WFI = sb(st2, "w_fi", [128, NJ, KC, 2, 128], BF16)
        WFO = sb(st2, "w_fo", [128, NJ, D], BF16)
        WFIb = [Buf("w_fi%d" % j) for j in range(NJ)]
        WFOb = [Buf("w_fo%d" % j) for j in range(NJ)]
        fov = w_fo.rearrange("(j p) c -> p j c", p=128)

        def load_ffn(j0, j1):
            def go():
                for j in range(j0, j1):
                    S.dma("pool", WFI[:, j].rearrange("p (k2 k) a c -> p k2 (k a c)", k2=2),
                          w_fi[j].rearrange("p (k2 r) -> p k2 r", k2=2), (), [WFIb[j]])
                    if j % 2 == 1:
                        S.dma("pool", WFO[:, j - 1:j + 1, :], fov[:, j - 1:j + 1, :], (), [WFOb[j - 1], WFOb[j]])
            return go

        ada_tables([(3, 0, "sh"), (4, 1, "sc"), (5, 2, "gt")], g_ffn, PB, nbuf=1,
                   hooks=[load_ffn(0, 2), load_ffn(2, 4), load_ffn(4, NJ)])
        S.barrier(False)

        with ExitStack() as st:
            X1C = sb(st, "x1c", [128, 4, D])
            X1Cb = [Buf("x1c%d" % i) for i in range(4)]
            TMPA = sb(st, "tmpa2", [128, D]); TMPB = sb(st, "tmpb2", [128, D])
            HBF = sb(st, "hbf2", [128, D], BF16)
            H2TS = [sb(st, "h2t%d" % i, [128, KC, 256], BF16) for i in range(2)]
            TT_ = [sb(st, "tt%d" % i, [128, 256]) for i in range(2)]
            GJ = [sb(st, "gj%d" % i, [128, 256], BF16) for i in range(3)]
            HIST = sb(st, "hist", [128, NJ, 2]); HC = sb(st, "hc", [128, NJ, 2]); HTMP = sb(st, "htmp", [128, NJ, 2])
            GS = sb(st, "gs", [128, NJ, NS], BF16)
            ABK = [PB[0], PB[1], PB[6]]
            FB = [[PB[2], PB[3]], [PB[4], PB[5]]]

            def prep_load(sti):
                for i in range(2):
                    tix = 2 * sti + i
                    slot = (sti % 2) * 2 + i
                    S.dma("sp", X1C[:, slot, :], x1s[tix * 128:(tix + 1) * 128, :], (), [X1Cb[slot]])

            def prep_a(sti, i):
                slot = (sti % 2) * 2 + i
                norm_mod_a(128, X1C[:, slot, :], X1Cb[slot], TAB[1], TAB[0], TMPA, HBF, i)

            def prep_b(sti, i):
                norm_mod_b(128, HBF, H2TS[sti % 2], H2TS[sti % 2].b, i * 128, 128)

            def prep(sti):
                prep_load(sti)
                for i in range(2):
                    prep_a(sti, i)
                    prep_b(sti, i)

            def ab(sti, j):
                bank = ABK[j % 3]
                h2t = H2TS[sti % 2]
                for half in range(2):
                    for kc in range(KC):
                        mm(bank[:, half * 256:(half + 1) * 256], WFI[:, j, kc, half, :], h2t[:, kc, :], kc == 0, kc == KC - 1,
                           [WFIb[j], h2t.b], [bank.b], kc == KC - 1 and half == 1)

            def cx(sti, j):
                bank = ABK[j % 3]; tb = TT_[j % 2]
                act(tb[:], bank[:, 0:256], AF.Identity, [bank.b, CW.b, CB.b], [tb.b], scale=CW[:, j, 2:3], bias=CB[:, j:j + 1])
                stt(tb[:, 1:256], bank[:, 0:255], CW[:, j, 1:2], tb[:, 1:256], ALU.mult, ALU.add, [bank.b, CW.b, tb.b], [tb.b])
                stt(tb[:, 2:256], bank[:, 0:254], CW[:, j, 0:1], tb[:, 2:256], ALU.mult, ALU.add, [bank.b, CW.b, tb.b], [tb.b])
                if sti > 0:
                    tt("dve", tb[:, 0:2], tb[:, 0:2], HC[:, j, :], ALU.add, [tb.b, HC.b], [tb.b])
                cpy("act", HIST[:, j, :], bank[:, 254:256], [bank.b], [HIST.b])

            def cy(j):
                bank = ABK[j % 3]; tb = TT_[j % 2]; gj = GJ[j % 3]
                act(tb[:], tb[:], AF.Silu, [tb.b], [tb.b])
                tt("dve", gj[:], tb[:], bank[:, 256:512], ALU.mult, [tb.b, bank.b], [gj.b])

            def ffn_out(j):
                gj = GJ[j % 3]
                for tix in range(2):
                    for n in range(2):
                        mm(FB[tix][n][:, :], gj[:, tix * 128:(tix + 1) * 128], WFO[:, j, n * 512:(n + 1) * 512], j == 0, j == NJ - 1,
                           [gj.b, WFOb[j]], [FB[tix][n].b], j == NJ - 1)

            def fin_evac(sti):
                tt("pool", HTMP[:, :, 0], HIST[:, :, 1], CW[:, :, 1], ALU.mult, [HIST.b, CW.b], [HTMP.b])
                tt("pool", HTMP[:, :, 1], HIST[:, :, 0], CW[:, :, 0], ALU.mult, [HIST.b, CW.b], [HTMP.b])
                tt("pool", HC[:, :, 0], HTMP[:, :, 0], HTMP[:, :, 1], ALU.add, [HTMP.b], [HC.b])
                tt("pool", HC[:, :, 1], HIST[:, :, 1], CW[:, :, 0], ALU.mult, [HIST.b, CW.b], [HC.b])
                for i in range(2):
                    stg = (TMPB, TMPA)[i]
                    for n in range(2):
                        tt("dve", stg[:, n * 512:(n + 1) * 512], FB[i][n][:, :], TAB[2][:, n * 512:(n + 1) * 512], ALU.mult,
                           [FB[i][n].b, TAB[2].b], [stg.b])

            def fin_rest(sti, i):
                tix = 2 * sti + i
                slot = (sti % 2) * 2 + i
                stg = (TMPB, TMPA)[i]
                tt("pool", stg[:], stg[:], X1C[:, slot, :], ALU.add, [stg.b, X1Cb[slot]], [stg.b])
                act(HBF[:], stg[:], AF.Square, [stg.b], [HBF.b, SS.b], accum=SS[:, 2 + i:3 + i])
                rsqrt(RS[:, 2 + i:3 + i], SS[:, 2 + i:3 + i], D * EPS, SS.b, RS.b, SQ[:, 2 + i:3 + i], SQ.b)
                stt(X1C[:, slot, :], stg[:], RS[:, 2 + i:3 + i], GF[:], ALU.mult, ALU.mult, [stg.b, RS.b, GF.b], [X1Cb[slot]])
                S.dma("sp", yp[tix * 128:(tix + 1) * 128, :], X1C[:, slot, :], [X1Cb[slot]], ())

            NST = NT // 2
            prep(0)
            for sti in range(NST):
                nxt = sti + 1 < NST
                for j in range(NJ):
                    ab(sti, j)
                    if j >= 2:
                        ffn_out(j - 2)
                    cx(sti, j)
                    if j >= 1:
                        cy(j - 1)
                    if sti > 0 and j == 1:
                        fin_rest(sti - 1, 0)
                    if sti > 0 and j == 3:
                        fin_rest(sti - 1, 1)
                    if nxt:
                        if j == 5:
                            prep_load(sti + 1)
                        if j == 7:
                            prep_a(sti + 1, 0)
                        if j == 10:
                            prep_b(sti + 1, 0)
                        if j == 12:
                            prep_a(sti + 1, 1)
                        if j == 15:
                            prep_b(sti + 1, 1)
                cy(NJ - 1)
                ffn_out(NJ - 2)
                ffn_out(NJ - 1)
                fin_evac(sti)
            fin_rest(NST - 1, 0)
            fin_rest(NST - 1, 1)

            S.barrier(False)
            H2T = H2TS[0]
            flat = X1C[:, 2:4, :].rearrange("p a d -> p (a d)")
            SCT = T(flat[:, 0:704].rearrange("p (j m) -> p j m", j=NJ), "sct")
            AH = T(flat[:, 704:1100].rearrange("p (j m) -> p j m", j=NJ), "ah")
            TS_ = T(flat[:, 1100:1452].rearrange("p (j m) -> p j m", j=NJ), "tsamp")
            TS2 = T(flat[:, 1452:1804].rearrange("p (j m) -> p j m", j=NJ), "tsamp2")
            M = NS
            norm_mod_T(M, X1S[:], X1S.b, TABS[1], TABS[0], TMPA, HBF, H2T, H2T.b, 0, 128, 0)
            AALL = PB[0]; BALL = PB[1]
            for j in range(NJ):
                for half, bank in ((0, AALL), (1, BALL)):
                    for kc in range(KC):
                        mm(bank[:, j * M:(j + 1) * M], WFI[:, j, kc, half, :], H2T[:, kc, 0:M], kc == 0, kc == KC - 1,
                           [WFIb[j], H2T.b], [bank.b], kc == KC - 1 and j == NJ - 1)
            SCA = PB[2]; SCB = PB[3]
            for q in range(3):
                c0 = q * 1024
                w = min(1024, DFF - c0)
                S.dma("sp", X1C[0:2 * NS, q % 2, 0:w], sconv[:, c0:c0 + w], (), [X1Cb[q % 2]])
                for jl in range(w // 128):
                    j = q * 8 + jl
                    bank = SCA if j < 11 else SCB
                    jj = j if j < 11 else j - 11
                    tr(bank[:, jj * 32:(jj + 1) * 32], X1C[0:2 * NS, q % 2, jl * 128:(jl + 1) * 128], IDF[0:32, 0:32],
                       [X1Cb[q % 2], IDF.b], [bank.b], True)
            cpy("act", SCT[:, 0:11, :], SCA[:, 0:352].rearrange("p (j m) -> p j m", j=11), [SCA.b], [SCT.b])
            cpy("act", SCT[:, 11:22, :], SCB[:, 0:352].rearrange("p (j m) -> p j m", j=11), [SCB.b], [SCT.b])
            av = AALL[:, 0:NJ * M].rearrange("p (j m) -> p j m", j=NJ)
            bv = BALL[:, 0:NJ * M].rearrange("p (j m) -> p j m", j=NJ)
            sctv = SCT[:].rearrange("p j (b r) -> p j b r", r=2)

            def bc(ap2):
                return ap2.unsqueeze(2).to_broadcast([128, NJ, M])

            cpy("act", AH[:, :, 2:2 + M], av, [AALL.b], [AH.b])
            cpy("pool", AH[:, :, 0:2], HIST[:], [HIST.b], [AH.b])
            tt("dve", TS_[:], av, bc(CW[:, :, 2]), ALU.mult, [AALL.b, CW.b], [TS_.b])
            tt("pool", TS2[:], sctv[:, :, :, 1], bc(CW[:, :, 1]), ALU.mult, [SCT.b, CW.b], [TS2.b])
            tt("dve", TS_[:], TS_[:], TS2[:], ALU.add, [TS_.b, TS2.b], [TS_.b])
            tt("pool", TS2[:], sctv[:, :, :, 0], bc(CW[:, :, 0]), ALU.mult, [SCT.b, CW.b], [TS2.b])
            tt("dve", TS_[:], TS_[:], TS2[:], ALU.add, [TS_.b, TS2.b], [TS_.b])
            tt("dve", TS_[:], TS_[:], bc(CB[:, :]), ALU.add, [TS_.b, CB.b], [TS_.b])
            act(TS2[:], TS_[:], AF.Silu, [TS_.b], [TS2.b])
            tt("dve", GS[:], TS2[:], bv, ALU.mult, [TS2.b, BALL.b], [GS.b])
            FS = [PB[4], PB[5]]
            for n in range(2):
                for j in range(NJ):
                    mm(FS[n][0:M, :], GS[:, j, :], WFO[:, j, n * 512:(n + 1) * 512], j == 0, j == NJ - 1,
                       [GS.b, WFOb[j]], [FS[n].b], j == NJ - 1)
                tt("dve", TMPB[0:M, n * 512:(n + 1) * 512], FS[n][0:M, :], TABS[2][0:M, n * 512:(n + 1) * 512], ALU.mult,
                   [FS[n].b, TABS[2].b], [TMPB.b])
            tt("pool", TMPB[0:M, :], TMPB[0:M, :], X1S[:], ALU.add, [TMPB.b, X1S.b], [TMPB.b])
            act(TMPA[0:M, :], TMPB[0:M, :], AF.Square, [TMPB.b], [TMPA.b, SS.b], accum=SS[0:M, 0:1])
            rsqrt(RS[0:M, 0:1], SS[0:M, 0:1], D * EPS, SS.b, RS.b, SQ[0:M, 0:1], SQ.b)
            stt(X1C[0:M, 0, :], TMPB[0:M, :], RS[0:M, 0:1], GF[0:M, :], ALU.mult, ALU.mult, [TMPB.b, RS.b, GF.b], [X1Cb[0]])
            S.dma("sp", ys[:, :], X1C[0:M, 0, :], [X1Cb[0]], ())
            CT = [PB[6], PB[2], PB[3], PB[0], PB[1], PB[4]]
            ncv = nconv_s.rearrange("(b r) c -> b r c", r=2)
            NCVb = [TMPA.b, TMPA.b]
            for g in range(6):
                bank = CT[g]
                nj = min(4, NJ - 4 * g)
                for jj in range(nj):
                    j = 4 * g + jj
                    tr(bank[0:2 + M, jj * 128:(jj + 1) * 128], AH[:, j, :], IDF[:, :], [AH.b, IDF.b], [bank.b], jj == nj - 1)
                w = nj * 128
                piece = TMPA[0:2 + M, (g % 2) * 512:(g % 2) * 512 + w]
                cpy("act", piece, bank[0:2 + M, 0:w], [bank.b], [NCVb[g % 2]])
                S.dma("sp", nconv_p[:, g * 512:g * 512 + w], TMPA[0:2, (g % 2) * 512:(g % 2) * 512 + w], [NCVb[g % 2]], ())
                S.dma("sp", ncv[:, 1, g * 512:g * 512 + w], TMPA[2:2 + M, (g % 2) * 512:(g % 2) * 512 + w], [NCVb[g % 2]], ())
            S.dma("sp", ncv[:, 0, :], sconv.rearrange("(b r) c -> b r c", r=2)[:, 1, :], (), ())
        st2.__exit__(None, None, None)

        S.finish()
        with nc.Block() as block:
            @block.sync
            def _(e):
                S.replay(e, "sp")

            @block.tensor
            def _(e):
                S.replay(e, "pe")

            @block.scalar
            def _(e):
                S.replay(e, "act")

            @block.vector
            def _(e):
                S.replay(e, "dve")

            @block.gpsimd
            def _(e):
                S.replay(e, "pool")
    return nc


_CACHE = {}


def kernel(x_prompt, x_sample, c_prompt, c_sample, state_pool, state_ret, state_conv,
           g_mix, g_ffn, w_ada, b_ada, w_in, w_pool, ls_pool, w_out,
           w_ffn_in, conv_w, conv_b, w_ffn_out, g_final):
    f = lambda a: np.ascontiguousarray(np.asarray(a, dtype=np.float32))
    consts, _, rope = _host_consts()
    ident = np.eye(128, dtype=np.float32)
    x_prompt = f(x_prompt); x_sample = f(x_sample); c_prompt = f(c_prompt); c_sample = f(c_sample)
    state_pool = f(state_pool); state_ret = f(state_ret); state_conv = f(state_conv)
    shared = {
        "g_mix": f(g_mix).reshape(1, D), "g_ffn": f(g_ffn).reshape(1, D), "g_fin": f(g_final).reshape(1, D),
        "w_ada": f(w_ada)[0], "b_ada": f(b_ada).reshape(1, 6 * D),
        "w_in": f(w_in)[0], "w_pool": f(w_pool).reshape(PW, 64),
        "ls_pool": f(f(ls_pool).reshape(2, 128).T),
        "w_out": f(w_out)[0],
        "w_fi": f(f(w_ffn_in)[0].reshape(KC, 128, 2, NJ, 128).transpose(3, 1, 0, 2, 4).reshape(NJ, 128, KC * 256)),
        "w_fo": f(w_ffn_out)[0],
        "cwT": f(f(conv_w)[0].reshape(3, NJ, 128).transpose(2, 1, 0).reshape(128, NJ * 3)),
        "cbT": f(f(conv_b)[0].reshape(NJ, 128).T),
        "consts": consts, "ident": ident, "rope": rope,
    }
    in_maps = []
    for i in range(NCORES):
        m = dict(shared)
        sl = slice(i * NS, (i + 1) * NS)
        m["xp"] = x_prompt[i]
        m["xs"] = f(x_sample[sl, 0, :])
        m["cp"] = f(c_prompt[i:i + 1])
        m["cs"] = f(c_sample[sl])
        m["spool"] = f(state_pool[0, sl].reshape(NS * 15, PW))
        m["sret"] = f(state_ret[0, sl])
        m["sconv"] = f(state_conv[0, sl].reshape(NS * 2, DFF))
        in_maps.append(m)
    if "nc" not in _CACHE:
        _CACHE["nc"] = build_nc()
    res = run_bass_kernel_spmd(_CACHE["nc"], in_maps, core_ids=list(range(NCORES)))
    R = res.results
    cat = lambda k: np.stack([np.asarray(r[k], dtype=np.float32) for r in R], axis=0)
    y_p = cat("yp")
    y_s = cat("ys").reshape(NCORES * NS, 1, D)
    np_p = cat("npool_p").reshape(1, NCORES, 15, PW)
    nr_p = cat("nret_p").reshape(1, NCORES, H, HD, HD)
    nc_p = cat("nconv_p").reshape(1, NCORES, 2, DFF)
    np_s = cat("npool_s").reshape(1, NCORES * NS, 15, PW)
    nr_s = cat("nret_s").reshape(1, NCORES * NS, H, HD, HD)
    nc_s = cat("nconv_s").reshape(1, NCORES * NS, 2, DFF)
    return (y_p, y_s, np_p, nr_p, nc_p, np_s, nr_s, nc_s)
```

```python
import math
from contextlib import ExitStack

import numpy as np
import concourse.bass as bass
import concourse.mybir as mybir
from concourse.bass_utils import run_bass_kernel_spmd

F32 = mybir.dt.float32
BF16 = mybir.dt.bfloat16
AF = mybir.ActivationFunctionType
ALU = mybir.AluOpType
AX = mybir.AxisListType

NCORES = 8
D = 1024
L = 2048
NT = 16
NS = 16
H = 6
HD = 128
RW = 768
PW = 256
INC = 3328
DFF = 2816
NJ = 22
PAST = 16384
EPS = 1e-6
KC = 8


class Buf:
    __slots__ = ("name", "w", "r")

    def __init__(self, name):
        self.name = name
        self.w = None
        self.r = {}


class Sched:
    ENG = ("pe", "act", "dve", "pool", "sp")

    def __init__(self, nc, es, ring_sizes):
        self.nc = nc
        self.q = {n: [] for n in self.ENG}
        self.sem = {}
        self.cnt = {}
        for n in ("pe", "act", "dve", "pool"):
            self.sem[n] = es.enter_context(nc.semaphore("s_" + n))
            self.cnt[n] = 0
        self.know = {n: {} for n in self.ENG}
        self.snap = {}
        self.tokseq = {}
        self.seq = 0
        self.ring = {}
        self.rpos = {}
        for qn, sz in ring_sizes.items():
            lst = []
            for i in range(sz):
                key = "d_%s_%d" % (qn, i)
                self.sem[key] = es.enter_context(nc.semaphore(key))
                lst.append([key, 0])
            self.ring[qn] = lst
            self.rpos[qn] = 0
        self.nwait = 0

    def _merge(self, eng, key, val):
        kn = self.know[eng]
        if kn.get(key, 0) < val:
            kn[key] = val
        sn = self.snap.get((key, val))
        if sn is not None:
            for k2, v2 in sn.items():
                if kn.get(k2, 0) < v2:
                    kn[k2] = v2

    def _waits(self, eng, deps, is_dma=False):
        need = {}
        for tok, kind in deps:
            key, val, src = tok
            if src == eng and not is_dma:
                if eng == "pe":
                    continue
            if src is not None and src in self.cnt and val > self.cnt[src]:
                if src == eng:
                    continue
                raise RuntimeError("dependency on unsignalled op of %s" % src)
            if need.get(key, 0) < val:
                need[key] = val
        order = sorted(need.items(), key=lambda kv: -self.tokseq.get(kv, 0))
        for key, val in order:
            if self.know[eng].get(key, 0) >= val:
                continue
            self.q[eng].append(("wait", key, val))
            self.nwait += 1
            self._merge(eng, key, val)

    def _deps(self, reads, writes):
        deps = []
        for b in reads:
            if b.w is not None:
                deps.append((b.w, "raw"))
        for b in writes:
            if b.w is not None:
                deps.append((b.w, "waw"))
            for t in b.r.values():
                deps.append((t, "war"))
        return deps

    def op(self, eng, fn, reads=(), writes=(), signal=True, embed=None):
        if embed is None:
            embed = eng in ("dve", "pool", "pe")
        self._waits(eng, self._deps(reads, writes))
        if signal:
            self.cnt[eng] += 1
            tok = (eng, self.cnt[eng], eng)
            sn = dict(self.know[eng])
            sn[eng] = self.cnt[eng]
            self.snap[(eng, self.cnt[eng])] = sn
            self.seq += 1
            self.tokseq[(eng, self.cnt[eng])] = self.seq
        else:
            tok = (eng, self.cnt[eng] + 1, eng)
        self.q[eng].append(("op", fn, signal, embed))
        for b in reads:
            b.r[eng] = tok
        for b in writes:
            b.w = tok
            b.r = {}
        return tok

    def dma(self, qn, out, in_, reads=(), writes=(), **kw):
        ring = self.ring[qn]
        i = self.rpos[qn]
        self.rpos[qn] = (i + 1) % len(ring)
        key, val = ring[i]
        deps = self._deps(reads, writes)
        if val > 0:
            deps.append(((key, val, None), "raw"))
        self._waits(qn, deps, is_dma=True)
        ring[i][1] = val + 16
        tok = (key, val + 16, None)
        sn = dict(self.know[qn])
        sn[key] = val + 16
        self.snap[(key, val + 16)] = sn
        self.seq += 1
        self.tokseq[(key, val + 16)] = self.seq
        self.q[qn].append(("dma", out, in_, key, kw))
        for b in reads:
            b.r[key] = tok
        for b in writes:
            b.w = tok
            b.r = {}
        return tok

    def barrier(self, include_pool_ring=True):
        toks = []
        for n in ("pe", "act", "dve", "pool"):
            if self.cnt[n] > 0:
                toks.append((n, self.cnt[n], n))
        for qn, ring in self.ring.items():
            if qn == "pool" and not include_pool_ring:
                continue
            for key, val in ring:
                if val > 0:
                    toks.append((key, val, None))
        for eng in self.ENG:
            for (key, val, src) in toks:
                if src == eng and eng != "pool":
                    continue
                if self.know[eng].get(key, 0) >= val:
                    continue
                self.q[eng].append(("wait", key, val))
                self._merge(eng, key, val)

    def finish(self):
        for qn, ring in self.ring.items():
            for key, val in ring:
                if val > 0 and self.know["sp"].get(key, 0) < val:
                    self.q["sp"].append(("wait", key, val))
                    self._merge("sp", key, val)
        for n in ("pe", "act", "dve", "pool"):
            if self.cnt[n] > 0:
                self.q["sp"].append(("wait", n, self.cnt[n]))

    def replay(self, e, qn):
        q = self.q[qn]
        pend = None
        for i, it in enumerate(q):
            if it[0] == "wait":
                if pend is not None:
                    e.wait_ge(self.sem[pend[1]], pend[2])
                    pend = None
                nxt = q[i + 1] if i + 1 < len(q) else None
                if nxt is not None and nxt[0] == "op" and nxt[3]:
                    pend = it
                else:
                    e.wait_ge(self.sem[it[1]], it[2])
            elif it[0] == "op":
                ins = it[1](e)
                if pend is not None:
                    ins._wait_ge(self.sem[pend[1]], pend[2])
                    pend = None
                if it[2]:
                    ins.then_inc(self.sem[qn], 1)
            else:
                _, out, in_, key, kw = it
                e.dma_start(out=out, in_=in_, **kw).then_inc(self.sem[key], 16)


class T:
    def __init__(self, t, name):
        self.t = t
        self.b = Buf(name)

    def __getitem__(self, k):
        return self.t[k]


def _log_gammas():
    return np.log(1.0 - np.exp2(-5.0 - np.arange(H, dtype=np.float64)))


def _host_consts():
    lg = _log_gammas()
    c = {}
    i = np.arange(128, dtype=np.float64)
    qdec = np.exp(lg[:, None] * (i[None, :] + 1.0)).reshape(1, RW)
    c["qdec"] = np.broadcast_to(qdec, (128, RW))
    j = i
    m = np.exp(-lg[None, :, None] * (j[:, None, None] + 1.0)) * (HD ** -0.5)
    m = m * (i[None, None, :] >= j[:, None, None])
    c["maskT"] = m.reshape(128, RW)
    c["kdec"] = np.exp(lg[None, :] * (127.0 - j[:, None])) * (HD ** -0.5)
    c["gam"] = np.broadcast_to(np.repeat(np.exp(lg), HD)[None, :], (128, RW))
    win = np.array([2, 4, 8, 16], dtype=np.float64)
    p = np.arange(128)
    invw = np.zeros((128, 2))
    corr = np.zeros((128, 2, 16))
    for cc in range(2):
        w = win[2 * cc + p // 64]
        invw[:, cc] = 1.0 / w
        tt = np.arange(16, dtype=np.float64)
        corr[:, cc, :] = w[:, None] / np.minimum(w[:, None], tt[None, :] + 1.0)
    c["invw"] = invw
    c["corr0"] = corr.reshape(128, 32)
    sel = np.zeros((128, 4, 3, 16))
    for g in range(4):
        w = int(win[g])
        for half in range(2):
            for bb in range(8):
                for r in range(15):
                    if r >= 16 - w:
                        sel[bb * 15 + r, g, half, half * 8 + bb] = 1.0 / w
        for bb in range(16):
            sel[bb, g, 2, bb] = 1.0 / w - 1.0
    c["sel"] = sel.reshape(128, 192)
    names = ["qdec", "maskT", "kdec", "gam", "invw", "corr0", "sel"]
    offs = {}
    o = 0
    cols = []
    for n in names:
        a = np.asarray(c[n], dtype=np.float64)
        offs[n] = (o, a.shape[1])
        o += a.shape[1]
        cols.append(a)
    arr = np.concatenate(cols, axis=1).astype(np.float32)
    half = HD // 2
    inv = (np.float32(10000.0) ** (-(np.arange(half, dtype=np.float32) / np.float32(half)))).astype(np.float32)
    pos = np.concatenate([np.arange(L), np.full(NS, PAST)]).astype(np.float32)
    ang = (pos[:, None] * inv[None, :]).astype(np.float32).astype(np.float64)
    rope = np.concatenate([np.cos(ang), np.sin(ang), -np.sin(ang)], axis=1).astype(np.float32)
    return arr, offs, rope


def build_nc(debug=False):
    CONSTS, COFF, _ = _host_consts()
    NCONST = CONSTS.shape[1]
    lg = _log_gammas()
    gamC = [float(np.exp(lg[h] * 128.0)) for h in range(H)]
    gam1 = [float(np.exp(lg[h])) for h in range(H)]

    nc = bass.Bass("TRN2", target_bir_lowering=False)

    def din(name, shape):
        return nc.dram_tensor(name, list(shape), F32, kind="ExternalInput").ap()

    def dout(name, shape):
        return nc.dram_tensor(name, list(shape), F32, kind="ExternalOutput").ap()

    xp = din("xp", [L, D]); xs = din("xs", [NS, D]); cp = din("cp", [1, D]); cs = din("cs", [NS, D])
    spool = din("spool", [NS * 15, PW]); sret = din("sret", [NS, H, HD, HD]); sconv = din("sconv", [NS * 2, DFF])
    g_mix = din("g_mix", [1, D]); g_ffn = din("g_ffn", [1, D]); g_fin = din("g_fin", [1, D])
    w_ada = din("w_ada", [D, 6 * D]); b_ada = din("b_ada", [1, 6 * D])
    w_in = din("w_in", [D, INC]); w_pool = din("w_pool", [PW, 64]); ls_pool = din("ls_pool", [128, 2])
    w_out = din("w_out", [D, D]); w_fi = din("w_fi", [NJ, 128, KC * 256]); w_fo = din("w_fo", [DFF, D])
    cwT = din("cwT", [128, NJ * 3]); cbT = din("cbT", [128, NJ])
    consts = din("consts", [128, NCONST]); ident = din("ident", [128, 128]); rope = din("rope", [L + NS, 192])

    yp = dout("yp", [L, D]); ys = dout("ys", [NS, D])
    npool_p = dout("npool_p", [15, PW]); nret_p = dout("nret_p", [H * HD, HD]); nconv_p = dout("nconv_p", [2, DFF])
    npool_s = dout("npool_s", [NS * 15, PW]); nret_s = dout("nret_s", [NS, H, HD, HD]); nconv_s = dout("nconv_s", [NS * 2, DFF])
    x1s = nc.dram_tensor("x1s", [L, D], F32, kind="Internal").ap()
    if debug:
        dbg_mix = dout("dbg_mix", [128, KC * NS]); dbg_x1 = dout("dbg_x1", [NS, D]); dbg_oi = dout("dbg_oi", [NS, RW])
        dbg_ret = dout("dbg_ret", [NS, RW])

    es = ExitStack()
    with es:
        S = Sched(nc, es, {"sp": 24, "pool": 12})

        uniq = {"n": 0}

        def sb(stack, name, shape, dt=F32):
            uniq["n"] += 1
            return T(stack.enter_context(nc.sbuf_tensor("sb%d_%s" % (uniq["n"], name), list(shape), dt)), name)

        def ps(stack, name, shape, dt=F32):
            return T(stack.enter_context(nc.psum_tensor("ps_" + name, list(shape), dt)), name)

        def mm(out, lhsT, rhs, start, stop, reads, writes, signal):
            S.op("pe", lambda e: e.matmul(out, lhsT=lhsT, rhs=rhs, start=start, stop=stop),
                 reads, writes, signal)

        def tr(out, in_, idn, reads, writes, signal):
            S.op("pe", lambda e: e.transpose(out, in_, idn), reads, writes, signal)

        def act(out, in_, func, reads, writes, scale=None, bias=None, accum=None, signal=True):
            kw = {}
            if scale is not None:
                kw["scale"] = scale
            if bias is not None:
                kw["bias"] = bias
            if accum is not None:
                kw["accum_out"] = accum
            S.op("act", lambda e: e.activation(out=out, in_=in_, func=func, **kw), reads, writes, signal,
                 embed=(accum is None))

        def tt(eng, out, in0, in1, op, reads, writes):
            S.op(eng, lambda e: e.tensor_tensor(out=out, in0=in0, in1=in1, op=op), reads, writes)

        def ts(eng, out, in0, s1, s2, op0, op1, reads, writes):
            if op1 is None:
                S.op(eng, lambda e: e.tensor_scalar(out=out, in0=in0, scalar1=s1, scalar2=None, op0=op0), reads, writes)
            else:
                S.op(eng, lambda e: e.tensor_scalar(out=out, in0=in0, scalar1=s1, scalar2=s2, op0=op0, op1=op1), reads, writes)

        def stt(out, in0, scalar, in1, op0, op1, reads, writes, signal=True):
            S.op("dve", lambda e: e.scalar_tensor_tensor(out=out, in0=in0, scalar=scalar, in1=in1, op0=op0, op1=op1),
                 reads, writes, signal)

        def cpy(eng, out, in_, reads, writes):
            if eng == "act":
                S.op("act", lambda e: e.copy(out=out, in_=in_), reads, writes, embed=True)
            else:
                S.op(eng, lambda e: e.tensor_copy(out=out, in_=in_), reads, writes)

        def red(out, in_, reads, writes):
            S.op("dve", lambda e: e.tensor_reduce(out=out, in_=in_, axis=AX.X, op=ALU.add), reads, writes)

        def rsqrt(out, in_, bias, in_b, out_b, tmp, tmp_b):
            n = tmp.shape[-1]
            ts("pool", tmp, in_, float(bias), None, ALU.add, None, [in_b], [tmp_b])
            tt("pool", out, tmp, NEGH[0:tmp.shape[0], 0:n], ALU.pow, [tmp_b, NEGH.b], [out_b])

        def mset(eng, ap, val, writes):
            S.op(eng, lambda e: e.memset(ap, val), (), writes)

        TAB = [sb(es, "tab%d" % i, [128, D]) for i in range(3)]
        TABS = [sb(es, "tabs%d" % i, [NS, D]) for i in range(3)]
        GF = sb(es, "gf", [128, D])
        IDF = sb(es, "idf", [128, 128]); IDB = sb(es, "idb", [128, 128], BF16)
        CTP = sb(es, "ctp", [128, KC, 128], BF16); CTS = sb(es, "cts", [128, KC, NS], BF16)
        X1S = sb(es, "x1samp", [NS, D])
        LS = sb(es, "ls", [128, 2]); CW = sb(es, "cw", [128, NJ, 3]); CB = sb(es, "cb", [128, NJ])
        SS = sb(es, "ss", [128, 8]); RS = sb(es, "rs", [128, 8]); SQ = sb(es, "sq", [128, 8])
        TRB = ps(es, "trb", [128, 1024], BF16)
        PB = [ps(es, "pb%d" % i, [128, 512]) for i in range(7)]

        NEGH = sb(es, "negh", [128, 8])
        mset("pool", NEGH[:], -0.5, [NEGH.b])
        S.dma("sp", IDF[:], ident[:, :], (), [IDF.b])
        cpy("dve", IDB[:], IDF[:], [IDF.b], [IDB.b])
        S.dma("sp", LS[:], ls_pool[:, :], (), [LS.b])
        S.dma("sp", CW[:].rearrange("p j i -> p (j i)"), cwT[:, :], (), [CW.b])
        S.dma("sp", CB[:], cbT[:, :], (), [CB.b])

        def norm_mod_a(M, src, src_b, tabG, tabSH, tmpA, hbf, sidx):
            act(tmpA[0:M, :], src, AF.Square, [src_b], [tmpA.b, SS.b], accum=SS[0:M, sidx:sidx + 1])
            rsqrt(RS[0:M, sidx:sidx + 1], SS[0:M, sidx:sidx + 1], D * EPS, SS.b, RS.b, SQ[0:M, sidx:sidx + 1], SQ.b)
            stt(tmpA[0:M, :], src, RS[0:M, sidx:sidx + 1], tabG[0:M, :], ALU.mult, ALU.mult,
                [src_b, RS.b, tabG.b], [tmpA.b])
            tt("pool", hbf[0:M, :], tmpA[0:M, :], tabSH[0:M, :], ALU.add, [tmpA.b, tabSH.b], [hbf.b])

        def norm_mod_b(M, hbf, dstT, dst_b, col0, ncols, trb=None):
            trb = TRB if trb is None else trb
            for kc in range(KC):
                tr(trb[:, kc * ncols: kc * ncols + M], hbf[0:M, kc * 128:(kc + 1) * 128], IDB[0:M, 0:M],
                   [hbf.b, IDB.b], [trb.b], kc == KC - 1)
            cpy("act", dstT[:, :, col0:col0 + M],
                trb[:, 0:KC * ncols].rearrange("p (k m) -> p k m", k=KC)[:, :, 0:M], [trb.b], [dst_b])

        def norm_mod_T(M, src, src_b, tabG, tabSH, tmpA, hbf, dstT, dst_b, col0, ncols, sidx):
            norm_mod_a(M, src, src_b, tabG, tabSH, tmpA, hbf, sidx)
            norm_mod_b(M, hbf, dstT, dst_b, col0, ncols)

        def ada_tables(groups, gvec, pst, nbuf=2, hooks=()):
            with ExitStack() as st:
                STG = [sb(st, "stg%d" % i, [128, KC, D], BF16) for i in range(nbuf)]
                BB = [sb(st, "bb%d" % i, [128, D]) for i in range(nbuf)]
                GV = sb(st, "gv", [128, D])
                S.dma("sp", GV[:], gvec[0, :].partition_broadcast(128), (), [GV.b])
                wv = w_ada.rearrange("(k p) c -> p k c", p=128)
                hooks = list(hooks)

                def issue(gi):
                    m = groups[gi][0]
                    stg = STG[gi % nbuf]; bb = BB[gi % nbuf]
                    S.dma("pool", stg[:], wv[:, :, m * D:(m + 1) * D], (), [stg.b])
                    S.dma("sp", bb[:], b_ada[0, m * D:(m + 1) * D].partition_broadcast(128), (), [bb.b])
                    if gi < len(hooks) and hooks[gi] is not None:
                        hooks[gi]()

                for gi in range(min(nbuf, len(groups))):
                    issue(gi)
                for gi, (m, ti, kind) in enumerate(groups):
                    stg = STG[gi % nbuf]; bb = BB[gi % nbuf]
                    for (M, ct, tabs) in ((128, CTP, TAB), (NS, CTS, TABS)):
                        for n in range(2):
                            bank = pst[(2 * gi + n) % len(pst)]
                            for kc in range(KC):
                                mm(bank[0:M, :], ct[:, kc, 0:M], stg[:, kc, n * 512:(n + 1) * 512], kc == 0, kc == KC - 1,
                                   [ct.b, stg.b], [bank.b], kc == KC - 1)
                            tt("dve", tabs[ti][0:M, n * 512:(n + 1) * 512], bank[0:M, :], bb[0:M, n * 512:(n + 1) * 512],
                               ALU.add, [bank.b, bb.b], [tabs[ti].b])
                        if kind == "sc":
                            stt(tabs[ti][0:M, :], tabs[ti][0:M, :], 1.0, GV[0:M, :], ALU.add, ALU.mult,
                                [tabs[ti].b, GV.b], [tabs[ti].b])
                            ts("dve", tabs[ti][0:M, :], tabs[ti][0:M, :], float(math.sqrt(D)), None, ALU.mult, None,
                               [tabs[ti].b], [tabs[ti].b])
                    if gi + nbuf < len(groups):
                        issue(gi + nbuf)

        st1 = ExitStack()
        st1.__enter__()
        WIN = sb(st1, "w_in", [128, KC, INC], BF16)
        WOUT = sb(st1, "w_out", [128, KC, D], BF16)
        WINb = [Buf("w_in%d" % k) for k in range(KC)]
        WOUTb = [Buf("w_out%d" % k) for k in range(KC)]
        CONST = sb(st1, "consts", [128, NCONST])
        WPB = sb(st1, "wpb", [128, 2, 128], BF16)
        S.dma("sp", CONST[:], consts[:, :], (), [CONST.b])

        with ExitStack() as st:
            CP = sb(st, "cpt", [128, D]); CS = sb(st, "cst", [NS, D]); CB16 = sb(st, "cb16", [128, D], BF16)
            S.dma("sp", CP[:], cp[0, :].partition_broadcast(128), (), [CP.b])
            S.dma("sp", CS[:], cs[:, :], (), [CS.b])
            for (M, src, dst) in ((128, CP, CTP), (NS, CS, CTS)):
                act(CB16[0:M, :], src[0:M, :], AF.Silu, [src.b], [CB16.b])
                for kc in range(KC):
                    tr(TRB[:, kc * 128: kc * 128 + M], CB16[0:M, kc * 128:(kc + 1) * 128], IDB[0:M, 0:M],
                       [CB16.b, IDB.b], [TRB.b], kc == KC - 1)
                cpy("act", dst[:, :, 0:M], TRB[:, :].rearrange("p (k m) -> p k m", k=KC)[:, :, 0:M], [TRB.b], [dst.b])
            S.dma("sp", GF[:], g_fin[0, :].partition_broadcast(128), (), [GF.b])
            ts("pool", GF[:], GF[:], float(math.sqrt(D)), None, ALU.mult, None, [GF.b], [GF.b])

        S.barrier(False)
        def load_mixer_weights():
            wiv = w_in.rearrange("(k p) (a c) -> p k a c", p=128, a=2)
            for kc in range(KC):
                S.dma("pool", WIN[:, kc, :].rearrange("p (a c) -> p a c", a=2), wiv[:, kc, :, :], (), [WINb[kc]])
            wov = w_out.rearrange("(k p) c -> p k c", p=128)
            for kc in range(0, KC, 2):
                S.dma("pool", WOUT[:, kc:kc + 2, :], wov[:, kc:kc + 2, :], (), [WOUTb[kc], WOUTb[kc + 1]])
            mset("pool", WPB[:], 0.0, [WPB.b])
            for g in range(4):
                pp = (g % 2) * 64
                S.dma("pool", WPB[pp:pp + 64, g // 2, pp:pp + 64], w_pool[g * 64:(g + 1) * 64, :], (), [WPB.b])

        def cst(name, rows=128):
            o, n = COFF[name]
            return CONST[0:rows, o:o + n]

        ada_tables([(0, 0, "sh"), (1, 1, "sc"), (2, 2, "gt")], g_mix, PB, nbuf=3, hooks=[None, None, load_mixer_weights])
        S.barrier(False)

        ring = {"i": 0}
        RING = PB[1:7]
        UY = PB[0]
        reserved = []

        def nb():
            while True:
                b = RING[ring["i"] % len(RING)]
                ring["i"] += 1
                if b not in reserved:
                    return b

        def rope_block(M, bank, blk, rt, dst, dst_b):
            pv = bank[0:M, :].rearrange("p (h s d) -> p h s d", h=4, s=2)
            cosb = rt[0:M, 0:64].unsqueeze(1).unsqueeze(1).to_broadcast([M, 4, 2, 64])
            sinb = rt[0:M, 64:128].unsqueeze(1).to_broadcast([M, 4, 64])
            nsinb = rt[0:M, 128:192].unsqueeze(1).to_broadcast([M, 4, 64])
            rav = RA[0:M, :].rearrange("p (h s d) -> p h s d", h=4, s=2)
            rbv = RB[0:M, :].rearrange("p (h s d) -> p h s d", h=4, s=2)
            tt("dve", rav, pv, cosb, ALU.mult, [bank.b, rt.b], [RA.b])
            tt("dve", rbv[:, :, 0, :], pv[:, :, 1, :], nsinb, ALU.mult, [bank.b, rt.b], [RB.b])
            tt("dve", rbv[:, :, 1, :], pv[:, :, 0, :], sinb, ALU.mult, [bank.b, rt.b], [RB.b])
            tt("pool", dst[0:M, blk * 512:(blk + 1) * 512], RA[0:M, :], RB[0:M, :], ALU.add, [RA.b, RB.b], [dst_b])

        def zblock(M, ht, blk):
            bank = nb()
            c0 = PW + blk * 512
            for kc in range(KC):
                mm(bank[0:M, :], ht[:, kc, 0:M], WIN[:, kc, c0:c0 + 512], kc == 0, kc == KC - 1,
                   [ht.b, WINb[kc]], [bank.b], kc == KC - 1)
            return bank

        def groupnorm_gate(M, OA, OB, sg, ret):
            act(TMPB[0:M, 0:512], OA[0:M, :], AF.Square, [OA.b], [TMPB.b])
            act(TMPB[0:M, 512:768], OB[0:M, 0:256], AF.Square, [OB.b], [TMPB.b])
            red(ST[0:M, 0:4], OA[0:M, :].rearrange("p (h d) -> p h d", h=4), [OA.b], [ST.b])
            red(ST[0:M, 4:6], OB[0:M, 0:256].rearrange("p (h d) -> p h d", h=2), [OB.b], [ST.b])
            red(ST[0:M, 6:12], TMPB[0:M, 0:768].rearrange("p (h d) -> p h d", h=6), [TMPB.b], [ST.b])
            ts("dve", ST[0:M, 12:18], ST[0:M, 0:6], 1.0 / HD, None, ALU.mult, None, [ST.b], [ST.b])
            tt("dve", ST[0:M, 18:24], ST[0:M, 12:18], ST[0:M, 12:18], ALU.mult, [ST.b], [ST.b])
            stt(ST[0:M, 24:30], ST[0:M, 6:12], 1.0 / HD, ST[0:M, 18:24], ALU.mult, ALU.subtract, [ST.b], [ST.b])
            rsqrt(RSTD[0:M, 0:6], ST[0:M, 24:30], EPS, ST.b, RSTD.b, SQ[0:M, 2:8], SQ.b)
            for h in range(H):
                src = OA[0:M, h * 128:(h + 1) * 128] if h < 4 else OB[0:M, (h - 4) * 128:(h - 3) * 128]
                srcb = OA.b if h < 4 else OB.b
                stt(ON[0:M, h * 128:(h + 1) * 128], src, ST[0:M, 12 + h:13 + h], sg[0:M, h * 128:(h + 1) * 128],
                    ALU.subtract, ALU.mult, [srcb, ST.b, sg.b], [ON.b], signal=(h == H - 1))
            for h in range(H):
                act(ret[0:M, h * 128:(h + 1) * 128], ON[0:M, h * 128:(h + 1) * 128], AF.Identity, [ON.b, RSTD.b], [ret.b],
                    scale=RSTD[0:M, h:h + 1], signal=(h == H - 1))

        def wout_res(M, mixt, xres, xres_b, tabGT, dst, dst_b):
            for n in range(2):
                bank = nb()
                for kc in range(KC):
                    mm(bank[0:M, :], mixt[:, kc, 0:M], WOUT[:, kc, n * 512:(n + 1) * 512], kc == 0, kc == KC - 1,
                       [mixt.b, WOUTb[kc]], [bank.b], kc == KC - 1)
                tt("dve", TMPB[0:M, n * 512:(n + 1) * 512], bank[0:M, :], tabGT[0:M, n * 512:(n + 1) * 512], ALU.mult,
                   [bank.b, tabGT.b], [TMPB.b])
            tt("pool", dst, TMPB[0:M, :], xres, ALU.add, [TMPB.b, xres_b], [dst_b])


        with ExitStack() as st:
            XT = [sb(st, "xt%d" % i, [128, D]) for i in range(4)]
            TMPA = sb(st, "tmpa", [128, D]); TMPB = sb(st, "tmpb", [128, D])
            HBF = sb(st, "hbf", [128, D], BF16)
            HT = [sb(st, "ht%d" % i, [128, KC, 128], BF16) for i in range(2)]
            RT = [sb(st, "rt%d" % i, [128, 192]) for i in range(2)]
            RA = sb(st, "ropea", [128, 512]); RB = sb(st, "ropeb", [128, 512])
            QK2 = [sb(st, "qk%d" % i, [128, 2 * RW], BF16) for i in range(2)]
            VB2 = [sb(st, "vb%d" % i, [128, RW], BF16) for i in range(2)]
            SG2 = [sb(st, "sg%d" % i, [128, RW]) for i in range(2)]
            QST = sb(st, "qst", [128, H, 128], BF16); KT = sb(st, "kt", [128, H, 128], BF16)
            KD = sb(st, "kd", [128, RW], BF16); ATT = sb(st, "att", [128, RW], BF16)
            ON = sb(st, "on", [128, RW]); RET = sb(st, "ret", [128, RW], BF16)
            MIXT2 = [sb(st, "mixt%d" % i, [128, KC, 128], BF16) for i in range(2)]
            UT = sb(st, "ut", [128, 2, 144])
            S2 = sb(st, "s2", [128, 2, 144]); S4 = sb(st, "s4", [128, 2, 144]); S8 = sb(st, "s8", [128, 144])
            WS = sb(st, "wsum", [128, 2, 128]); PT = sb(st, "pt", [128, 2, 128], BF16)
            S32 = sb(st, "s32", [128, RW]); SBF = sb(st, "sbf", [128, RW], BF16)
            ST = sb(st, "stat", [128, 32]); RSTD = sb(st, "rstd", [128, 8])
            X12 = [sb(st, "x1_%d" % i, [128, D]) for i in range(2)]
            NP = sb(st, "npool", [16, PW])
            TRB2 = T(PB[6][:, :].bitcast(BF16), "trb2")
            TRB2.b = PB[6].b
            ZB = [PB[1], PB[2]]
            R0 = PB[3]; R1 = PB[4]; R2 = PB[5]
            zc = {"i": 0}

            mset("dve", UT[:], 0.0, [UT.b])

            def stL(t):
                xt = XT[t % 4]
                S.dma("sp", xt[:], xp[t * 128:(t + 1) * 128, :], (), [xt.b])

            def stFa(t):
                xt = XT[t % 4]
                norm_mod_a(128, xt[:], xt.b, TAB[1], TAB[0], TMPA, HBF, 0)

            def stFb(t):
                ht = HT[t % 2]
                S.dma("sp", RT[t % 2][:], rope[t * 128:(t + 1) * 128, :], (), [RT[t % 2].b])
                norm_mod_b(128, HBF, ht, ht.b, 0, 128)

            def stF(t):
                stL(t)
                stFa(t)
                stFb(t)

            zbank = {}

            def stZa(t, blk):
                ht = HT[t % 2]
                bank = ZB[zc["i"] % 2]
                zc["i"] += 1
                zbank[(t, blk)] = bank
                c0 = PW + blk * 512
                for kc in range(KC):
                    mm(bank[:, :], ht[:, kc, :], WIN[:, kc, c0:c0 + 512], kc == 0, kc == KC - 1,
                       [ht.b, WINb[kc]], [bank.b], kc == KC - 1)

            def stZb(t, blk):
                p = t % 2
                ht = HT[p]
                bank = zbank.pop((t, blk))
                if blk < 3:
                    rope_block(128, bank, blk, RT[p], QK2[p], QK2[p].b)
                elif blk == 3:
                    cpy("act", VB2[p][:, 0:512], bank[:, :], [bank.b], [VB2[p].b])
                elif blk == 4:
                    cpy("act", VB2[p][:, 512:768], bank[:, 0:256], [bank.b], [VB2[p].b])
                    act(SG2[p][:, 0:256], bank[:, 256:512], AF.Silu, [bank.b], [SG2[p].b])
                else:
                    act(SG2[p][:, 256:768], bank[:, :], AF.Silu, [bank.b], [SG2[p].b])
                    if t == NT - 1:
                        bk = ZB[zc["i"] % 2]
                        zc["i"] += 1
                        for kc in range(KC):
                            mm(bk[0:15, 0:PW], ht[:, kc, 113:128], WIN[:, kc, 0:PW], kc == 0, kc == KC - 1,
                               [ht.b, WINb[kc]], [bk.b], kc == KC - 1)
                        cpy("act", NP[0:15, :], bk[0:15, 0:PW], [bk.b], [NP.b])
                        S.dma("sp", npool_p[:, :], NP[0:15, :], [NP.b], ())

            def stZ(t, blk):
                stZa(t, blk)
                stZb(t, blk)

            def stUa(t):
                p = t % 2
                ht = HT[p]
                for c in range(2):
                    for kc in range(KC):
                        mm(UY[:, c * 128:(c + 1) * 128], WIN[:, kc, c * 128:(c + 1) * 128], ht[:, kc, :], kc == 0, kc == KC - 1,
                           [ht.b, WINb[kc]], [UY.b], kc == KC - 1 and c == 1)
                if t > 0:
                    cpy("pool", UT[:, :, 0:16], UT[:, :, 128:144], [UT.b], [UT.b])
                cpy("act", UT[:, :, 16:144], UY[:, 0:256].rearrange("p (c m) -> p c m", c=2), [UY.b], [UT.b])
                U = UT
                tt("pool", S2[:, :, 2:144], U[:, :, 2:144], U[:, :, 1:143], ALU.add, [UT.b], [S2.b])
                cpy("pool", WS[0:64, 0, :], S2[0:64, 0, 16:144], [S2.b], [WS.b])
                tt("pool", S4[:, :, 4:144], S2[:, :, 4:144], S2[:, :, 2:142], ALU.add, [S2.b], [S4.b])
                cpy("pool", WS[64:128, 0, :], S4[64:128, 0, 16:144], [S4.b], [WS.b])
                tt("pool", S8[:, 8:144], S4[:, 1, 8:144], S4[:, 1, 4:140], ALU.add, [S4.b], [S8.b])
                cpy("pool", WS[0:64, 1, :], S8[0:64, 16:144], [S8.b], [WS.b])
                tt("pool", WS[64:128, 1, :], S8[64:128, 16:144], S8[64:128, 8:136], ALU.add, [S8.b], [WS.b])
                if t == 0:
                    tt("pool", WS[:, :, 0:16], WS[:, :, 0:16], cst("corr0").rearrange("p (c m) -> p c m", c=2), ALU.mult,
                       [WS.b, CONST.b], [WS.b])

            def stUb(t):
                mixt = MIXT2[t % 2]
                for c in range(2):
                    stt(PT[:, c, :], WS[:, c, :], cst("invw")[:, c:c + 1], UT[:, c, 16:144], ALU.mult, ALU.subtract,
                        [WS.b, CONST.b, UT.b], [PT.b])
                for c in range(2):
                    mm(UY[:, 256 + c * 128:256 + (c + 1) * 128], WPB[:, c, :], PT[:, c, :], True, True,
                       [WPB.b, PT.b], [UY.b], c == 1)
                for c in range(2):
                    act(mixt[:, c, :], UY[:, 256 + c * 128:256 + (c + 1) * 128], AF.Identity, [UY.b, LS.b], [mixt.b],
                        scale=LS[:, c:c + 1])

            def stR1a(t):
                QK = QK2[t % 2]
                for h in range(H):
                    tr(TRB2[:, h * 128:(h + 1) * 128], QK[:, h * 128:(h + 1) * 128], IDB[:, :], [QK.b, IDB.b], [TRB2.b], h == H - 1)
                tt("dve", QST[:].rearrange("p h m -> p (h m)"), TRB2[:, 0:RW], cst("qdec"), ALU.mult, [TRB2.b, CONST.b], [QST.b])
                for h in range(H):
                    tr(TRB[:, h * 128:(h + 1) * 128], QK[:, RW + h * 128:RW + (h + 1) * 128], IDB[:, :], [QK.b, IDB.b], [TRB.b],
                       h == H - 1)
                cpy("act", KT[:].rearrange("p h m -> p (h m)"), TRB[:, 0:RW], [TRB.b], [KT.b])
                for h in range(H):
                    act(KD[:, h * 128:(h + 1) * 128], QK[:, RW + h * 128:RW + (h + 1) * 128], AF.Identity, [QK.b, CONST.b], [KD.b],
                        scale=cst("kdec")[:, h:h + 1], signal=(h == H - 1))

            def stR1b(t):
                for h in range(H):
                    bank = R0 if h < 4 else R1
                    hh = h if h < 4 else h - 4
                    mm(bank[:, hh * 128:(hh + 1) * 128], KT[:, h, :], QST[:, h, :], True, True, [KT.b, QST.b], [bank.b],
                       h == 3 or h == 5)
                mk = cst("maskT")
                tt("dve", ATT[:, 0:512], R0[:, :], mk[:, 0:512], ALU.mult, [R0.b, CONST.b], [ATT.b])
                tt("dve", ATT[:, 512:768], R1[:, 0:256], mk[:, 512:768], ALU.mult, [R1.b, CONST.b], [ATT.b])

            def stR2a(t):
                VB = VB2[t % 2]
                for h in range(H):
                    bank = R0 if h < 4 else R1
                    hh = h if h < 4 else h - 4
                    last = (h == 3 or h == 5)
                    mm(bank[:, hh * 128:(hh + 1) * 128], ATT[:, h * 128:(h + 1) * 128], VB[:, h * 128:(h + 1) * 128], True, t == 0,
                       [ATT.b, VB.b], [bank.b], last and t == 0)
                    if t > 0:
                        mm(bank[:, hh * 128:(hh + 1) * 128], QST[:, h, :], SBF[:, h * 128:(h + 1) * 128], False, True,
                           [QST.b, SBF.b], [bank.b], last)
                for h in range(H):
                    if h < 4:
                        o_ = R2[:, h * 128:(h + 1) * 128]; ob = R2.b
                    else:
                        o_ = R1[:, 256 + (h - 4) * 128:256 + (h - 3) * 128]; ob = R1.b
                    mm(o_, KD[:, h * 128:(h + 1) * 128], VB[:, h * 128:(h + 1) * 128], True, True, [KD.b, VB.b], [ob],
                       h == 3 or h == 5)
                if t == 0:
                    cpy("act", S32[:, 0:512], R2[:, :], [R2.b], [S32.b])
                    cpy("act", S32[:, 512:768], R1[:, 256:512], [R1.b], [S32.b])
                else:
                    for h in range(H):
                        if h < 4:
                            i_ = R2[:, h * 128:(h + 1) * 128]; ib = R2.b
                        else:
                            i_ = R1[:, 256 + (h - 4) * 128:256 + (h - 3) * 128]; ib = R1.b
                        stt(S32[:, h * 128:(h + 1) * 128], S32[:, h * 128:(h + 1) * 128], gamC[h], i_,
                            ALU.mult, ALU.add, [S32.b, ib], [S32.b], signal=(h == H - 1))
                if t < NT - 1:
                    cpy("act", SBF[:], S32[:], [S32.b], [SBF.b])
                else:
                    S.dma("sp", nret_p.rearrange("(h k) v -> k h v", h=H), S32[:].rearrange("p (h v) -> p h v", h=H), [S32.b], ())

            def stR2b(t):
                groupnorm_gate(128, R0, R1, SG2[t % 2], RET)

            def stR2c(t):
                mixt = MIXT2[t % 2]
                for h in range(H):
                    tr(TRB2[:, h * 128:(h + 1) * 128], RET[:, h * 128:(h + 1) * 128], IDB[:, :], [RET.b, IDB.b], [TRB2.b], h == H - 1)
                cpy("act", mixt[:, 2:8, :].rearrange("p h m -> p (h m)"), TRB2[:, 0:RW], [TRB2.b], [mixt.b])

            def stW(t):
                p = t % 2
                mixt = MIXT2[p]; xt = XT[t % 4]; x1 = X12[p]
                for n in range(2):
                    bank = (R0, R2)[n]
                    for kc in range(KC):
                        mm(bank[:, :], mixt[:, kc, :], WOUT[:, kc, n * 512:(n + 1) * 512], kc == 0, kc == KC - 1,
                           [mixt.b, WOUTb[kc]], [bank.b], kc == KC - 1)
                    tt("dve", TMPB[:, n * 512:(n + 1) * 512], bank[:, :], TAB[2][:, n * 512:(n + 1) * 512], ALU.mult,
                       [bank.b, TAB[2].b], [TMPB.b])
                tt("pool", x1[:], TMPB[:], xt[:], ALU.add, [TMPB.b, xt.b], [x1.b])
                S.dma("sp", x1s[t * 128:(t + 1) * 128, :], x1[:], [x1.b], ())

            for i in range(4):
                stL(i)
            for t0 in range(2):
                stFa(t0)
                stFb(t0)
                for blk in range(6):
                    stZ(t0, blk)
                stUa(t0)
                stUb(t0)
            stFa(2)
            stFb(2)
            stR1a(0)
            stR1b(0)
            for t in range(NT):
                n1 = t + 1 < NT
                n2 = t + 2 < NT
                n3 = t + 3 < NT
                stR2a(t)
                if n1:
                    stR1a(t + 1)
                if n2:
                    stZa(t + 2, 0)
                    stZa(t + 2, 1)
                if n3:
                    stFa(t + 3)
                stR2b(t)
                if n2:
                    stZb(t + 2, 0)
                    stZb(t + 2, 1)
                    stZa(t + 2, 2)
                if n1:
                    stR1b(t + 1)
                if n2:
                    stUa(t + 2)
                stR2c(t)
                stW(t)
                if n2:
                    stZb(t + 2, 2)
                    stZa(t + 2, 3)
                if n3:
                    stFb(t + 3)
                if n2:
                    stZb(t + 2, 3)
                    stZa(t + 2, 4)
                    stZb(t + 2, 4)
                    stZa(t + 2, 5)
                    stZb(t + 2, 5)
                    stUb(t + 2)
                if t + 4 < NT:
                    stL(t + 4)

        S.barrier(False)
        with ExitStack() as st:
            XT = [sb(st, "xts", [NS, D])]
            TMPA = sb(st, "tmpas", [NS, D]); TMPB = sb(st, "tmpbs", [NS, D])
            HBF = sb(st, "hbfs", [NS, D], BF16)
            HT = [sb(st, "hts", [128, KC, 128], BF16)]
            RT = [sb(st, "rts", [NS, 192])]
            RA = sb(st, "ropeas", [NS, 512]); RB = sb(st, "ropebs", [NS, 512])
            SG = sb(st, "sgs", [NS, RW]); ON = sb(st, "ons", [NS, RW]); RET = sb(st, "rets", [NS, RW], BF16)
            MIXT = sb(st, "mixts", [128, KC, 128], BF16)
            PT = sb(st, "pts", [128, 2, 128], BF16)
            ST = sb(st, "stats", [NS, 32]); RSTD = sb(st, "rstds", [NS, 8])
            S32 = sb(st, "ois", [NS, RW])
            SIN_ = [sb(st, "sin%d" % i, [128, RW]) for i in range(3)]
            SOUT = [sb(st, "sout%d" % i, [128, RW]) for i in range(2)]
            QM = [sb(st, "qm%d" % i, [128, H, NS], BF16) for i in range(2)]
            KM = [sb(st, "km%d" % i, [NS, RW], BF16) for i in range(2)]
            SP0 = sb(st, "sp0", [120, PW]); SP1 = sb(st, "sp1", [120, PW])
            QKF = sb(st, "qkf", [NS, 2 * RW]); VF = sb(st, "vf", [NS, RW]); QTS = sb(st, "qts", [128, H, NS], BF16)
            VF16 = sb(st, "vf16", [NS, RW], BF16)
            SB16 = [sb(st, "sb16_%d" % i, [128, RW], BF16) for i in range(2)]
            UNEW = sb(st, "unew", [NS, PW])
            M = NS
            xts = XT[0]; hts = HT[0]; rts = RT[0]
            S.dma("sp", xts[0:M, :], xs[:, :], (), [xts.b])
            S.dma("sp", rts[0:M, :], rope[L:L + M, :], (), [rts.b])
            S.dma("sp", SP0[:], spool[0:120, :], (), [SP0.b])
            S.dma("sp", SP1[:], spool[120:240, :], (), [SP1.b])
            norm_mod_T(M, xts[0:M, :], xts.b, TABS[1], TABS[0], TMPA, HBF, hts, hts.b, 0, 128, 0)
            bk = nb()
            for kc in range(KC):
                mm(bk[0:M, 0:PW], hts[:, kc, 0:M], WIN[:, kc, 0:PW], kc == 0, kc == KC - 1, [hts.b, WINb[kc]], [bk.b], kc == KC - 1)
            cpy("act", UNEW[:], bk[0:M, 0:PW], [bk.b], [UNEW.b])
            npv = npool_s.rearrange("(b r) c -> b r c", r=15)
            S.dma("sp", npv[:, 14, :], UNEW[:], [UNEW.b], ())
            S.dma("sp", npv[:, 0:14, :], spool.rearrange("(b r) c -> b r c", r=15)[:, 1:15, :], (), ())
            selv = cst("sel").rearrange("p (g k m) -> p g k m", g=4, k=3)
            for c in range(2):
                for gg in range(2):
                    g = 2 * c + gg
                    bank = nb()
                    mm(bank[:, 0:M], SP0[:, c * 128:(c + 1) * 128], selv[0:120, g, 0, :], True, False, [SP0.b, CONST.b], [bank.b], False)
                    mm(bank[:, 0:M], SP1[:, c * 128:(c + 1) * 128], selv[0:120, g, 1, :], False, False, [SP1.b, CONST.b], [bank.b], False)
                    mm(bank[:, 0:M], UNEW[:, c * 128:(c + 1) * 128], selv[0:M, g, 2, :], False, True, [UNEW.b, CONST.b], [bank.b], True)
                    cpy("act", PT[gg * 64:(gg + 1) * 64, c, 0:M], bank[gg * 64:(gg + 1) * 64, 0:M], [bank.b], [PT.b])
            for c in range(2):
                mm(UY[:, 256 + c * 128:256 + c * 128 + M], WPB[:, c, :], PT[:, c, 0:M], True, True, [WPB.b, PT.b], [UY.b], c == 1)
            for c in range(2):
                act(MIXT[:, c, 0:M], UY[:, 256 + c * 128:256 + c * 128 + M], AF.Identity, [UY.b, LS.b], [MIXT.b], scale=LS[:, c:c + 1])
            for blk in range(3):
                bank = zblock(M, hts, blk)
                rope_block(M, bank, blk, rts, QKF, QKF.b)
            b3 = zblock(M, hts, 3)
            cpy("act", VF[:, 0:512], b3[0:M, :], [b3.b], [VF.b])
            b4 = zblock(M, hts, 4)
            cpy("act", VF[:, 512:768], b4[0:M, 0:256], [b4.b], [VF.b])
            act(SG[0:M, 0:256], b4[0:M, 256:512], AF.Silu, [b4.b], [SG.b])
            b5 = zblock(M, hts, 5)
            act(SG[0:M, 256:768], b5[0:M, :], AF.Silu, [b5.b], [SG.b])
            ts("pool", QKF[:, RW:2 * RW], QKF[:, RW:2 * RW], float(HD ** -0.5), None, ALU.mult, None, [QKF.b], [QKF.b])
            tt("dve", TMPB[0:M, 0:RW], QKF[:, 0:RW], QKF[:, RW:2 * RW], ALU.mult, [QKF.b], [TMPB.b])
            red(ST[0:M, 0:6], TMPB[0:M, 0:RW].rearrange("p (h d) -> p h d", h=H), [TMPB.b], [ST.b])
            for h in range(H):
                bk = nb()
                tr(bk[:, 0:M], QKF[:, h * 128:(h + 1) * 128], IDF[0:M, 0:M], [QKF.b, IDF.b], [bk.b], True)
                cpy("act", QTS[:, h, :], bk[:, 0:M], [bk.b], [QTS.b])
            OI = T(S32[0:NS, :], "oi")
            OI.b = S32.b
            cpy("act", VF16[:], VF[:], [VF.b], [VF16.b])
            OIA = nb(); OIB = nb()
            reserved.extend([OIA, OIB])
            gam = cst("gam")
            for b in range(NS):
                si = SIN_[b % 3]; so = SOUT[b % 2]; km = KM[b % 2]; qm = QM[b % 2]
                mset("pool", qm[:].rearrange("p h m -> p (h m)"), 0.0, [qm.b])
                cpy("pool", qm[:, :, b], QTS[:, :, b], [QTS.b], [qm.b])
                if b == 0:
                    for b2 in range(2):
                        S.dma("sp", SIN_[b2][:].rearrange("p (h v) -> p h v", h=H), sret[b2].rearrange("h k v -> k h v"), (),
                              [SIN_[b2].b])
                if b + 2 < NS:
                    sn = SIN_[(b + 2) % 3]
                    S.dma("sp", sn[:].rearrange("p (h v) -> p h v", h=H), sret[b + 2].rearrange("h k v -> k h v"), (), [sn.b])
                ts("dve", km[:], QKF[:, RW:2 * RW], IDF[0:M, b:b + 1], None, ALU.mult, None, [QKF.b, IDF.b], [km.b])
                s16 = SB16[b % 2]
                cpy("act", s16[:], si[:], [si.b], [s16.b])
                for h in range(H):
                    bank = OIA if h < 4 else OIB
                    hh = h if h < 4 else h - 4
                    S.op("pe", (lambda o_, l_, r_, st_, sp_: (lambda e: e.matmul(o_, lhsT=l_, rhs=r_, start=st_, stop=sp_,
                                                                                   skip_group_check=True)))(
                        bank[0:M, hh * 128:(hh + 1) * 128], qm[:, h, :], s16[:, h * 128:(h + 1) * 128],
                        b == 0 and hh == 0, b == NS - 1),
                        [qm.b, s16.b], [bank.b], b == NS - 1 and (h == 3 or h == 5))
                SA = nb(); SBk = nb()
                for h in range(H):
                    bank = SA if h < 4 else SBk
                    hh = h if h < 4 else h - 4
                    mm(bank[:, hh * 128:(hh + 1) * 128], km[:, h * 128:(h + 1) * 128], VF16[:, h * 128:(h + 1) * 128], True, True,
                       [km.b, VF16.b], [bank.b], h == 3 or h == 5)
                for h in range(H):
                    bank = SA if h < 4 else SBk
                    hh = h if h < 4 else h - 4
                    stt(so[:, h * 128:(h + 1) * 128], si[:, h * 128:(h + 1) * 128], gam1[h], bank[:, hh * 128:(hh + 1) * 128],
                        ALU.mult, ALU.add, [si.b, bank.b], [so.b])
                S.dma("sp", nret_s[b].rearrange("h k v -> k h v"), so[:].rearrange("p (h v) -> p h v", h=H), [so.b], ())
            tt("dve", OI[:, 0:512], OIA[0:M, :], gam[0:M, 0:512], ALU.mult, [OIA.b, CONST.b], [OI.b])
            tt("dve", OI[:, 512:768], OIB[0:M, 0:256], gam[0:M, 512:768], ALU.mult, [OIB.b, CONST.b], [OI.b])
            for h in range(H):
                stt(OI[:, h * 128:(h + 1) * 128], VF[:, h * 128:(h + 1) * 128], ST[0:M, h:h + 1], OI[:, h * 128:(h + 1) * 128],
                    ALU.mult, ALU.add, [VF.b, ST.b, OI.b], [OI.b])

            class _V:
                def __init__(self, base, off):
                    self.base = base; self.off = off; self.b = base.b

                def __getitem__(self, k):
                    r, c = k
                    if isinstance(c, slice):
                        c0 = 0 if c.start is None else c.start
                        c1 = (512 if self.off == 0 else 256) if c.stop is None else c.stop
                        return self.base.t[r, self.off + c0:self.off + c1]
                    raise KeyError

            if debug:
                S.dma("sp", dbg_oi[:, :], OI[:, :], [OI.b], ())
            groupnorm_gate(M, _V(OI, 0), _V(OI, 512), SG, RET)
            if debug:
                S.dma("pool", dbg_ret[:, :], RET[0:M, :], [RET.b], ())
            for h in range(H):
                tr(TRB[:, h * 128:h * 128 + M], RET[0:M, h * 128:(h + 1) * 128], IDB[0:M, 0:M], [RET.b, IDB.b], [TRB.b], h == H - 1)
            cpy("act", MIXT[:, 2:8, 0:M], TRB[:, 0:RW].rearrange("p (h m) -> p h m", h=H)[:, :, 0:M], [TRB.b], [MIXT.b])
            wout_res(M, MIXT, xts[0:M, :], xts.b, TABS[2], X1S[:], X1S.b)
            if debug:
                S.dma("pool", dbg_mix.rearrange("p (k m) -> p k m", k=KC), MIXT[:, :, 0:M], [MIXT.b], ())
                S.dma("sp", dbg_x1[:, :], X1S[:], [X1S.b], ())

        st1.__exit__(None, None, None)
        S.barrier(True)

        st2 = ExitStack()
        st2.__enter__()
        WFI = sb(st2, "w_fi", [128, NJ, KC, 2, 128], BF16)
        WFO = sb(st2, "w_fo", [128, NJ, D], BF16)
        WFIb = [Buf("w_fi%d" % j) for j in range(NJ)]
        WFOb = [Buf("w_fo%d" % j) for j in range(NJ)]
        fov = w_fo.rearrange("(j p) c -> p j c", p=128)

        def load_ffn(j0, j1):
            def go():
                for j in range(j0, j1):
                    S.dma("pool", WFI[:, j].rearrange("p (k2 k) a c -> p k2 (k a c)", k2=2),
                          w_fi[j].rearrange("p (k2 r) -> p k2 r", k2=2), (), [WFIb[j]])
                    if j % 2 == 1:
                        S.dma("pool", WFO[:, j - 1:j + 1, :], fov[:, j - 1:j + 1, :], (), [WFOb[j - 1], WFOb[j]])
            return go

        ada_tables([(3, 0, "sh"), (4, 1, "sc"), (5, 2, "gt")], g_ffn, PB, nbuf=1,
                   hooks=[load_ffn(0, 2), load_ffn(2, 4), load_ffn(4, NJ)])
        S.barrier(False)

        with ExitStack() as st:
            X1C = sb(st, "x1c", [128, 4, D])
            X1Cb = [Buf("x1c%d" % i) for i in range(4)]
            TMPA = sb(st, "tmpa2", [128, D]); TMPB = sb(st, "tmpb2", [128, D])
            HBF = sb(st, "hbf2", [128, D], BF16)
            H2TS = [sb(st, "h2t%d" % i, [128, KC, 256], BF16) for i in range(2)]
            TT_ = [sb(st, "tt%d" % i, [128, 256]) for i in range(2)]
            GJ = [sb(st, "gj%d" % i, [128, 256], BF16) for i in range(3)]
            HIST = sb(st, "hist", [128, NJ, 2]); HC = sb(st, "hc", [128, NJ, 2]); HTMP = sb(st, "htmp", [128, NJ, 2])
            GS = sb(st, "gs", [128, NJ, NS], BF16)
            ABK = [PB[0], PB[1], PB[6]]
            FB = [[PB[2], PB[3]], [PB[4], PB[5]]]

            def prep_load(sti):
                for i in range(2):
                    tix = 2 * sti + i
                    slot = (sti % 2) * 2 + i
                    S.dma("sp", X1C[:, slot, :], x1s[tix * 128:(tix + 1) * 128, :], (), [X1Cb[slot]])

            def prep_a(sti, i):
                slot = (sti % 2) * 2 + i
                norm_mod_a(128, X1C[:, slot, :], X1Cb[slot], TAB[1], TAB[0], TMPA, HBF, i)

            def prep_b(sti, i):
                norm_mod_b(128, HBF, H2TS[sti % 2], H2TS[sti % 2].b, i * 128, 128)

            def prep(sti):
                prep_load(sti)
                for i in range(2):
                    prep_a(sti, i)
                    prep_b(sti, i)

            def ab(sti, j):
                bank = ABK[j % 3]
                h2t = H2TS[sti % 2]
                for half in range(2):
                    for kc in range(KC):
                        mm(bank[:, half * 256:(half + 1) * 256], WFI[:, j, kc, half, :], h2t[:, kc, :], kc == 0, kc == KC - 1,
                           [WFIb[j], h2t.b], [bank.b], kc == KC - 1 and half == 1)

            def cx(sti, j):
                bank = ABK[j % 3]; tb = TT_[j % 2]
                act(tb[:], bank[:, 0:256], AF.Identity, [bank.b, CW.b, CB.b], [tb.b], scale=CW[:, j, 2:3], bias=CB[:, j:j + 1])
                stt(tb[:, 1:256], bank[:, 0:255], CW[:, j, 1:2], tb[:, 1:256], ALU.mult, ALU.add, [bank.b, CW.b, tb.b], [tb.b])
                stt(tb[:, 2:256], bank[:, 0:254], CW[:, j, 0:1], tb[:, 2:256], ALU.mult, ALU.add, [bank.b, CW.b, tb.b], [tb.b])
                if sti > 0:
                    tt("dve", tb[:, 0:2], tb[:, 0:2], HC[:, j, :], ALU.add, [tb.b, HC.b], [tb.b])
                cpy("act", HIST[:, j, :], bank[:, 254:256], [bank.b], [HIST.b])

            def cy(j):
                bank = ABK[j % 3]; tb = TT_[j % 2]; gj = GJ[j % 3]
                act(tb[:], tb[:], AF.Silu, [tb.b], [tb.b])
                tt("dve", gj[:], tb[:], bank[:, 256:512], ALU.mult, [tb.b, bank.b], [gj.b])

            def ffn_out(j):
                gj = GJ[j % 3]
                for tix in range(2):
                    for n in range(2):
                        mm(FB[tix][n][:, :], gj[:, tix * 128:(tix + 1) * 128], WFO[:, j, n * 512:(n + 1) * 512], j == 0, j == NJ - 1,
                           [gj.b, WFOb[j]], [FB[tix][n].b], j == NJ - 1)

            def fin_evac(sti):
                tt("pool", HTMP[:, :, 0], HIST[:, :, 1], CW[:, :, 1], ALU.mult, [HIST.b, CW.b], [HTMP.b])
                tt("pool", HTMP[:, :, 1], HIST[:, :, 0], CW[:, :, 0], ALU.mult, [HIST.b, CW.b], [HTMP.b])
                tt("pool", HC[:, :, 0], HTMP[:, :, 0], HTMP[:, :, 1], ALU.add, [HTMP.b], [HC.b])
                tt("pool", HC[:, :, 1], HIST[:, :, 1], CW[:, :, 0], ALU.mult, [HIST.b, CW.b], [HC.b])
                for i in range(2):
                    stg = (TMPB, TMPA)[i]
                    for n in range(2):
                        tt("dve", stg[:, n * 512:(n + 1) * 512], FB[i][n][:, :], TAB[2][:, n * 512:(n + 1) * 512], ALU.mult,
                           [FB[i][n].b, TAB[2].b], [stg.b])

            def fin_rest(sti, i):
                tix = 2 * sti + i
                slot = (sti % 2) * 2 + i
                stg = (TMPB, TMPA)[i]
                tt("pool", stg[:], stg[:], X1C[:, slot, :], ALU.add, [stg.b, X1Cb[slot]], [stg.b])
                act(HBF[:], stg[:], AF.Square, [stg.b], [HBF.b, SS.b], accum=SS[:, 2 + i:3 + i])
                rsqrt(RS[:, 2 + i:3 + i], SS[:, 2 + i:3 + i], D * EPS, SS.b, RS.b, SQ[:, 2 + i:3 + i], SQ.b)
                stt(X1C[:, slot, :], stg[:], RS[:, 2 + i:3 + i], GF[:], ALU.mult, ALU.mult, [stg.b, RS.b, GF.b], [X1Cb[slot]])
                S.dma("sp", yp[tix * 128:(tix + 1) * 128, :], X1C[:, slot, :], [X1Cb[slot]], ())

            NST = NT // 2
            prep(0)
            for sti in range(NST):
                nxt = sti + 1 < NST
                for j in range(NJ):
                    ab(sti, j)
                    if j >= 2:
                        ffn_out(j - 2)
                    cx(sti, j)
                    if j >= 1:
                        cy(j - 1)
                    if sti > 0 and j == 1:
                        fin_rest(sti - 1, 0)
                    if sti > 0 and j == 3:
                        fin_rest(sti - 1, 1)
                    if nxt:
                        if j == 5:
                            prep_load(sti + 1)
                        if j == 7:
                            prep_a(sti + 1, 0)
                        if j == 10:
                            prep_b(sti + 1, 0)
                        if j == 12:
                            prep_a(sti + 1, 1)
                        if j == 15:
                            prep_b(sti + 1, 1)
                cy(NJ - 1)
                ffn_out(NJ - 2)
                ffn_out(NJ - 1)
                fin_evac(sti)
            fin_rest(NST - 1, 0)
            fin_rest(NST - 1, 1)

            S.barrier(False)
            H2T = H2TS[0]
            flat = X1C[:, 2:4, :].rearrange("p a d -> p (a d)")
            SCT = T(flat[:, 0:704].rearrange("p (j m) -> p j m", j=NJ), "sct")
            AH = T(flat[:, 704:1100].rearrange("p (j m) -> p j m", j=NJ), "ah")
            TS_ = T(flat[:, 1100:1452].rearrange("p (j m) -> p j m", j=NJ), "tsamp")
            TS2 = T(flat[:, 1452:1804].rearrange("p (j m) -> p j m", j=NJ), "tsamp2")
            M = NS
            norm_mod_T(M, X1S[:], X1S.b, TABS[1], TABS[0], TMPA, HBF, H2T, H2T.b, 0, 128, 0)
            AALL = PB[0]; BALL = PB[1]
            for j in range(NJ):
                for half, bank in ((0, AALL), (1, BALL)):
                    for kc in range(KC):
                        mm(bank[:, j * M:(j + 1) * M], WFI[:, j, kc, half, :], H2T[:, kc, 0:M], kc == 0, kc == KC - 1,
                           [WFIb[j], H2T.b], [bank.b], kc == KC - 1 and j == NJ - 1)
            SCA = PB[2]; SCB = PB[3]
            for q in range(3):
                c0 = q * 1024
                w = min(1024, DFF - c0)
                S.dma("sp", X1C[0:2 * NS, q % 2, 0:w], sconv[:, c0:c0 + w], (), [X1Cb[q % 2]])
                for jl in range(w // 128):
                    j = q * 8 + jl
                    bank = SCA if j < 11 else SCB
                    jj = j if j < 11 else j - 11
                    tr(bank[:, jj * 32:(jj + 1) * 32], X1C[0:2 * NS, q % 2, jl * 128:(jl + 1) * 128], IDF[0:32, 0:32],
                       [X1Cb[q % 2], IDF.b], [bank.b], True)
            cpy("act", SCT[:, 0:11, :], SCA[:, 0:352].rearrange("p (j m) -> p j m", j=11), [SCA.b], [SCT.b])
            cpy("act", SCT[:, 11:22, :], SCB[:, 0:352].rearrange("p (j m) -> p j m", j=11), [SCB.b], [SCT.b])
            av = AALL[:, 0:NJ * M].rearrange("p (j m) -> p j m", j=NJ)
            bv = BALL[:, 0:NJ * M].rearrange("p (j m) -> p j m", j=NJ)
            sctv = SCT[:].rearrange("p j (b r) -> p j b r", r=2)

            def bc(ap2):
                return ap2.unsqueeze(2).to_broadcast([128, NJ, M])

            cpy("act", AH[:, :, 2:2 + M], av, [AALL.b], [AH.b])
            cpy("pool", AH[:, :, 0:2], HIST[:], [HIST.b], [AH.b])
            tt("dve", TS_[:], av, bc(CW[:, :, 2]), ALU.mult, [AALL.b, CW.b], [TS_.b])
            tt("pool", TS2[:], sctv[:, :, :, 1], bc(CW[:, :, 1]), ALU.mult, [SCT.b, CW.b], [TS2.b])
            tt("dve", TS_[:], TS_[:], TS2[:], ALU.add, [TS_.b, TS2.b], [TS_.b])
            tt("pool", TS2[:], sctv[:, :, :, 0], bc(CW[:, :, 0]), ALU.mult, [SCT.b, CW.b], [TS2.b])
            tt("dve", TS_[:], TS_[:], TS2[:], ALU.add, [TS_.b, TS2.b], [TS_.b])
            tt("dve", TS_[:], TS_[:], bc(CB[:, :]), ALU.add, [TS_.b, CB.b], [TS_.b])
            act(TS2[:], TS_[:], AF.Silu, [TS_.b], [TS2.b])
            tt("dve", GS[:], TS2[:], bv, ALU.mult, [TS2.b, BALL.b], [GS.b])
            FS = [PB[4], PB[5]]
            for n in range(2):
                for j in range(NJ):
                    mm(FS[n][0:M, :], GS[:, j, :], WFO[:, j, n * 512:(n + 1) * 512], j == 0, j == NJ - 1,
                       [GS.b, WFOb[j]], [FS[n].b], j == NJ - 1)
                tt("dve", TMPB[0:M, n * 512:(n + 1) * 512], FS[n][0:M, :], TABS[2][0:M, n * 512:(n + 1) * 512], ALU.mult,
                   [FS[n].b, TABS[2].b], [TMPB.b])
            tt("pool", TMPB[0:M, :], TMPB[0:M, :], X1S[:], ALU.add, [TMPB.b, X1S.b], [TMPB.b])
            act(TMPA[0:M, :], TMPB[0:M, :], AF.Square, [TMPB.b], [TMPA.b, SS.b], accum=SS[0:M, 0:1])
            rsqrt(RS[0:M, 0:1], SS[0:M, 0:1], D * EPS, SS.b, RS.b, SQ[0:M, 0:1], SQ.b)
            stt(X1C[0:M, 0, :], TMPB[0:M, :], RS[0:M, 0:1], GF[0:M, :], ALU.mult, ALU.mult, [TMPB.b, RS.b, GF.b], [X1Cb[0]])
            S.dma("sp", ys[:, :], X1C[0:M, 0, :], [X1Cb[0]], ())
            CT = [PB[6], PB[2], PB[3], PB[0], PB[1], PB[4]]
            ncv = nconv_s.rearrange("(b r) c -> b r c", r=2)
            NCVb = [TMPA.b, TMPA.b]
            for g in range(6):
                bank = CT[g]
                nj = min(4, NJ - 4 * g)
                for jj in range(nj):
                    j = 4 * g + jj
                    tr(bank[0:2 + M, jj * 128:(jj + 1) * 128], AH[:, j, :], IDF[:, :], [AH.b, IDF.b], [bank.b], jj == nj - 1)
                w = nj * 128
                piece = TMPA[0:2 + M, (g % 2) * 512:(g % 2) * 512 + w]
                cpy("act", piece, bank[0:2 + M, 0:w], [bank.b], [NCVb[g % 2]])
                S.dma("sp", nconv_p[:, g * 512:g * 512 + w], TMPA[0:2, (g % 2) * 512:(g % 2) * 512 + w], [NCVb[g % 2]], ())
                S.dma("sp", ncv[:, 1, g * 512:g * 512 + w], TMPA[2:2 + M, (g % 2) * 512:(g % 2) * 512 + w], [NCVb[g % 2]], ())
            S.dma("sp", ncv[:, 0, :], sconv.rearrange("(b r) c -> b r c", r=2)[:, 1, :], (), ())
        st2.__exit__(None, None, None)

        S.finish()
        with nc.Block() as block:
            @block.sync
            def _(e):
                S.replay(e, "sp")

            @block.tensor
            def _(e):
                S.replay(e, "pe")

            @block.scalar
            def _(e):
                S.replay(e, "act")

            @block.vector
            def _(e):
                S.replay(e, "dve")

            @block.gpsimd
            def _(e):
                S.replay(e, "pool")
    return nc


_CACHE = {}


def kernel(x_prompt, x_sample, c_prompt, c_sample, state_pool, state_ret, state_conv,
           g_mix, g_ffn, w_ada, b_ada, w_in, w_pool, ls_pool, w_out,
           w_ffn_in, conv_w, conv_b, w_ffn_out, g_final):
    f = lambda a: np.ascontiguousarray(np.asarray(a, dtype=np.float32))
    consts, _, rope = _host_consts()
    ident = np.eye(128, dtype=np.float32)
    x_prompt = f(x_prompt); x_sample = f(x_sample); c_prompt = f(c_prompt); c_sample = f(c_sample)
    state_pool = f(state_pool); state_ret = f(state_ret); state_conv = f(state_conv)
    shared = {
        "g_mix": f(g_mix).reshape(1, D), "g_ffn": f(g_ffn).reshape(1, D), "g_fin": f(g_final).reshape(1, D),
        "w_ada": f(w_ada)[0], "b_ada": f(b_ada).reshape(1, 6 * D),
        "w_in": f(w_in)[0], "w_pool": f(w_pool).reshape(PW, 64),
        "ls_pool": f(f(ls_pool).reshape(2, 128).T),
        "w_out": f(w_out)[0],
        "w_fi": f(f(w_ffn_in)[0].reshape(KC, 128, 2, NJ, 128).transpose(3, 1, 0, 2, 4).reshape(NJ, 128, KC * 256)),
        "w_fo": f(w_ffn_out)[0],
        "cwT": f(f(conv_w)[0].reshape(3, NJ, 128).transpose(2, 1, 0).reshape(128, NJ * 3)),
        "cbT": f(f(conv_b)[0].reshape(NJ, 128).T),
        "consts": consts, "ident": ident, "rope": rope,
    }
    in_maps = []
    for i in range(NCORES):
        m = dict(shared)
        sl = slice(i * NS, (i + 1) * NS)
        m["xp"] = x_prompt[i]
        m["xs"] = f(x_sample[sl, 0, :])
        m["cp"] = f(c_prompt[i:i + 1])
        m["cs"] = f(c_sample[sl])
        m["spool"] = f(state_pool[0, sl].reshape(NS * 15, PW))
        m["sret"] = f(state_ret[0, sl])
        m["sconv"] = f(state_conv[0, sl].reshape(NS * 2, DFF))
        in_maps.append(m)
    if "nc" not in _CACHE:
        _CACHE["nc"] = build_nc()
    res = run_bass_kernel_spmd(_CACHE["nc"], in_maps, core_ids=list(range(NCORES)))
    R = res.results
    cat = lambda k: np.stack([np.asarray(r[k], dtype=np.float32) for r in R], axis=0)
    y_p = cat("yp")
    y_s = cat("ys").reshape(NCORES * NS, 1, D)
    np_p = cat("npool_p").reshape(1, NCORES, 15, PW)
    nr_p = cat("nret_p").reshape(1, NCORES, H, HD, HD)
    nc_p = cat("nconv_p").reshape(1, NCORES, 2, DFF)
    np_s = cat("npool_s").reshape(1, NCORES * NS, 15, PW)
    nr_s = cat("nret_s").reshape(1, NCORES * NS, H, HD, HD)
    nc_s = cat("nconv_s").reshape(1, NCORES * NS, 2, DFF)
    return (y_p, y_s, np_p, nr_p, nc_p, np_s, nr_s, nc_s)
```

```python
import math
from contextlib import ExitStack

import numpy as np
import concourse.bass as bass
import concourse.mybir as mybir
from concourse.bass_utils import run_bass_kernel_spmd

F32 = mybir.dt.float32
BF16 = mybir.dt.bfloat16
AF = mybir.ActivationFunctionType
ALU = mybir.AluOpType
AX = mybir.AxisListType

NCORES = 8
D = 1024
L = 2048
NT = 16
NS = 16
H = 6
HD = 128
RW = 768
PW = 256
INC = 3328
DFF = 2816
NJ = 22
PAST = 16384
EPS = 1e-6
KC = 8


class Buf:
    __slots__ = ("name", "w", "r")

    def __init__(self, name):
        self.name = name
        self.w = None
        self.r = {}


class Sched:
    ENG = ("pe", "act", "dve", "pool", "sp")

    def __init__(self, nc, es, ring_sizes):
        self.nc = nc
        self.q = {n: [] for n in self.ENG}
        self.sem = {}
        self.cnt = {}
        for n in ("pe", "act", "dve", "pool"):
            self.sem[n] = es.enter_context(nc.semaphore("s_" + n))
            self.cnt[n] = 0
        self.know = {n: {} for n in self.ENG}
        self.snap = {}
        self.tokseq = {}
        self.seq = 0
        self.ring = {}
        self.rpos = {}
        for qn, sz in ring_sizes.items():
            lst = []
            for i in range(sz):
                key = "d_%s_%d" % (qn, i)
                self.sem[key] = es.enter_context(nc.semaphore(key))
                lst.append([key, 0])
            self.ring[qn] = lst
            self.rpos[qn] = 0
        self.nwait = 0

    def _merge(self, eng, key, val):
        kn = self.know[eng]
        if kn.get(key, 0) < val:
            kn[key] = val
        sn = self.snap.get((key, val))
        if sn is not None:
            for k2, v2 in sn.items():
                if kn.get(k2, 0) < v2:
                    kn[k2] = v2

    def _waits(self, eng, deps, is_dma=False):
        need = {}
        for tok, kind in deps:
            key, val, src = tok
            if src == eng and not is_dma:
                if eng == "pe":
                    continue
            if src is not None and src in self.cnt and val > self.cnt[src]:
                if src == eng:
                    continue
                raise RuntimeError("dependency on unsignalled op of %s" % src)
            if need.get(key, 0) < val:
                need[key] = val
        order = sorted(need.items(), key=lambda kv: -self.tokseq.get(kv, 0))
        for key, val in order:
            if self.know[eng].get(key, 0) >= val:
                continue
            self.q[eng].append(("wait", key, val))
            self.nwait += 1
            self._merge(eng, key, val)

    def _deps(self, reads, writes):
        deps = []
        for b in reads:
            if b.w is not None:
                deps.append((b.w, "raw"))
        for b in writes:
            if b.w is not None:
                deps.append((b.w, "waw"))
            for t in b.r.values():
                deps.append((t, "war"))
        return deps

    def op(self, eng, fn, reads=(), writes=(), signal=True, embed=None):
        if embed is None:
            embed = eng in ("dve", "pool")
        self._waits(eng, self._deps(reads, writes))
        if signal:
            self.cnt[eng] += 1
            tok = (eng, self.cnt[eng], eng)
            sn = dict(self.know[eng])
            sn[eng] = self.cnt[eng]
            self.snap[(eng, self.cnt[eng])] = sn
            self.seq += 1
            self.tokseq[(eng, self.cnt[eng])] = self.seq
        else:
            tok = (eng, self.cnt[eng] + 1, eng)
        self.q[eng].append(("op", fn, signal, embed))
        for b in reads:
            b.r[eng] = tok
        for b in writes:
            b.w = tok
            b.r = {}
        return tok

    def dma(self, qn, out, in_, reads=(), writes=(), **kw):
        ring = self.ring[qn]
        i = self.rpos[qn]
        self.rpos[qn] = (i + 1) % len(ring)
        key, val = ring[i]
        deps = self._deps(reads, writes)
        if val > 0:
            deps.append(((key, val, None), "raw"))
        self._waits(qn, deps, is_dma=True)
        ring[i][1] = val + 16
        tok = (key, val + 16, None)
        sn = dict(self.know[qn])
        sn[key] = val + 16
        self.snap[(key, val + 16)] = sn
        self.seq += 1
        self.tokseq[(key, val + 16)] = self.seq
        self.q[qn].append(("dma", out, in_, key, kw))
        for b in reads:
            b.r[key] = tok
        for b in writes:
            b.w = tok
            b.r = {}
        return tok

    def barrier(self, include_pool_ring=True):
        toks = []
        for n in ("pe", "act", "dve", "pool"):
            if self.cnt[n] > 0:
                toks.append((n, self.cnt[n], n))
        for qn, ring in self.ring.items():
            if qn == "pool" and not include_pool_ring:
                continue
            for key, val in ring:
                if val > 0:
                    toks.append((key, val, None))
        for eng in self.ENG:
            for (key, val, src) in toks:
                if src == eng and eng != "pool":
                    continue
                if self.know[eng].get(key, 0) >= val:
                    continue
                self.q[eng].append(("wait", key, val))
                self._merge(eng, key, val)

    def finish(self):
        for qn, ring in self.ring.items():
            for key, val in ring:
                if val > 0 and self.know["sp"].get(key, 0) < val:
                    self.q["sp"].append(("wait", key, val))
                    self._merge("sp", key, val)
        for n in ("pe", "act", "dve", "pool"):
            if self.cnt[n] > 0:
                self.q["sp"].append(("wait", n, self.cnt[n]))

    def replay(self, e, qn):
        q = self.q[qn]
        pend = None
        for i, it in enumerate(q):
            if it[0] == "wait":
                if pend is not None:
                    e.wait_ge(self.sem[pend[1]], pend[2])
                    pend = None
                nxt = q[i + 1] if i + 1 < len(q) else None
                if nxt is not None and nxt[0] == "op" and nxt[3]:
                    pend = it
                else:
                    e.wait_ge(self.sem[it[1]], it[2])
            elif it[0] == "op":
                ins = it[1](e)
                if pend is not None:
                    ins._wait_ge(self.sem[pend[1]], pend[2])
                    pend = None
                if it[2]:
                    ins.then_inc(self.sem[qn], 1)
            else:
                _, out, in_, key, kw = it
                e.dma_start(out=out, in_=in_, **kw).then_inc(self.sem[key], 16)


class T:
    def __init__(self, t, name):
        self.t = t
        self.b = Buf(name)

    def __getitem__(self, k):
        return self.t[k]


def _log_gammas():
    return np.log(1.0 - np.exp2(-5.0 - np.arange(H, dtype=np.float64)))


def _host_consts():
    lg = _log_gammas()
    c = {}
    i = np.arange(128, dtype=np.float64)
    qdec = np.exp(lg[:, None] * (i[None, :] + 1.0)).reshape(1, RW)
    c["qdec"] = np.broadcast_to(qdec, (128, RW))
    j = i
    m = np.exp(-lg[None, :, None] * (j[:, None, None] + 1.0)) * (HD ** -0.5)
    m = m * (i[None, None, :] >= j[:, None, None])
    c["maskT"] = m.reshape(128, RW)
    c["kdec"] = np.exp(lg[None, :] * (127.0 - j[:, None])) * (HD ** -0.5)
    c["gam"] = np.broadcast_to(np.repeat(np.exp(lg), HD)[None, :], (128, RW))
    win = np.array([2, 4, 8, 16], dtype=np.float64)
    p = np.arange(128)
    invw = np.zeros((128, 2))
    corr = np.zeros((128, 2, 16))
    for cc in range(2):
        w = win[2 * cc + p // 64]
        invw[:, cc] = 1.0 / w
        tt = np.arange(16, dtype=np.float64)
        corr[:, cc, :] = w[:, None] / np.minimum(w[:, None], tt[None, :] + 1.0)
    c["invw"] = invw
    c["corr0"] = corr.reshape(128, 32)
    sel = np.zeros((128, 4, 3, 16))
    for g in range(4):
        w = int(win[g])
        for half in range(2):
            for bb in range(8):
                for r in range(15):
                    if r >= 16 - w:
                        sel[bb * 15 + r, g, half, half * 8 + bb] = 1.0 / w
        for bb in range(16):
            sel[bb, g, 2, bb] = 1.0 / w - 1.0
    c["sel"] = sel.reshape(128, 192)
    names = ["qdec", "maskT", "kdec", "gam", "invw", "corr0", "sel"]
    offs = {}
    o = 0
    cols = []
    for n in names:
        a = np.asarray(c[n], dtype=np.float64)
        offs[n] = (o, a.shape[1])
        o += a.shape[1]
        cols.append(a)
    arr = np.concatenate(cols, axis=1).astype(np.float32)
    half = HD // 2
    inv = (np.float32(10000.0) ** (-(np.arange(half, dtype=np.float32) / np.float32(half)))).astype(np.float32)
    pos = np.concatenate([np.arange(L), np.full(NS, PAST)]).astype(np.float32)
    ang = (pos[:, None] * inv[None, :]).astype(np.float32).astype(np.float64)
    rope = np.concatenate([np.cos(ang), np.sin(ang), -np.sin(ang)], axis=1).astype(np.float32)
    return arr, offs, rope


def build_nc(debug=False):
    CONSTS, COFF, _ = _host_consts()
    NCONST = CONSTS.shape[1]
    lg = _log_gammas()
    gamC = [float(np.exp(lg[h] * 128.0)) for h in range(H)]
    gam1 = [float(np.exp(lg[h])) for h in range(H)]

    nc = bass.Bass("TRN2", target_bir_lowering=False)

    def din(name, shape):
        return nc.dram_tensor(name, list(shape), F32, kind="ExternalInput").ap()

    def dout(name, shape):
        return nc.dram_tensor(name, list(shape), F32, kind="ExternalOutput").ap()

    xp = din("xp", [L, D]); xs = din("xs", [NS, D]); cp = din("cp", [1, D]); cs = din("cs", [NS, D])
    spool = din("spool", [NS * 15, PW]); sret = din("sret", [NS, H, HD, HD]); sconv = din("sconv", [NS * 2, DFF])
    g_mix = din("g_mix", [1, D]); g_ffn = din("g_ffn", [1, D]); g_fin = din("g_fin", [1, D])
    w_ada = din("w_ada", [D, 6 * D]); b_ada = din("b_ada", [1, 6 * D])
    w_in = din("w_in", [D, INC]); w_pool = din("w_pool", [PW, 64]); ls_pool = din("ls_pool", [128, 2])
    w_out = din("w_out", [D, D]); w_fi = din("w_fi", [NJ, 128, KC * 256]); w_fo = din("w_fo", [DFF, D])
    cwT = din("cwT", [128, NJ * 3]); cbT = din("cbT", [128, NJ])
    consts = din("consts", [128, NCONST]); ident = din("ident", [128, 128]); rope = din("rope", [L + NS, 192])

    yp = dout("yp", [L, D]); ys = dout("ys", [NS, D])
    npool_p = dout("npool_p", [15, PW]); nret_p = dout("nret_p", [H * HD, HD]); nconv_p = dout("nconv_p", [2, DFF])
    npool_s = dout("npool_s", [NS * 15, PW]); nret_s = dout("nret_s", [NS, H, HD, HD]); nconv_s = dout("nconv_s", [NS * 2, DFF])
    x1s = nc.dram_tensor("x1s", [L, D], F32, kind="Internal").ap()
    if debug:
        dbg_mix = dout("dbg_mix", [128, KC * NS]); dbg_x1 = dout("dbg_x1", [NS, D]); dbg_oi = dout("dbg_oi", [NS, RW])
        dbg_ret = dout("dbg_ret", [NS, RW])

    es = ExitStack()
    with es:
        S = Sched(nc, es, {"sp": 24, "pool": 12})

        uniq = {"n": 0}

        def sb(stack, name, shape, dt=F32):
            uniq["n"] += 1
            return T(stack.enter_context(nc.sbuf_tensor("sb%d_%s" % (uniq["n"], name), list(shape), dt)), name)

        def ps(stack, name, shape, dt=F32):
            return T(stack.enter_context(nc.psum_tensor("ps_" + name, list(shape), dt)), name)

        def mm(out, lhsT, rhs, start, stop, reads, writes, signal):
            S.op("pe", lambda e: e.matmul(out, lhsT=lhsT, rhs=rhs, start=start, stop=stop),
                 reads, writes, signal)

        def tr(out, in_, idn, reads, writes, signal):
            S.op("pe", lambda e: e.transpose(out, in_, idn), reads, writes, signal)

        def act(out, in_, func, reads, writes, scale=None, bias=None, accum=None, signal=True):
            kw = {}
            if scale is not None:
                kw["scale"] = scale
            if bias is not None:
                kw["bias"] = bias
            if accum is not None:
                kw["accum_out"] = accum
            S.op("act", lambda e: e.activation(out=out, in_=in_, func=func, **kw), reads, writes, signal,
                 embed=(accum is None))

        def tt(eng, out, in0, in1, op, reads, writes):
            S.op(eng, lambda e: e.tensor_tensor(out=out, in0=in0, in1=in1, op=op), reads, writes)

        def ts(eng, out, in0, s1, s2, op0, op1, reads, writes):
            if op1 is None:
                S.op(eng, lambda e: e.tensor_scalar(out=out, in0=in0, scalar1=s1, scalar2=None, op0=op0), reads, writes)
            else:
                S.op(eng, lambda e: e.tensor_scalar(out=out, in0=in0, scalar1=s1, scalar2=s2, op0=op0, op1=op1), reads, writes)

        def stt(out, in0, scalar, in1, op0, op1, reads, writes, signal=True):
            S.op("dve", lambda e: e.scalar_tensor_tensor(out=out, in0=in0, scalar=scalar, in1=in1, op0=op0, op1=op1),
                 reads, writes, signal)

        def cpy(eng, out, in_, reads, writes):
            if eng == "act":
                S.op("act", lambda e: e.copy(out=out, in_=in_), reads, writes, embed=True)
            else:
                S.op(eng, lambda e: e.tensor_copy(out=out, in_=in_), reads, writes)

        def red(out, in_, reads, writes):
            S.op("dve", lambda e: e.tensor_reduce(out=out, in_=in_, axis=AX.X, op=ALU.add), reads, writes)

        def rsqrt(out, in_, bias, in_b, out_b, tmp, tmp_b):
            n = tmp.shape[-1]
            ts("pool", tmp, in_, float(bias), None, ALU.add, None, [in_b], [tmp_b])
            tt("pool", out, tmp, NEGH[0:tmp.shape[0], 0:n], ALU.pow, [tmp_b, NEGH.b], [out_b])

        def mset(eng, ap, val, writes):
            S.op(eng, lambda e: e.memset(ap, val), (), writes)

        TAB = [sb(es, "tab%d" % i, [128, D]) for i in range(3)]
        TABS = [sb(es, "tabs%d" % i, [NS, D]) for i in range(3)]
        GF = sb(es, "gf", [128, D])
        IDF = sb(es, "idf", [128, 128]); IDB = sb(es, "idb", [128, 128], BF16)
        CTP = sb(es, "ctp", [128, KC, 128], BF16); CTS = sb(es, "cts", [128, KC, NS], BF16)
        X1S = sb(es, "x1samp", [NS, D])
        LS = sb(es, "ls", [128, 2]); CW = sb(es, "cw", [128, NJ, 3]); CB = sb(es, "cb", [128, NJ])
        SS = sb(es, "ss", [128, 8]); RS = sb(es, "rs", [128, 8]); SQ = sb(es, "sq", [128, 8])
        TRB = ps(es, "trb", [128, 1024], BF16)
        PB = [ps(es, "pb%d" % i, [128, 512]) for i in range(7)]

        NEGH = sb(es, "negh", [128, 8])
        mset("pool", NEGH[:], -0.5, [NEGH.b])
        S.dma("sp", IDF[:], ident[:, :], (), [IDF.b])
        cpy("dve", IDB[:], IDF[:], [IDF.b], [IDB.b])
        S.dma("sp", LS[:], ls_pool[:, :], (), [LS.b])
        S.dma("sp", CW[:].rearrange("p j i -> p (j i)"), cwT[:, :], (), [CW.b])
        S.dma("sp", CB[:], cbT[:, :], (), [CB.b])

        def norm_mod_a(M, src, src_b, tabG, tabSH, tmpA, hbf, sidx):
            act(tmpA[0:M, :], src, AF.Square, [src_b], [tmpA.b, SS.b], accum=SS[0:M, sidx:sidx + 1])
            rsqrt(RS[0:M, sidx:sidx + 1], SS[0:M, sidx:sidx + 1], D * EPS, SS.b, RS.b, SQ[0:M, sidx:sidx + 1], SQ.b)
            stt(tmpA[0:M, :], src, RS[0:M, sidx:sidx + 1], tabG[0:M, :], ALU.mult, ALU.mult,
                [src_b, RS.b, tabG.b], [tmpA.b])
            tt("pool", hbf[0:M, :], tmpA[0:M, :], tabSH[0:M, :], ALU.add, [tmpA.b, tabSH.b], [hbf.b])

        def norm_mod_b(M, hbf, dstT, dst_b, col0, ncols, trb=None):
            trb = TRB if trb is None else trb
            for kc in range(KC):
                tr(trb[:, kc * ncols: kc * ncols + M], hbf[0:M, kc * 128:(kc + 1) * 128], IDB[0:M, 0:M],
                   [hbf.b, IDB.b], [trb.b], kc == KC - 1)
            cpy("act", dstT[:, :, col0:col0 + M],
                trb[:, 0:KC * ncols].rearrange("p (k m) -> p k m", k=KC)[:, :, 0:M], [trb.b], [dst_b])

        def norm_mod_T(M, src, src_b, tabG, tabSH, tmpA, hbf, dstT, dst_b, col0, ncols, sidx):
            norm_mod_a(M, src, src_b, tabG, tabSH, tmpA, hbf, sidx)
            norm_mod_b(M, hbf, dstT, dst_b, col0, ncols)

        def ada_tables(groups, gvec, pst, nbuf=2, hooks=()):
            with ExitStack() as st:
                STG = [sb(st, "stg%d" % i, [128, KC, D], BF16) for i in range(nbuf)]
                BB = [sb(st, "bb%d" % i, [128, D]) for i in range(nbuf)]
                GV = sb(st, "gv", [128, D])
                S.dma("sp", GV[:], gvec[0, :].partition_broadcast(128), (), [GV.b])
                wv = w_ada.rearrange("(k p) c -> p k c", p=128)
                hooks = list(hooks)

                def issue(gi):
                    m = groups[gi][0]
                    stg = STG[gi % nbuf]; bb = BB[gi % nbuf]
                    S.dma("pool", stg[:], wv[:, :, m * D:(m + 1) * D], (), [stg.b])
                    S.dma("sp", bb[:], b_ada[0, m * D:(m + 1) * D].partition_broadcast(128), (), [bb.b])
                    if gi < len(hooks) and hooks[gi] is not None:
                        hooks[gi]()

                for gi in range(min(nbuf, len(groups))):
                    issue(gi)
                for gi, (m, ti, kind) in enumerate(groups):
                    stg = STG[gi % nbuf]; bb = BB[gi % nbuf]
                    for (M, ct, tabs) in ((128, CTP, TAB), (NS, CTS, TABS)):
                        for n in range(2):
                            bank = pst[(2 * gi + n) % len(pst)]
                            for kc in range(KC):
                                mm(bank[0:M, :], ct[:, kc, 0:M], stg[:, kc, n * 512:(n + 1) * 512], kc == 0, kc == KC - 1,
                                   [ct.b, stg.b], [bank.b], kc == KC - 1)
                            tt("dve", tabs[ti][0:M, n * 512:(n + 1) * 512], bank[0:M, :], bb[0:M, n * 512:(n + 1) * 512],
                               ALU.add, [bank.b, bb.b], [tabs[ti].b])
                        if kind == "sc":
                            stt(tabs[ti][0:M, :], tabs[ti][0:M, :], 1.0, GV[0:M, :], ALU.add, ALU.mult,
                                [tabs[ti].b, GV.b], [tabs[ti].b])
                            ts("dve", tabs[ti][0:M, :], tabs[ti][0:M, :], float(math.sqrt(D)), None, ALU.mult, None,
                               [tabs[ti].b], [tabs[ti].b])
                    if gi + nbuf < len(groups):
                        issue(gi + nbuf)

        st1 = ExitStack()
        st1.__enter__()
        WIN = sb(st1, "w_in", [128, KC, INC], BF16)
        WOUT = sb(st1, "w_out", [128, KC, D], BF16)
        WINb = [Buf("w_in%d" % k) for k in range(KC)]
        WOUTb = [Buf("w_out%d" % k) for k in range(KC)]
        CONST = sb(st1, "consts", [128, NCONST])
        WPB = sb(st1, "wpb", [128, 2, 128], BF16)
        S.dma("sp", CONST[:], consts[:, :], (), [CONST.b])

        with ExitStack() as st:
            CP = sb(st, "cpt", [128, D]); CS = sb(st, "cst", [NS, D]); CB16 = sb(st, "cb16", [128, D], BF16)
            S.dma("sp", CP[:], cp[0, :].partition_broadcast(128), (), [CP.b])
            S.dma("sp", CS[:], cs[:, :], (), [CS.b])
            for (M, src, dst) in ((128, CP, CTP), (NS, CS, CTS)):
                act(CB16[0:M, :], src[0:M, :], AF.Silu, [src.b], [CB16.b])
                for kc in range(KC):
                    tr(TRB[:, kc * 128: kc * 128 + M], CB16[0:M, kc * 128:(kc + 1) * 128], IDB[0:M, 0:M],
                       [CB16.b, IDB.b], [TRB.b], kc == KC - 1)
                cpy("act", dst[:, :, 0:M], TRB[:, :].rearrange("p (k m) -> p k m", k=KC)[:, :, 0:M], [TRB.b], [dst.b])
            S.dma("sp", GF[:], g_fin[0, :].partition_broadcast(128), (), [GF.b])
            ts("pool", GF[:], GF[:], float(math.sqrt(D)), None, ALU.mult, None, [GF.b], [GF.b])

        S.barrier(False)
        def load_mixer_weights():
            wiv = w_in.rearrange("(k p) (a c) -> p k a c", p=128, a=2)
            for kc in range(KC):
                S.dma("pool", WIN[:, kc, :].rearrange("p (a c) -> p a c", a=2), wiv[:, kc, :, :], (), [WINb[kc]])
            wov = w_out.rearrange("(k p) c -> p k c", p=128)
            for kc in range(0, KC, 2):
                S.dma("pool", WOUT[:, kc:kc + 2, :], wov[:, kc:kc + 2, :], (), [WOUTb[kc], WOUTb[kc + 1]])
            mset("pool", WPB[:], 0.0, [WPB.b])
            for g in range(4):
                pp = (g % 2) * 64
                S.dma("pool", WPB[pp:pp + 64, g // 2, pp:pp + 64], w_pool[g * 64:(g + 1) * 64, :], (), [WPB.b])

        def cst(name, rows=128):
            o, n = COFF[name]
            return CONST[0:rows, o:o + n]

        ada_tables([(0, 0, "sh"), (1, 1, "sc"), (2, 2, "gt")], g_mix, PB, nbuf=3, hooks=[None, None, load_mixer_weights])
        S.barrier(False)

        ring = {"i": 0}
        RING = PB[1:7]
        UY = PB[0]
        reserved = []

        def nb():
            while True:
                b = RING[ring["i"] % len(RING)]
                ring["i"] += 1
                if b not in reserved:
                    return b

        def rope_block(M, bank, blk, rt, dst, dst_b):
            pv = bank[0:M, :].rearrange("p (h s d) -> p h s d", h=4, s=2)
            cosb = rt[0:M, 0:64].unsqueeze(1).unsqueeze(1).to_broadcast([M, 4, 2, 64])
            sinb = rt[0:M, 64:128].unsqueeze(1).to_broadcast([M, 4, 64])
            nsinb = rt[0:M, 128:192].unsqueeze(1).to_broadcast([M, 4, 64])
            rav = RA[0:M, :].rearrange("p (h s d) -> p h s d", h=4, s=2)
            rbv = RB[0:M, :].rearrange("p (h s d) -> p h s d", h=4, s=2)
            tt("dve", rav, pv, cosb, ALU.mult, [bank.b, rt.b], [RA.b])
            tt("dve", rbv[:, :, 0, :], pv[:, :, 1, :], nsinb, ALU.mult, [bank.b, rt.b], [RB.b])
            tt("dve", rbv[:, :, 1, :], pv[:, :, 0, :], sinb, ALU.mult, [bank.b, rt.b], [RB.b])
            tt("pool", dst[0:M, blk * 512:(blk + 1) * 512], RA[0:M, :], RB[0:M, :], ALU.add, [RA.b, RB.b], [dst_b])

        def zblock(M, ht, blk):
            bank = nb()
            c0 = PW + blk * 512
            for kc in range(KC):
                mm(bank[0:M, :], ht[:, kc, 0:M], WIN[:, kc, c0:c0 + 512], kc == 0, kc == KC - 1,
                   [ht.b, WINb[kc]], [bank.b], kc == KC - 1)
            return bank

        def groupnorm_gate(M, OA, OB, sg, ret):
            act(TMPB[0:M, 0:512], OA[0:M, :], AF.Square, [OA.b], [TMPB.b])
            act(TMPB[0:M, 512:768], OB[0:M, 0:256], AF.Square, [OB.b], [TMPB.b])
            red(ST[0:M, 0:4], OA[0:M, :].rearrange("p (h d) -> p h d", h=4), [OA.b], [ST.b])
            red(ST[0:M, 4:6], OB[0:M, 0:256].rearrange("p (h d) -> p h d", h=2), [OB.b], [ST.b])
            red(ST[0:M, 6:12], TMPB[0:M, 0:768].rearrange("p (h d) -> p h d", h=6), [TMPB.b], [ST.b])
            ts("dve", ST[0:M, 12:18], ST[0:M, 0:6], 1.0 / HD, None, ALU.mult, None, [ST.b], [ST.b])
            tt("dve", ST[0:M, 18:24], ST[0:M, 12:18], ST[0:M, 12:18], ALU.mult, [ST.b], [ST.b])
            stt(ST[0:M, 24:30], ST[0:M, 6:12], 1.0 / HD, ST[0:M, 18:24], ALU.mult, ALU.subtract, [ST.b], [ST.b])
            rsqrt(RSTD[0:M, 0:6], ST[0:M, 24:30], EPS, ST.b, RSTD.b, SQ[0:M, 2:8], SQ.b)
            for h in range(H):
                src = OA[0:M, h * 128:(h + 1) * 128] if h < 4 else OB[0:M, (h - 4) * 128:(h - 3) * 128]
                srcb = OA.b if h < 4 else OB.b
                stt(ON[0:M, h * 128:(h + 1) * 128], src, ST[0:M, 12 + h:13 + h], sg[0:M, h * 128:(h + 1) * 128],
                    ALU.subtract, ALU.mult, [srcb, ST.b, sg.b], [ON.b], signal=(h == H - 1))
            for h in range(H):
                act(ret[0:M, h * 128:(h + 1) * 128], ON[0:M, h * 128:(h + 1) * 128], AF.Identity, [ON.b, RSTD.b], [ret.b],
                    scale=RSTD[0:M, h:h + 1], signal=(h == H - 1))

        def wout_res(M, mixt, xres, xres_b, tabGT, dst, dst_b):
            for n in range(2):
                bank = nb()
                for kc in range(KC):
                    mm(bank[0:M, :], mixt[:, kc, 0:M], WOUT[:, kc, n * 512:(n + 1) * 512], kc == 0, kc == KC - 1,
                       [mixt.b, WOUTb[kc]], [bank.b], kc == KC - 1)
                tt("dve", TMPB[0:M, n * 512:(n + 1) * 512], bank[0:M, :], tabGT[0:M, n * 512:(n + 1) * 512], ALU.mult,
                   [bank.b, tabGT.b], [TMPB.b])
            tt("pool", dst, TMPB[0:M, :], xres, ALU.add, [TMPB.b, xres_b], [dst_b])


        with ExitStack() as st:
            XT = [sb(st, "xt%d" % i, [128, D]) for i in range(4)]
            TMPA = sb(st, "tmpa", [128, D]); TMPB = sb(st, "tmpb", [128, D])
            HBF = sb(st, "hbf", [128, D], BF16)
            HT = [sb(st, "ht%d" % i, [128, KC, 128], BF16) for i in range(2)]
            RT = [sb(st, "rt%d" % i, [128, 192]) for i in range(2)]
            RA = sb(st, "ropea", [128, 512]); RB = sb(st, "ropeb", [128, 512])
            QK2 = [sb(st, "qk%d" % i, [128, 2 * RW], BF16) for i in range(2)]
            VB2 = [sb(st, "vb%d" % i, [128, RW], BF16) for i in range(2)]
            SG2 = [sb(st, "sg%d" % i, [128, RW]) for i in range(2)]
            QST = sb(st, "qst", [128, H, 128], BF16); KT = sb(st, "kt", [128, H, 128], BF16)
            KD = sb(st, "kd", [128, RW], BF16); ATT = sb(st, "att", [128, RW], BF16)
            ON = sb(st, "on", [128, RW]); RET = sb(st, "ret", [128, RW], BF16)
            MIXT2 = [sb(st, "mixt%d" % i, [128, KC, 128], BF16) for i in range(2)]
            UT = sb(st, "ut", [128, 2, 144])
            S2 = sb(st, "s2", [128, 2, 144]); S4 = sb(st, "s4", [128, 2, 144]); S8 = sb(st, "s8", [128, 144])
            WS = sb(st, "wsum", [128, 2, 128]); PT = sb(st, "pt", [128, 2, 128], BF16)
            S32 = sb(st, "s32", [128, RW]); SBF = sb(st, "sbf", [128, RW], BF16)
            ST = sb(st, "stat", [128, 32]); RSTD = sb(st, "rstd", [128, 8])
            X12 = [sb(st, "x1_%d" % i, [128, D]) for i in range(2)]
            NP = sb(st, "npool", [16, PW])
            TRB2 = T(PB[6][:, :].bitcast(BF16), "trb2")
            TRB2.b = PB[6].b
            ZB = [PB[1], PB[2]]
            R0 = PB[3]; R1 = PB[4]; R2 = PB[5]
            zc = {"i": 0}

            mset("dve", UT[:], 0.0, [UT.b])

            def stL(t):
                xt = XT[t % 4]
                S.dma("sp", xt[:], xp[t * 128:(t + 1) * 128, :], (), [xt.b])

            def stFa(t):
                xt = XT[t % 4]
                norm_mod_a(128, xt[:], xt.b, TAB[1], TAB[0], TMPA, HBF, 0)

            def stFb(t):
                ht = HT[t % 2]
                S.dma("sp", RT[t % 2][:], rope[t * 128:(t + 1) * 128, :], (), [RT[t % 2].b])
                norm_mod_b(128, HBF, ht, ht.b, 0, 128)

            def stF(t):
                stL(t)
                stFa(t)
                stFb(t)

            zbank = {}

            def stZa(t, blk):
                ht = HT[t % 2]
                bank = ZB[zc["i"] % 2]
                zc["i"] += 1
                zbank[(t, blk)] = bank
                c0 = PW + blk * 512
                for kc in range(KC):
                    mm(bank[:, :], ht[:, kc, :], WIN[:, kc, c0:c0 + 512], kc == 0, kc == KC - 1,
                       [ht.b, WINb[kc]], [bank.b], kc == KC - 1)

            def stZb(t, blk):
                p = t % 2
                ht = HT[p]
                bank = zbank.pop((t, blk))
                if blk < 3:
                    rope_block(128, bank, blk, RT[p], QK2[p], QK2[p].b)
                elif blk == 3:
                    cpy("act", VB2[p][:, 0:512], bank[:, :], [bank.b], [VB2[p].b])
                elif blk == 4:
                    cpy("act", VB2[p][:, 512:768], bank[:, 0:256], [bank.b], [VB2[p].b])
                    act(SG2[p][:, 0:256], bank[:, 256:512], AF.Silu, [bank.b], [SG2[p].b])
                else:
                    act(SG2[p][:, 256:768], bank[:, :], AF.Silu, [bank.b], [SG2[p].b])
                    if t == NT - 1:
                        bk = ZB[zc["i"] % 2]
                        zc["i"] += 1
                        for kc in range(KC):
                            mm(bk[0:15, 0:PW], ht[:, kc, 113:128], WIN[:, kc, 0:PW], kc == 0, kc == KC - 1,
                               [ht.b, WINb[kc]], [bk.b], kc == KC - 1)
                        cpy("act", NP[0:15, :], bk[0:15, 0:PW], [bk.b], [NP.b])
                        S.dma("sp", npool_p[:, :], NP[0:15, :], [NP.b], ())

            def stZ(t, blk):
                stZa(t, blk)
                stZb(t, blk)

            def stUa(t):
                p = t % 2
                ht = HT[p]
                for c in range(2):
                    for kc in range(KC):
                        mm(UY[:, c * 128:(c + 1) * 128], WIN[:, kc, c * 128:(c + 1) * 128], ht[:, kc, :], kc == 0, kc == KC - 1,
                           [ht.b, WINb[kc]], [UY.b], kc == KC - 1 and c == 1)
                if t > 0:
                    cpy("pool", UT[:, :, 0:16], UT[:, :, 128:144], [UT.b], [UT.b])
                cpy("act", UT[:, :, 16:144], UY[:, 0:256].rearrange("p (c m) -> p c m", c=2), [UY.b], [UT.b])
                U = UT
                tt("pool", S2[:, :, 2:144], U[:, :, 2:144], U[:, :, 1:143], ALU.add, [UT.b], [S2.b])
                cpy("pool", WS[0:64, 0, :], S2[0:64, 0, 16:144], [S2.b], [WS.b])
                tt("pool", S4[:, :, 4:144], S2[:, :, 4:144], S2[:, :, 2:142], ALU.add, [S2.b], [S4.b])
                cpy("pool", WS[64:128, 0, :], S4[64:128, 0, 16:144], [S4.b], [WS.b])
                tt("pool", S8[:, 8:144], S4[:, 1, 8:144], S4[:, 1, 4:140], ALU.add, [S4.b], [S8.b])
                cpy("pool", WS[0:64, 1, :], S8[0:64, 16:144], [S8.b], [WS.b])
                tt("pool", WS[64:128, 1, :], S8[64:128, 16:144], S8[64:128, 8:136], ALU.add, [S8.b], [WS.b])
                if t == 0:
                    tt("pool", WS[:, :, 0:16], WS[:, :, 0:16], cst("corr0").rearrange("p (c m) -> p c m", c=2), ALU.mult,
                       [WS.b, CONST.b], [WS.b])

            def stUb(t):
                mixt = MIXT2[t % 2]
                for c in range(2):
                    stt(PT[:, c, :], WS[:, c, :], cst("invw")[:, c:c + 1], UT[:, c, 16:144], ALU.mult, ALU.subtract,
                        [WS.b, CONST.b, UT.b], [PT.b])
                for c in range(2):
                    mm(UY[:, 256 + c * 128:256 + (c + 1) * 128], WPB[:, c, :], PT[:, c, :], True, True,
                       [WPB.b, PT.b], [UY.b], c == 1)
                for c in range(2):
                    act(mixt[:, c, :], UY[:, 256 + c * 128:256 + (c + 1) * 128], AF.Identity, [UY.b, LS.b], [mixt.b],
                        scale=LS[:, c:c + 1])

            def stR1a(t):
                QK = QK2[t % 2]
                for h in range(H):
                    tr(TRB2[:, h * 128:(h + 1) * 128], QK[:, h * 128:(h + 1) * 128], IDB[:, :], [QK.b, IDB.b], [TRB2.b], h == H - 1)
                tt("dve", QST[:].rearrange("p h m -> p (h m)"), TRB2[:, 0:RW], cst("qdec"), ALU.mult, [TRB2.b, CONST.b], [QST.b])
                for h in range(H):
                    tr(TRB[:, h * 128:(h + 1) * 128], QK[:, RW + h * 128:RW + (h + 1) * 128], IDB[:, :], [QK.b, IDB.b], [TRB.b],
                       h == H - 1)
                cpy("act", KT[:].rearrange("p h m -> p (h m)"), TRB[:, 0:RW], [TRB.b], [KT.b])
                for h in range(H):
                    act(KD[:, h * 128:(h + 1) * 128], QK[:, RW + h * 128:RW + (h + 1) * 128], AF.Identity, [QK.b, CONST.b], [KD.b],
                        scale=cst("kdec")[:, h:h + 1], signal=(h == H - 1))

            def stR1b(t):
                for h in range(H):
                    bank = R0 if h < 4 else R1
                    hh = h if h < 4 else h - 4
                    mm(bank[:, hh * 128:(hh + 1) * 128], KT[:, h, :], QST[:, h, :], True, True, [KT.b, QST.b], [bank.b],
                       h == 3 or h == 5)
                mk = cst("maskT")
                tt("dve", ATT[:, 0:512], R0[:, :], mk[:, 0:512], ALU.mult, [R0.b, CONST.b], [ATT.b])
                tt("dve", ATT[:, 512:768], R1[:, 0:256], mk[:, 512:768], ALU.mult, [R1.b, CONST.b], [ATT.b])

            def stR2a(t):
                VB = VB2[t % 2]
                for h in range(H):
                    bank = R0 if h < 4 else R1
                    hh = h if h < 4 else h - 4
                    last = (h == 3 or h == 5)
                    mm(bank[:, hh * 128:(hh + 1) * 128], ATT[:, h * 128:(h + 1) * 128], VB[:, h * 128:(h + 1) * 128], True, t == 0,
                       [ATT.b, VB.b], [bank.b], last and t == 0)
                    if t > 0:
                        mm(bank[:, hh * 128:(hh + 1) * 128], QST[:, h, :], SBF[:, h * 128:(h + 1) * 128], False, True,
                           [QST.b, SBF.b], [bank.b], last)
                for h in range(H):
                    if h < 4:
                        o_ = R2[:, h * 128:(h + 1) * 128]; ob = R2.b
                    else:
                        o_ = R1[:, 256 + (h - 4) * 128:256 + (h - 3) * 128]; ob = R1.b
                    mm(o_, KD[:, h * 128:(h + 1) * 128], VB[:, h * 128:(h + 1) * 128], True, True, [KD.b, VB.b], [ob],
                       h == 3 or h == 5)
                if t == 0:
                    cpy("act", S32[:, 0:512], R2[:, :], [R2.b], [S32.b])
                    cpy("act", S32[:, 512:768], R1[:, 256:512], [R1.b], [S32.b])
                else:
                    for h in range(H):
                        if h < 4:
                            i_ = R2[:, h * 128:(h + 1) * 128]; ib = R2.b
                        else:
                            i_ = R1[:, 256 + (h - 4) * 128:256 + (h - 3) * 128]; ib = R1.b
                        stt(S32[:, h * 128:(h + 1) * 128], S32[:, h * 128:(h + 1) * 128], gamC[h], i_,
                            ALU.mult, ALU.add, [S32.b, ib], [S32.b], signal=(h == H - 1))
                if t < NT - 1:
                    cpy("act", SBF[:], S32[:], [S32.b], [SBF.b])
                else:
                    S.dma("sp", nret_p.rearrange("(h k) v -> k h v", h=H), S32[:].rearrange("p (h v) -> p h v", h=H), [S32.b], ())

            def stR2b(t):
                groupnorm_gate(128, R0, R1, SG2[t % 2], RET)

            def stR2c(t):
                mixt = MIXT2[t % 2]
                for h in range(H):
                    tr(TRB2[:, h * 128:(h + 1) * 128], RET[:, h * 128:(h + 1) * 128], IDB[:, :], [RET.b, IDB.b], [TRB2.b], h == H - 1)
                cpy("act", mixt[:, 2:8, :].rearrange("p h m -> p (h m)"), TRB2[:, 0:RW], [TRB2.b], [mixt.b])

            def stW(t):
                p = t % 2
                mixt = MIXT2[p]; xt = XT[t % 4]; x1 = X12[p]
                for n in range(2):
                    bank = (R0, R2)[n]
                    for kc in range(KC):
                        mm(bank[:, :], mixt[:, kc, :], WOUT[:, kc, n * 512:(n + 1) * 512], kc == 0, kc == KC - 1,
                           [mixt.b, WOUTb[kc]], [bank.b], kc == KC - 1)
                    tt("dve", TMPB[:, n * 512:(n + 1) * 512], bank[:, :], TAB[2][:, n * 512:(n + 1) * 512], ALU.mult,
                       [bank.b, TAB[2].b], [TMPB.b])
                tt("pool", x1[:], TMPB[:], xt[:], ALU.add, [TMPB.b, xt.b], [x1.b])
                S.dma("sp", x1s[t * 128:(t + 1) * 128, :], x1[:], [x1.b], ())

            for i in range(4):
                stL(i)
            for t0 in range(2):
                stFa(t0)
                stFb(t0)
                for blk in range(6):
                    stZ(t0, blk)
                stUa(t0)
                stUb(t0)
            stFa(2)
            stFb(2)
            stR1a(0)
            stR1b(0)
            for t in range(NT):
                n1 = t + 1 < NT
                n2 = t + 2 < NT
                n3 = t + 3 < NT
                stR2a(t)
                if n1:
                    stR1a(t + 1)
                if n2:
                    stZa(t + 2, 0)
                    stZa(t + 2, 1)
                if n3:
                    stFa(t + 3)
                stR2b(t)
                if n2:
                    stZb(t + 2, 0)
                    stZb(t + 2, 1)
                    stZa(t + 2, 2)
                if n1:
                    stR1b(t + 1)
                if n2:
                    stUa(t + 2)
                stR2c(t)
                stW(t)
                if n2:
                    stZb(t + 2, 2)
                    stZa(t + 2, 3)
                if n3:
                    stFb(t + 3)
                if n2:
                    stZb(t + 2, 3)
                    stZa(t + 2, 4)
                    stZb(t + 2, 4)
                    stZa(t + 2, 5)
                    stZb(t + 2, 5)
                    stUb(t + 2)
                if t + 4 < NT:
                    stL(t + 4)

        S.barrier(False)
        with ExitStack() as st:
            XT = [sb(st, "xts", [NS, D])]
            TMPA = sb(st, "tmpas", [NS, D]); TMPB = sb(st, "tmpbs", [NS, D])
            HBF = sb(st, "hbfs", [NS, D], BF16)
            HT = [sb(st, "hts", [128, KC, 128], BF16)]
            RT = [sb(st, "rts", [NS, 192])]
            RA = sb(st, "ropeas", [NS, 512]); RB = sb(st, "ropebs", [NS, 512])
            SG = sb(st, "sgs", [NS, RW]); ON = sb(st, "ons", [NS, RW]); RET = sb(st, "rets", [NS, RW], BF16)
            MIXT = sb(st, "mixts", [128, KC, 128], BF16)
            PT = sb(st, "pts", [128, 2, 128], BF16)
            ST = sb(st, "stats", [NS, 32]); RSTD = sb(st, "rstds", [NS, 8])
            S32 = sb(st, "ois", [NS, RW])
            SIN_ = [sb(st, "sin%d" % i, [128, RW]) for i in range(3)]
            SOUT = [sb(st, "sout%d" % i, [128, RW]) for i in range(2)]
            QM = [sb(st, "qm%d" % i, [128, H, NS], BF16) for i in range(2)]
            KM = [sb(st, "km%d" % i, [NS, RW], BF16) for i in range(2)]
            SP0 = sb(st, "sp0", [120, PW]); SP1 = sb(st, "sp1", [120, PW])
            QKF = sb(st, "qkf", [NS, 2 * RW]); VF = sb(st, "vf", [NS, RW]); QTS = sb(st, "qts", [128, H, NS], BF16)
            VF16 = sb(st, "vf16", [NS, RW], BF16)
            SB16 = [sb(st, "sb16_%d" % i, [128, RW], BF16) for i in range(2)]
            UNEW = sb(st, "unew", [NS, PW])
            M = NS
            xts = XT[0]; hts = HT[0]; rts = RT[0]
            S.dma("sp", xts[0:M, :], xs[:, :], (), [xts.b])
            S.dma("sp", rts[0:M, :], rope[L:L + M, :], (), [rts.b])
            S.dma("sp", SP0[:], spool[0:120, :], (), [SP0.b])
            S.dma("sp", SP1[:], spool[120:240, :], (), [SP1.b])
            norm_mod_T(M, xts[0:M, :], xts.b, TABS[1], TABS[0], TMPA, HBF, hts, hts.b, 0, 128, 0)
            bk = nb()
            for kc in range(KC):
                mm(bk[0:M, 0:PW], hts[:, kc, 0:M], WIN[:, kc, 0:PW], kc == 0, kc == KC - 1, [hts.b, WINb[kc]], [bk.b], kc == KC - 1)
            cpy("act", UNEW[:], bk[0:M, 0:PW], [bk.b], [UNEW.b])
            npv = npool_s.rearrange("(b r) c -> b r c", r=15)
            S.dma("sp", npv[:, 14, :], UNEW[:], [UNEW.b], ())
            S.dma("sp", npv[:, 0:14, :], spool.rearrange("(b r) c -> b r c", r=15)[:, 1:15, :], (), ())
            selv = cst("sel").rearrange("p (g k m) -> p g k m", g=4, k=3)
            for c in range(2):
                for gg in range(2):
                    g = 2 * c + gg
                    bank = nb()
                    mm(bank[:, 0:M], SP0[:, c * 128:(c + 1) * 128], selv[0:120, g, 0, :], True, False, [SP0.b, CONST.b], [bank.b], False)
                    mm(bank[:, 0:M], SP1[:, c * 128:(c + 1) * 128], selv[0:120, g, 1, :], False, False, [SP1.b, CONST.b], [bank.b], False)
                    mm(bank[:, 0:M], UNEW[:, c * 128:(c + 1) * 128], selv[0:M, g, 2, :], False, True, [UNEW.b, CONST.b], [bank.b], True)
                    cpy("act", PT[gg * 64:(gg + 1) * 64, c, 0:M], bank[gg * 64:(gg + 1) * 64, 0:M], [bank.b], [PT.b])
            for c in range(2):
                mm(UY[:, 256 + c * 128:256 + c * 128 + M], WPB[:, c, :], PT[:, c, 0:M], True, True, [WPB.b, PT.b], [UY.b], c == 1)
            for c in range(2):
                act(MIXT[:, c, 0:M], UY[:, 256 + c * 128:256 + c * 128 + M], AF.Identity, [UY.b, LS.b], [MIXT.b], scale=LS[:, c:c + 1])
            for blk in range(3):
                bank = zblock(M, hts, blk)
                rope_block(M, bank, blk, rts, QKF, QKF.b)
            b3 = zblock(M, hts, 3)
            cpy("act", VF[:, 0:512], b3[0:M, :], [b3.b], [VF.b])
            b4 = zblock(M, hts, 4)
            cpy("act", VF[:, 512:768], b4[0:M, 0:256], [b4.b], [VF.b])
            act(SG[0:M, 0:256], b4[0:M, 256:512], AF.Silu, [b4.b], [SG.b])
            b5 = zblock(M, hts, 5)
            act(SG[0:M, 256:768], b5[0:M, :], AF.Silu, [b5.b], [SG.b])
            ts("pool", QKF[:, RW:2 * RW], QKF[:, RW:2 * RW], float(HD ** -0.5), None, ALU.mult, None, [QKF.b], [QKF.b])
            tt("dve", TMPB[0:M, 0:RW], QKF[:, 0:RW], QKF[:, RW:2 * RW], ALU.mult, [QKF.b], [TMPB.b])
            red(ST[0:M, 0:6], TMPB[0:M, 0:RW].rearrange("p (h d) -> p h d", h=H), [TMPB.b], [ST.b])
            for h in range(H):
                bk = nb()
                tr(bk[:, 0:M], QKF[:, h * 128:(h + 1) * 128], IDF[0:M, 0:M], [QKF.b, IDF.b], [bk.b], True)
                cpy("act", QTS[:, h, :], bk[:, 0:M], [bk.b], [QTS.b])
            OI = T(S32[0:NS, :], "oi")
            OI.b = S32.b
            cpy("act", VF16[:], VF[:], [VF.b], [VF16.b])
            OIA = nb(); OIB = nb()
            reserved.extend([OIA, OIB])
            gam = cst("gam")
            for b in range(NS):
                si = SIN_[b % 3]; so = SOUT[b % 2]; km = KM[b % 2]; qm = QM[b % 2]
                mset("pool", qm[:].rearrange("p h m -> p (h m)"), 0.0, [qm.b])
                cpy("pool", qm[:, :, b], QTS[:, :, b], [QTS.b], [qm.b])
                if b == 0:
                    for b2 in range(2):
                        S.dma("sp", SIN_[b2][:].rearrange("p (h v) -> p h v", h=H), sret[b2].rearrange("h k v -> k h v"), (),
                              [SIN_[b2].b])
                if b + 2 < NS:
                    sn = SIN_[(b + 2) % 3]
                    S.dma("sp", sn[:].rearrange("p (h v) -> p h v", h=H), sret[b + 2].rearrange("h k v -> k h v"), (), [sn.b])
                ts("dve", km[:], QKF[:, RW:2 * RW], IDF[0:M, b:b + 1], None, ALU.mult, None, [QKF.b, IDF.b], [km.b])
                s16 = SB16[b % 2]
                cpy("act", s16[:], si[:], [si.b], [s16.b])
                for h in range(H):
                    bank = OIA if h < 4 else OIB
                    hh = h if h < 4 else h - 4
                    S.op("pe", (lambda o_, l_, r_, st_, sp_: (lambda e: e.matmul(o_, lhsT=l_, rhs=r_, start=st_, stop=sp_,
                                                                                   skip_group_check=True)))(
                        bank[0:M, hh * 128:(hh + 1) * 128], qm[:, h, :], s16[:, h * 128:(h + 1) * 128],
                        b == 0 and hh == 0, b == NS - 1),
                        [qm.b, s16.b], [bank.b], b == NS - 1 and (h == 3 or h == 5))
                SA = nb(); SBk = nb()
                for h in range(H):
                    bank = SA if h < 4 else SBk
                    hh = h if h < 4 else h - 4
                    mm(bank[:, hh * 128:(hh + 1) * 128], km[:, h * 128:(h + 1) * 128], VF16[:, h * 128:(h + 1) * 128], True, True,
                       [km.b, VF16.b], [bank.b], h == 3 or h == 5)
                for h in range(H):
                    bank = SA if h < 4 else SBk
                    hh = h if h < 4 else h - 4
                    stt(so[:, h * 128:(h + 1) * 128], si[:, h * 128:(h + 1) * 128], gam1[h], bank[:, hh * 128:(hh + 1) * 128],
                        ALU.mult, ALU.add, [si.b, bank.b], [so.b])
                S.dma("sp", nret_s[b].rearrange("h k v -> k h v"), so[:].rearrange("p (h v) -> p h v", h=H), [so.b], ())
            tt("dve", OI[:, 0:512], OIA[0:M, :], gam[0:M, 0:512], ALU.mult, [OIA.b, CONST.b], [OI.b])
            tt("dve", OI[:, 512:768], OIB[0:M, 0:256], gam[0:M, 512:768], ALU.mult, [OIB.b, CONST.b], [OI.b])
            for h in range(H):
                stt(OI[:, h * 128:(h + 1) * 128], VF[:, h * 128:(h + 1) * 128], ST[0:M, h:h + 1], OI[:, h * 128:(h + 1) * 128],
                    ALU.mult, ALU.add, [VF.b, ST.b, OI.b], [OI.b])

            class _V:
                def __init__(self, base, off):
                    self.base = base; self.off = off; self.b = base.b

                def __getitem__(self, k):
                    r, c = k
                    if isinstance(c, slice):
                        c0 = 0 if c.start is None else c.start
                        c1 = (512 if self.off == 0 else 256) if c.stop is None else c.stop
                        return self.base.t[r, self.off + c0:self.off + c1]
                    raise KeyError

            if debug:
                S.dma("sp", dbg_oi[:, :], OI[:, :], [OI.b], ())
            groupnorm_gate(M, _V(OI, 0), _V(OI, 512), SG, RET)
            if debug:
                S.dma("pool", dbg_ret[:, :], RET[0:M, :], [RET.b], ())
            for h in range(H):
                tr(TRB[:, h * 128:h * 128 + M], RET[0:M, h * 128:(h + 1) * 128], IDB[0:M, 0:M], [RET.b, IDB.b], [TRB.b], h == H - 1)
            cpy("act", MIXT[:, 2:8, 0:M], TRB[:, 0:RW].rearrange("p (h m) -> p h m", h=H)[:, :, 0:M], [TRB.b], [MIXT.b])
            wout_res(M, MIXT, xts[0:M, :], xts.b, TABS[2], X1S[:], X1S.b)
            if debug:
                S.dma("pool", dbg_mix.rearrange("p (k m) -> p k m", k=KC), MIXT[:, :, 0:M], [MIXT.b], ())
                S.dma("sp", dbg_x1[:, :], X1S[:], [X1S.b], ())

        st1.__exit__(None, None, None)
        S.barrier(True)

        st2 = ExitStack()
        st2.__enter__()
        WFI = sb(st2, "w_fi", [128, NJ, KC, 2, 128], BF16)
        WFO = sb(st2, "w_fo", [128, NJ, D], BF16)
        WFIb = [Buf("w_fi%d" % j) for j in range(NJ)]
        WFOb = [Buf("w_fo%d" % j) for j in range(NJ)]
        fov = w_fo.rearrange("(j p) c -> p j c", p=128)

        def load_ffn(j0, j1):
            def go():
                for j in range(j0, j1):
                    S.dma("pool", WFI[:, j].rearrange("p (k2 k) a c -> p k2 (k a c)", k2=2),
                          w_fi[j].rearrange("p (k2 r) -> p k2 r", k2=2), (), [WFIb[j]])
                    if j % 2 == 1:
                        S.dma("pool", WFO[:, j - 1:j + 1, :], fov[:, j - 1:j + 1, :], (), [WFOb[j - 1], WFOb[j]])
            return go

        ada_tables([(3, 0, "sh"), (4, 1, "sc"), (5, 2, "gt")], g_ffn, PB, nbuf=1,
                   hooks=[load_ffn(0, 2), load_ffn(2, 4), load_ffn(4, NJ)])
        S.barrier(False)

        with ExitStack() as st:
            X1C = sb(st, "x1c", [128, 4, D])
            X1Cb = [Buf("x1c%d" % i) for i in range(4)]
            TMPA = sb(st, "tmpa2", [128, D]); TMPB = sb(st, "tmpb2", [128, D])
            HBF = sb(st, "hbf2", [128, D], BF16)
            H2TS = [sb(st, "h2t%d" % i, [128, KC, 256], BF16) for i in range(2)]
            TT_ = [sb(st, "tt%d" % i, [128, 256]) for i in range(2)]
            GJ = [sb(st, "gj%d" % i, [128, 256], BF16) for i in range(3)]
            HIST = sb(st, "hist", [128, NJ, 2]); HC = sb(st, "hc", [128, NJ, 2]); HTMP = sb(st, "htmp", [128, NJ, 2])
            GS = sb(st, "gs", [128, NJ, NS], BF16)
            ABK = [PB[0], PB[1], PB[6]]
            FB = [[PB[2], PB[3]], [PB[4], PB[5]]]

            def prep_load(sti):
                for i in range(2):
                    tix = 2 * sti + i
                    slot = (sti % 2) * 2 + i
                    S.dma("sp", X1C[:, slot, :], x1s[tix * 128:(tix + 1) * 128, :], (), [X1Cb[slot]])

            def prep_a(sti, i):
                slot = (sti % 2) * 2 + i
                norm_mod_a(128, X1C[:, slot, :], X1Cb[slot], TAB[1], TAB[0], TMPA, HBF, i)

            def prep_b(sti, i):
                norm_mod_b(128, HBF, H2TS[sti % 2], H2TS[sti % 2].b, i * 128, 128)

            def prep(sti):
                prep_load(sti)
                for i in range(2):
                    prep_a(sti, i)
                    prep_b(sti, i)

            def ab(sti, j):
                bank = ABK[j % 3]
                h2t = H2TS[sti % 2]
                for half in range(2):
                    for kc in range(KC):
                        mm(bank[:, half * 256:(half + 1) * 256], WFI[:, j, kc, half, :], h2t[:, kc, :], kc == 0, kc == KC - 1,
                           [WFIb[j], h2t.b], [bank.b], kc == KC - 1 and half == 1)

            def cx(sti, j):
                bank = ABK[j % 3]; tb = TT_[j % 2]
                act(tb[:], bank[:, 0:256], AF.Identity, [bank.b, CW.b, CB.b], [tb.b], scale=CW[:, j, 2:3], bias=CB[:, j:j + 1])
                stt(tb[:, 1:256], bank[:, 0:255], CW[:, j, 1:2], tb[:, 1:256], ALU.mult, ALU.add, [bank.b, CW.b, tb.b], [tb.b])
                stt(tb[:, 2:256], bank[:, 0:254], CW[:, j, 0:1], tb[:, 2:256], ALU.mult, ALU.add, [bank.b, CW.b, tb.b], [tb.b])
                if sti > 0:
                    tt("dve", tb[:, 0:2], tb[:, 0:2], HC[:, j, :], ALU.add, [tb.b, HC.b], [tb.b])
                cpy("act", HIST[:, j, :], bank[:, 254:256], [bank.b], [HIST.b])

            def cy(j):
                bank = ABK[j % 3]; tb = TT_[j % 2]; gj = GJ[j % 3]
                act(tb[:], tb[:], AF.Silu, [tb.b], [tb.b])
                tt("dve", gj[:], tb[:], bank[:, 256:512], ALU.mult, [tb.b, bank.b], [gj.b])

            def ffn_out(j):
                gj = GJ[j % 3]
                for tix in range(2):
                    for n in range(2):
                        mm(FB[tix][n][:, :], gj[:, tix * 128:(tix + 1) * 128], WFO[:, j, n * 512:(n + 1) * 512], j == 0, j == NJ - 1,
                           [gj.b, WFOb[j]], [FB[tix][n].b], j == NJ - 1)

            def fin_evac(sti):
                tt("pool", HTMP[:, :, 0], HIST[:, :, 1], CW[:, :, 1], ALU.mult, [HIST.b, CW.b], [HTMP.b])
                tt("pool", HTMP[:, :, 1], HIST[:, :, 0], CW[:, :, 0], ALU.mult, [HIST.b, CW.b], [HTMP.b])
                tt("pool", HC[:, :, 0], HTMP[:, :, 0], HTMP[:, :, 1], ALU.add, [HTMP.b], [HC.b])
                tt("pool", HC[:, :, 1], HIST[:, :, 1], CW[:, :, 0], ALU.mult, [HIST.b, CW.b], [HC.b])
                for i in range(2):
                    stg = (TMPB, TMPA)[i]
                    for n in range(2):
                        tt("dve", stg[:, n * 512:(n + 1) * 512], FB[i][n][:, :], TAB[2][:, n * 512:(n + 1) * 512], ALU.mult,
                           [FB[i][n].b, TAB[2].b], [stg.b])

            def fin_rest(sti, i):
                tix = 2 * sti + i
                slot = (sti % 2) * 2 + i
                stg = (TMPB, TMPA)[i]
                tt("pool", stg[:], stg[:], X1C[:, slot, :], ALU.add, [stg.b, X1Cb[slot]], [stg.b])
                act(HBF[:], stg[:], AF.Square, [stg.b], [HBF.b, SS.b], accum=SS[:, 2 + i:3 + i])
                rsqrt(RS[:, 2 + i:3 + i], SS[:, 2 + i:3 + i], D * EPS, SS.b, RS.b, SQ[:, 2 + i:3 + i], SQ.b)
                stt(X1C[:, slot, :], stg[:], RS[:, 2 + i:3 + i], GF[:], ALU.mult, ALU.mult, [stg.b, RS.b, GF.b], [X1Cb[slot]])
                S.dma("sp", yp[tix * 128:(tix + 1) * 128, :], X1C[:, slot, :], [X1Cb[slot]], ())

            NST = NT // 2
            prep(0)
            for sti in range(NST):
                nxt = sti + 1 < NST
                for j in range(NJ):
                    ab(sti, j)
                    if j >= 2:
                        ffn_out(j - 2)
                    cx(sti, j)
                    if j >= 1:
                        cy(j - 1)
                    if sti > 0 and j == 1:
                        fin_rest(sti - 1, 0)
                    if sti > 0 and j == 3:
                        fin_rest(sti - 1, 1)
                    if nxt:
                        if j == 5:
                            prep_load(sti + 1)
                        if j == 7:
                            prep_a(sti + 1, 0)
                        if j == 10:
                            prep_b(sti + 1, 0)
                        if j == 12:
                            prep_a(sti + 1, 1)
                        if j == 15:
                            prep_b(sti + 1, 1)
                cy(NJ - 1)
                ffn_out(NJ - 2)
                ffn_out(NJ - 1)
                fin_evac(sti)
            fin_rest(NST - 1, 0)
            fin_rest(NST - 1, 1)

            S.barrier(False)
            H2T = H2TS[0]
            flat = X1C[:, 2:4, :].rearrange("p a d -> p (a d)")
            SCT = T(flat[:, 0:704].rearrange("p (j m) -> p j m", j=NJ), "sct")
            AH = T(flat[:, 704:1100].rearrange("p (j m) -> p j m", j=NJ), "ah")
            TS_ = T(flat[:, 1100:1452].rearrange("p (j m) -> p j m", j=NJ), "tsamp")
            TS2 = T(flat[:, 1452:1804].rearrange("p (j m) -> p j m", j=NJ), "tsamp2")
            M = NS
            norm_mod_T(M, X1S[:], X1S.b, TABS[1], TABS[0], TMPA, HBF, H2T, H2T.b, 0, 128, 0)
            AALL = PB[0]; BALL = PB[1]
            for j in range(NJ):
                for half, bank in ((0, AALL), (1, BALL)):
                    for kc in range(KC):
                        mm(bank[:, j * M:(j + 1) * M], WFI[:, j, kc, half, :], H2T[:, kc, 0:M], kc == 0, kc == KC - 1,
                           [WFIb[j], H2T.b], [bank.b], kc == KC - 1 and j == NJ - 1)
            SCA = PB[2]; SCB = PB[3]
            for q in range(3):
                c0 = q * 1024
                w = min(1024, DFF - c0)
                S.dma("sp", X1C[0:2 * NS, q % 2, 0:w], sconv[:, c0:c0 + w], (), [X1Cb[q % 2]])
                for jl in range(w // 128):
                    j = q * 8 + jl
                    bank = SCA if j < 11 else SCB
                    jj = j if j < 11 else j - 11
                    tr(bank[:, jj * 32:(jj + 1) * 32], X1C[0:2 * NS, q % 2, jl * 128:(jl + 1) * 128], IDF[0:32, 0:32],
                       [X1Cb[q % 2], IDF.b], [bank.b], True)
            cpy("act", SCT[:, 0:11, :], SCA[:, 0:352].rearrange("p (j m) -> p j m", j=11), [SCA.b], [SCT.b])
            cpy("act", SCT[:, 11:22, :], SCB[:, 0:352].rearrange("p (j m) -> p j m", j=11), [SCB.b], [SCT.b])
            av = AALL[:, 0:NJ * M].rearrange("p (j m) -> p j m", j=NJ)
            bv = BALL[:, 0:NJ * M].rearrange("p (j m) -> p j m", j=NJ)
            sctv = SCT[:].rearrange("p j (b r) -> p j b r", r=2)

            def bc(ap2):
                return ap2.unsqueeze(2).to_broadcast([128, NJ, M])

            cpy("act", AH[:, :, 2:2 + M], av, [AALL.b], [AH.b])
            cpy("pool", AH[:, :, 0:2], HIST[:], [HIST.b], [AH.b])
            tt("dve", TS_[:], av, bc(CW[:, :, 2]), ALU.mult, [AALL.b, CW.b], [TS_.b])
            tt("pool", TS2[:], sctv[:, :, :, 1], bc(CW[:, :, 1]), ALU.mult, [SCT.b, CW.b], [TS2.b])
            tt("dve", TS_[:], TS_[:], TS2[:], ALU.add, [TS_.b, TS2.b], [TS_.b])
            tt("pool", TS2[:], sctv[:, :, :, 0], bc(CW[:, :, 0]), ALU.mult, [SCT.b, CW.b], [TS2.b])
            tt("dve", TS_[:], TS_[:], TS2[:], ALU.add, [TS_.b, TS2.b], [TS_.b])
            tt("dve", TS_[:], TS_[:], bc(CB[:, :]), ALU.add, [TS_.b, CB.b], [TS_.b])
            act(TS2[:], TS_[:], AF.Silu, [TS_.b], [TS2.b])
            tt("dve", GS[:], TS2[:], bv, ALU.mult, [TS2.b, BALL.b], [GS.b])
            FS = [PB[4], PB[5]]
            for n in range(2):
                for j in range(NJ):
                    mm(FS[n][0:M, :], GS[:, j, :], WFO[:, j, n * 512:(n + 1) * 512], j == 0, j == NJ - 1,
                       [GS.b, WFOb[j]], [FS[n].b], j == NJ - 1)
                tt("dve", TMPB[0:M, n * 512:(n + 1) * 512], FS[n][0:M, :], TABS[2][0:M, n * 512:(n + 1) * 512], ALU.mult,
                   [FS[n].b, TABS[2].b], [TMPB.b])
            tt("pool", TMPB[0:M, :], TMPB[0:M, :], X1S[:], ALU.add, [TMPB.b, X1S.b], [TMPB.b])
            act(TMPA[0:M, :], TMPB[0:M, :], AF.Square, [TMPB.b], [TMPA.b, SS.b], accum=SS[0:M, 0:1])
            rsqrt(RS[0:M, 0:1], SS[0:M, 0:1], D * EPS, SS.b, RS.b, SQ[0:M, 0:1], SQ.b)
            stt(X1C[0:M, 0, :], TMPB[0:M, :], RS[0:M, 0:1], GF[0:M, :], ALU.mult, ALU.mult, [TMPB.b, RS.b, GF.b], [X1Cb[0]])
            S.dma("sp", ys[:, :], X1C[0:M, 0, :], [X1Cb[0]], ())
            CT = [PB[6], PB[2], PB[3], PB[0], PB[1], PB[4]]
            ncv = nconv_s.rearrange("(b r) c -> b r c", r=2)
            NCVb = [TMPA.b, TMPA.b]
            for g in range(6):
                bank = CT[g]
                nj = min(4, NJ - 4 * g)
                for jj in range(nj):
                    j = 4 * g + jj
                    tr(bank[0:2 + M, jj * 128:(jj + 1) * 128], AH[:, j, :], IDF[:, :], [AH.b, IDF.b], [bank.b], jj == nj - 1)
                w = nj * 128
                piece = TMPA[0:2 + M, (g % 2) * 512:(g % 2) * 512 + w]
                cpy("act", piece, bank[0:2 + M, 0:w], [bank.b], [NCVb[g % 2]])
                S.dma("sp", nconv_p[:, g * 512:g * 512 + w], TMPA[0:2, (g % 2) * 512:(g % 2) * 512 + w], [NCVb[g % 2]], ())
                S.dma("sp", ncv[:, 1, g * 512:g * 512 + w], TMPA[2:2 + M, (g % 2) * 512:(g % 2) * 512 + w], [NCVb[g % 2]], ())
            S.dma("sp", ncv[:, 0, :], sconv.rearrange("(b r) c -> b r c", r=2)[:, 1, :], (), ())
        st2.__exit__(None, None, None)

        S.finish()
        with nc.Block() as block:
            @block.sync
            def _(e):
                S.replay(e, "sp")

            @block.tensor
            def _(e):
                S.replay(e, "pe")

            @block.scalar
            def _(e):
                S.replay(e, "act")

            @block.vector
            def _(e):
                S.replay(e, "dve")

            @block.gpsimd
            def _(e):
                S.replay(e, "pool")
    return nc


_CACHE = {}


def kernel(x_prompt, x_sample, c_prompt, c_sample, state_pool, state_ret, state_conv,
           g_mix, g_ffn, w_ada, b_ada, w_in, w_pool, ls_pool, w_out,
           w_ffn_in, conv_w, conv_b, w_ffn_out, g_final):
    f = lambda a: np.ascontiguousarray(np.asarray(a, dtype=np.float32))
    consts, _, rope = _host_consts()
    ident = np.eye(128, dtype=np.float32)
    x_prompt = f(x_prompt); x_sample = f(x_sample); c_prompt = f(c_prompt); c_sample = f(c_sample)
    state_pool = f(state_pool); state_ret = f(state_ret); state_conv = f(state_conv)
    shared = {
        "g_mix": f(g_mix).reshape(1, D), "g_ffn": f(g_ffn).reshape(1, D), "g_fin": f(g_final).reshape(1, D),
        "w_ada": f(w_ada)[0], "b_ada": f(b_ada).reshape(1, 6 * D),
        "w_in": f(w_in)[0], "w_pool": f(w_pool).reshape(PW, 64),
        "ls_pool": f(f(ls_pool).reshape(2, 128).T),
        "w_out": f(w_out)[0],
        "w_fi": f(f(w_ffn_in)[0].reshape(KC, 128, 2, NJ, 128).transpose(3, 1, 0, 2, 4).reshape(NJ, 128, KC * 256)),
        "w_fo": f(w_ffn_out)[0],
        "cwT": f(f(conv_w)[0].reshape(3, NJ, 128).transpose(2, 1, 0).reshape(128, NJ * 3)),
        "cbT": f(f(conv_b)[0].reshape(NJ, 128).T),
        "consts": consts, "ident": ident, "rope": rope,
    }
    in_maps = []
    for i in range(NCORES):
        m = dict(shared)
        sl = slice(i * NS, (i + 1) * NS)
        m["xp"] = x_prompt[i]
        m["xs"] = f(x_sample[sl, 0, :])
        m["cp"] = f(c_prompt[i:i + 1])
        m["cs"] = f(c_sample[sl])
        m["spool"] = f(state_pool[0, sl].reshape(NS * 15, PW))
        m["sret"] = f(state_ret[0, sl])
        m["sconv"] = f(state_conv[0, sl].reshape(NS * 2, DFF))
        in_maps.append(m)
    if "nc" not in _CACHE:
        _CACHE["nc"] = build_nc()
    res = run_bass_kernel_spmd(_CACHE["nc"], in_maps, core_ids=list(range(NCORES)))
    R = res.results
    cat = lambda k: np.stack([np.asarray(r[k], dtype=np.float32) for r in R], axis=0)
    y_p = cat("yp")
    y_s = cat("ys").reshape(NCORES * NS, 1, D)
    np_p = cat("npool_p").reshape(1, NCORES, 15, PW)
    nr_p = cat("nret_p").reshape(1, NCORES, H, HD, HD)
    nc_p = cat("nconv_p").reshape(1, NCORES, 2, DFF)
    np_s = cat("npool_s").reshape(1, NCORES * NS, 15, PW)
    nr_s = cat("nret_s").reshape(1, NCORES * NS, H, HD, HD)
    nc_s = cat("nconv_s").reshape(1, NCORES * NS, 2, DFF)
    return (y_p, y_s, np_p, nr_p, nc_p, np_s, nr_s, nc_s)
```
